# Optimizing a Trainium2 kernel written in Bass

```python
import math, functools
import jax, jax.numpy as jnp
from jax import lax
import numpy as np

D_MODEL = 1024
BATCH = 2
SEQ = 8192
DEPTH = 1
DEC_BATCH = 128
DEC_SEQ = 8
PAST_LEN = 16384
PAGE_SIZE = 128

M_HEADS = 4
M_DK = 64
M_DV = 128
M_CHUNK = 64
M_QK_W = M_HEADS * M_DK
M_V_W = M_HEADS * M_DV
FORGET_BIAS_MEAN = 3.0
A_HEADS = 8
A_KV_HEADS = 2
A_HEAD_DIM = 64
A_GROUPS = A_HEADS // A_KV_HEADS
A_Q_W = A_HEADS * A_HEAD_DIM
A_KV_W = A_KV_HEADS * A_HEAD_DIM
WINDOW = 128
D_FF = -(-8 * D_MODEL // (3 * 256)) * 256
EPS = 1e-6
IN_SIZES = (M_QK_W, M_QK_W, M_V_W, M_V_W, M_HEADS, M_HEADS, A_Q_W, A_KV_W, A_KV_W, D_MODEL, D_MODEL)
D_IN = sum(IN_SIZES)

kernel_name = "hybrid_mlstm_swa_sink_decoder_step"


def rms_norm(x, w):
    xf = x.astype(jnp.float32)
    y = xf * lax.rsqrt(jnp.mean(xf * xf, axis=-1, keepdims=True) + EPS)
    return (y * w.astype(jnp.float32)).astype(x.dtype)


def mlstm_chunkwise(q, k, v, i_pre, f_pre, C0, n0, m0):
    f32 = jnp.float32
    B, T = q.shape[0], q.shape[1]
    L = math.gcd(T, M_CHUNK)
    nc = T // L

    def chunks(a):
        a = a.astype(f32).reshape((B, nc, L) + a.shape[2:])
        return jnp.moveaxis(jnp.moveaxis(a, 1, 0), 3, 2)

    qc = chunks(q)
    kc = chunks(k) * (M_DK ** -0.5)
    vc = chunks(v)
    ic = chunks(i_pre)
    bc = jnp.cumsum(jax.nn.log_sigmoid(chunks(f_pre)), axis=-1)
    causal = jnp.tril(jnp.ones((L, L), dtype=bool))

    def step(carry, xs):
        C, n, m = carry
        q_, k_, v_, i_, b_ = xs
        logw = jnp.where(causal, b_[..., :, None] - b_[..., None, :] + i_[..., None, :], -jnp.inf)
        inter = b_ + m[..., None]
        m_t = jnp.maximum(inter, logw.max(axis=-1))
        w_inter = jnp.exp(inter - m_t)
        w = jnp.exp(logw - m_t[..., None])
        s = jnp.einsum('bhtd,bhsd->bhts', q_, k_) * w
        num = w_inter[..., None] * jnp.einsum('bhtd,bhde->bhte', q_, C) + jnp.einsum('bhts,bhse->bhte', s, v_)
        den = w_inter * jnp.einsum('bhtd,bhd->bht', q_, n) + s.sum(axis=-1)
        h = num / jnp.maximum(jnp.abs(den), jnp.exp(-m_t))[..., None]
        w_last = w[..., -1, :]
        g = w_inter[..., -1]
        C = g[..., None, None] * C + jnp.einsum('bhs,bhsd,bhse->bhde', w_last, k_, v_)
        n = g[..., None] * n + jnp.einsum('bhs,bhsd->bhd', w_last, k_)
        return (C, n, m_t[..., -1]), h

    (C, n, m), h = lax.scan(step, (C0.astype(f32), n0.astype(f32), m0.astype(f32)), (qc, kc, vc, ic, bc))
    h = jnp.moveaxis(jnp.moveaxis(h, 2, 3), 0, 1).reshape(B, T, M_HEADS, M_DV)
    return h, C, n, m


def sink_softmax(scores, mask, sinks):
    s = jnp.where(mask, scores, -jnp.inf)
    sk = sinks.astype(jnp.float32).reshape(A_KV_HEADS, A_GROUPS)[:, :, None, None]
    mx = jnp.maximum(s.max(axis=-1, keepdims=True), sk)
    p = jnp.exp(s - mx)
    return p / (p.sum(axis=-1, keepdims=True) + jnp.exp(sk - mx))


def swa_prompt(q, k, v, sinks):
    f32 = jnp.float32
    B, T = q.shape[0], q.shape[1]
    nb = T // WINDOW
    qb = q.astype(f32).reshape(B, nb, WINDOW, A_KV_HEADS, A_GROUPS, A_HEAD_DIM)
    kb = k.astype(f32).reshape(B, nb, WINDOW, A_KV_HEADS, A_HEAD_DIM)
    vb = v.astype(f32).reshape(B, nb, WINDOW, A_KV_HEADS, A_HEAD_DIM)

    def with_prev(a):
        prev = jnp.pad(a, ((0, 0), (1, 0), (0, 0), (0, 0), (0, 0)))[:, :-1]
        return jnp.concatenate([prev, a], axis=2)

    kk, vv = with_prev(kb), with_prev(vb)
    scores = jnp.einsum('bnqkgd,bnskd->bnkgqs', qb, kk) * (A_HEAD_DIM ** -0.5)
    dist = jnp.arange(WINDOW)[:, None] + WINDOW - jnp.arange(2 * WINDOW)[None, :]
    local = (dist >= 0) & (dist < WINDOW)
    valid = (jnp.arange(nb)[:, None, None] > 0) | (jnp.arange(2 * WINDOW)[None, None, :] >= WINDOW)
    mask = local[None] & valid
    p = sink_softmax(scores, mask[None, :, None, None], sinks)
    out = jnp.einsum('bnkgqs,bnskd->bnqkgd', p, vv).reshape(B, T, A_Q_W)
    wb = min(WINDOW, T)
    return out.astype(q.dtype), k[:, T - wb:], v[:, T - wb:]


def swa_sample(q, k, v, sinks, k_buf, v_buf):
    f32 = jnp.float32
    B, T = q.shape[0], q.shape[1]
    wb = k_buf.shape[1]
    kc = jnp.concatenate([k_buf.astype(k.dtype), k], axis=1)
    vc = jnp.concatenate([v_buf.astype(v.dtype), v], axis=1)
    qg = q.astype(f32).reshape(B, T, A_KV_HEADS, A_GROUPS, A_HEAD_DIM)
    scores = jnp.einsum('btkgd,bskd->bkgts', qg, kc.astype(f32)) * (A_HEAD_DIM ** -0.5)
    dist = jnp.arange(T)[:, None] + wb - jnp.arange(wb + T)[None, :]
    mask = (dist >= 0) & (dist < WINDOW)
    p = sink_softmax(scores, mask, sinks)
    out = jnp.einsum('bkgts,bskd->btkgd', p, vc.astype(f32)).reshape(B, T, A_Q_W)
    return out.astype(q.dtype), kc[:, T:], vc[:, T:]


def layer(x, C0, n0, m0, attn, norm_mix_w, w_in, i_bias, f_bias, m_norm_w, q_norm_w, k_norm_w, sinks,
          w_branch_a, w_branch_b, w_out, norm_ffn_w, w_gate, w_up, w_down):
    B, T = x.shape[0], x.shape[1]
    hn = rms_norm(x, norm_mix_w)
    proj = hn @ w_in
    idx = np.cumsum(IN_SIZES)[:-1].tolist()
    q_m, k_m, v_m, o_m, i_m, f_m, q_a, k_a, v_a, g_a, g_b = jnp.split(proj, idx, axis=-1)

    def heads(a, h):
        return a.reshape(B, T, h, -1)

    h_m, C, n, m = mlstm_chunkwise(heads(q_m, M_HEADS), heads(k_m, M_HEADS), heads(v_m, M_HEADS),
                                   i_m + i_bias, f_m + f_bias, C0, n0, m0)
    h_m = rms_norm(h_m, m_norm_w).reshape(B, T, M_V_W)
    h_m = (h_m * jax.nn.sigmoid(o_m.astype(jnp.float32))).astype(x.dtype)
    qa = rms_norm(heads(q_a, A_HEADS), q_norm_w)
    ka = rms_norm(heads(k_a, A_KV_HEADS), k_norm_w)
    h_a, k_win, v_win = attn(qa, ka, heads(v_a, A_KV_HEADS), sinks)
    mix = jax.nn.sigmoid(g_a) * (h_m @ w_branch_a) + jax.nn.sigmoid(g_b) * (h_a @ w_branch_b)
    x = x + mix @ w_out
    hf = rms_norm(x, norm_ffn_w)
    x = x + (jax.nn.silu(hf @ w_gate) * (hf @ w_up)) @ w_down
    return x, C, n, m, k_win, v_win


def setup_inputs(seed: int = 0) -> dict:
    key = jax.random.key(seed)
    ks = jax.random.split(key, 24)
    f32 = jnp.float32
    wb = min(WINDOW, PAST_LEN)
    nrm = lambda k, shape, scale: jax.random.normal(k, shape, f32) * scale
    return {
        'x_prompt': nrm(ks[0], (BATCH, SEQ, D_MODEL), 1.0),
        'x_sample': nrm(ks[1], (DEC_BATCH, DEC_SEQ, D_MODEL), 1.0),
        'state_mlstm_C': nrm(ks[2], (DEPTH, DEC_BATCH, M_HEADS, M_DK, M_DV), 0.5),
        'state_mlstm_n': nrm(ks[3], (DEPTH, DEC_BATCH, M_HEADS, M_DK), 0.5),
        'state_mlstm_m': nrm(ks[4], (DEPTH, DEC_BATCH, M_HEADS), 1.0),
        'cache_swa_k': nrm(ks[5], (DEPTH, DEC_BATCH, wb, A_KV_HEADS, A_HEAD_DIM), 1.0),
        'cache_swa_v': nrm(ks[6], (DEPTH, DEC_BATCH, wb, A_KV_HEADS, A_HEAD_DIM), 1.0),
        'norm_mix_w': 1.0 + nrm(ks[7], (DEPTH, D_MODEL), 0.02),
        'w_in': nrm(ks[8], (DEPTH, D_MODEL, D_IN), D_MODEL ** -0.5),
        'mlstm_i_bias': nrm(ks[9], (DEPTH, M_HEADS), 0.1),
        'mlstm_f_bias': FORGET_BIAS_MEAN + nrm(ks[10], (DEPTH, M_HEADS), 0.5),
        'mlstm_norm_w': 1.0 + nrm(ks[11], (DEPTH, M_HEADS, M_DV), 0.02),
        'q_norm_w': 1.0 + nrm(ks[12], (DEPTH, A_HEAD_DIM), 0.02),
        'k_norm_w': 1.0 + nrm(ks[13], (DEPTH, A_HEAD_DIM), 0.02),
        'attn_sinks': nrm(ks[14], (DEPTH, A_HEADS), 0.5),
        'w_branch_a': nrm(ks[15], (DEPTH, M_V_W, D_MODEL), M_V_W ** -0.5),
        'w_branch_b': nrm(ks[16], (DEPTH, A_Q_W, D_MODEL), A_Q_W ** -0.5),
        'w_out': nrm(ks[17], (DEPTH, D_MODEL, D_MODEL), D_MODEL ** -0.5),
        'norm_ffn_w': 1.0 + nrm(ks[18], (DEPTH, D_MODEL), 0.02),
        'w_gate': nrm(ks[19], (DEPTH, D_MODEL, D_FF), D_MODEL ** -0.5),
        'w_up': nrm(ks[20], (DEPTH, D_MODEL, D_FF), D_MODEL ** -0.5),
        'w_down': nrm(ks[21], (DEPTH, D_FF, D_MODEL), D_FF ** -0.5),
    }


def reference(x_prompt, x_sample, state_mlstm_C, state_mlstm_n, state_mlstm_m, cache_swa_k, cache_swa_v,
              norm_mix_w, w_in, mlstm_i_bias, mlstm_f_bias, mlstm_norm_w, q_norm_w, k_norm_w, attn_sinks,
              w_branch_a, w_branch_b, w_out, norm_ffn_w, w_gate, w_up, w_down):
    f32 = jnp.float32
    bp = x_prompt.shape[0]
    yp, ys = x_prompt, x_sample
    Cp_l, np_l, mp_l, kp_l, vp_l = [], [], [], [], []
    Cs_l, ns_l, ms_l, ks_l, vs_l = [], [], [], [], []
    for l in range(DEPTH):
        w = (norm_mix_w[l], w_in[l], mlstm_i_bias[l], mlstm_f_bias[l], mlstm_norm_w[l], q_norm_w[l],
             k_norm_w[l], attn_sinks[l], w_branch_a[l], w_branch_b[l], w_out[l], norm_ffn_w[l],
             w_gate[l], w_up[l], w_down[l])
        C0 = jnp.zeros((bp, M_HEADS, M_DK, M_DV), f32)
        n0 = jnp.zeros((bp, M_HEADS, M_DK), f32)
        m0 = jnp.zeros((bp, M_HEADS), f32)
        yp, Cp, np_, mp, kp, vp = layer(yp, C0, n0, m0, swa_prompt, *w)
        attn_s = functools.partial(swa_sample, k_buf=cache_swa_k[l], v_buf=cache_swa_v[l])
        ys, Cs, ns, ms, kS, vS = layer(ys, state_mlstm_C[l], state_mlstm_n[l], state_mlstm_m[l], attn_s, *w)
        Cp_l.append(Cp); np_l.append(np_); mp_l.append(mp); kp_l.append(kp); vp_l.append(vp)
        Cs_l.append(Cs); ns_l.append(ns); ms_l.append(ms); ks_l.append(kS); vs_l.append(vS)
    return (yp, ys,
            jnp.stack(Cp_l), jnp.stack(np_l), jnp.stack(mp_l), jnp.stack(kp_l), jnp.stack(vp_l),
            jnp.stack(Cs_l), jnp.stack(ns_l), jnp.stack(ms_l), jnp.stack(ks_l), jnp.stack(vs_l))
```

```python
import contextlib
import math
import numpy as np
import concourse.bass as bass
import concourse.mybir as mybir
from concourse.bass_utils import run_bass_kernel_spmd

F32 = mybir.dt.float32
BF16 = mybir.dt.bfloat16
AF = mybir.ActivationFunctionType
ALU = mybir.AluOpType
AX = mybir.AxisListType

D = 1024
DIN = 4360
DFF = 2816
NFF = 22
C_QM, C_KM, C_VM, C_OM, C_I, C_F, C_QA, C_KA, C_VA, C_GA, C_GB = 0, 256, 512, 1024, 1536, 1540, 1544, 2056, 2184, 2312, 3336
WRES = 2312
EPS = 1e-6
BIG = 1.0e30
SEG = 2048
NPRE_FULL = 48


class Op:
    __slots__ = ("eng", "fn", "deps", "skey", "is_dma", "signal", "sem", "val", "total")

    def __init__(self, eng, fn, skey, is_dma, total):
        self.eng = eng
        self.fn = fn
        self.deps = []
        self.skey = skey
        self.is_dma = is_dma
        self.signal = False
        self.sem = None
        self.val = 0
        self.total = total


class Prog:
    ENGS = ("pe", "act", "dve", "pool", "sp")

    def __init__(self, nc):
        self.nc = nc
        self.ops = []
        self.recs = {}

    def _track(self, op, r, w):
        deps = {}
        for (sp, lo, hi) in r:
            for rec in self.recs.get(sp, ()):
                if rec[3] == "w" and rec[0] < hi and lo < rec[1]:
                    deps[id(rec[2])] = rec[2]
        for (sp, lo, hi) in w:
            for rec in self.recs.get(sp, ()):
                if rec[0] < hi and lo < rec[1]:
                    deps[id(rec[2])] = rec[2]
        for d in deps.values():
            if d is op:
                continue
            if d.eng == "pe" and op.eng == "pe" and not d.is_dma:
                continue
            op.deps.append(d)
        for (sp, lo, hi) in w:
            lst = self.recs.setdefault(sp, [])
            lst[:] = [rec for rec in lst if not (lo <= rec[0] and rec[1] <= hi)]
            lst.append([lo, hi, op, "w"])
        for (sp, lo, hi) in r:
            lst = self.recs.setdefault(sp, [])
            lst[:] = [rec for rec in lst if not (rec[3] == "r" and rec[0] == lo and rec[1] == hi and rec[2].eng == op.eng and not rec[2].is_dma and not op.is_dma)]
            lst.append([lo, hi, op, "r"])

    def add(self, eng, fn, r=(), w=(), skey=None, is_dma=False, total=False):
        op = Op(eng, fn, skey, is_dma, total)
        r2, w2 = [], []
        for (sp, lo, hi) in r:
            if sp == "ps":
                w2.append((sp, lo // 2048 * 2048, (hi + 2047) // 2048 * 2048))
            else:
                r2.append((sp, lo, hi))
        for (sp, lo, hi) in w:
            if sp == "ps":
                w2.append((sp, lo // 2048 * 2048, (hi + 2047) // 2048 * 2048))
            else:
                w2.append((sp, lo, hi))
        self._track(op, r2, w2)
        self.ops.append(op)
        return op

    def pe(self, fn, r=(), w=()):
        return self.add("pe", fn, r, w)

    def act(self, fn, r=(), w=()):
        return self.add("act", fn, r, w)

    def dve(self, fn, r=(), w=()):
        return self.add("dve", fn, r, w)

    def pool(self, fn, r=(), w=()):
        return self.add("pool", fn, r, w)

    def dma(self, q, skey, fn, r=(), w=(), total=False):
        return self.add(q, fn, r, w, skey=skey, is_dma=True, total=total)

    def emit(self, final_wait_ops=()):
        nc = self.nc
        ops = self.ops
        for op in ops:
            for d in op.deps:
                d.signal = True
        for op in final_wait_ops:
            op.signal = True
        for op in ops:
            if op.is_dma:
                op.signal = True
        stack = contextlib.ExitStack()
        sems = {}

        def get_sem(name):
            if name not in sems:
                sems[name] = stack.enter_context(nc.semaphore(name))
            return sems[name]

        counts = {}
        for op in ops:
            if not op.signal:
                continue
            if op.is_dma:
                sname = "d_" + str(op.skey)
                inc = 16
            else:
                sname = "e_" + op.eng
                inc = 1
            counts[sname] = counts.get(sname, 0) + inc
            op.sem = get_sem(sname)
            op.val = counts[sname]
        for op in ops:
            if op.signal and op.is_dma and op.total:
                op.val = counts["d_" + str(op.skey)]
        self.n_sems = len(sems)
        per_eng = {e: [o for o in ops if o.eng == e] for e in self.ENGS}
        engobj = {"pe": "tensor", "act": "scalar", "dve": "vector", "pool": "gpsimd", "sp": "sync"}
        self.n_waits = 0
        with stack:
            with nc.Block() as block:
                def make(ename):
                    def body(eng):
                        waited = {}
                        for op in per_eng[ename]:
                            need = {}
                            for d in op.deps:
                                k = id(d.sem)
                                if need.get(k, (None, 0))[1] < d.val:
                                    need[k] = (d.sem, d.val)
                            for k, (s, v) in need.items():
                                if waited.get(k, 0) >= v:
                                    continue
                                eng.wait_ge(s, v)
                                waited[k] = v
                                self.n_waits += 1
                            ins = op.fn(eng)
                            if op.signal:
                                ins.then_inc(op.sem, 16 if op.is_dma else 1)
                        if ename == "sp":
                            for sname, cnt in counts.items():
                                if sname.startswith("d_"):
                                    eng.wait_ge(sems[sname], cnt)
                    return body

                for ename in self.ENGS:
                    getattr(block, engobj[ename])(make(ename))


class Buf:
    def __init__(self, ap, space, lo, hi):
        self.ap = ap
        self.space = space
        self.lo = lo
        self.hi = hi

    @property
    def k(self):
        return (self.space, self.lo, self.hi)

    def sub(self, i, n):
        sz = (self.hi - self.lo) // n
        return (self.space, self.lo + i * sz, self.lo + (i + 1) * sz)

    def rng(self, lo_b, hi_b):
        return (self.space, self.lo + lo_b, self.lo + hi_b)


def _shape_view(ap, shape):
    if len(shape) == 1:
        return ap
    if len(shape) == 2:
        return ap.rearrange("p (a b) -> p a b", a=shape[0])
    if len(shape) == 3:
        return ap.rearrange("p (a b c) -> p a b c", a=shape[0], b=shape[1])
    raise ValueError(shape)


class Arena:
    def __init__(self, tensor, space, nbytes):
        self.t = tensor
        self.space = space
        self.nbytes = nbytes
        self.top = 0

    def at(self, lo, shape, dt):
        n = int(np.prod(shape))
        esz = 2 if dt == BF16 else 4
        hi = lo + n * esz
        assert hi <= self.nbytes, (self.space, hi, self.nbytes)
        v = self.t[:, lo // 4:(hi + 3) // 4]
        if dt != F32:
            v = v.bitcast(dt)[:, 0:n]
        return Buf(_shape_view(v, shape), self.space, lo, hi)

    def alloc(self, shape, dt):
        lo = (self.top + 31) // 32 * 32
        b = self.at(lo, shape, dt)
        self.top = b.hi
        return b


class _Stop(Exception):
    pass


def build_program(NSUP=4, NPRE=NPRE_FULL, SAMPLE=True, STOP=None):
    nc = bass.Bass("TRN2", target_bir_lowering=False)

    def ck(n):
        if STOP is not None and n >= STOP:
            raise _Stop()
    NTOK = NSUP * 512

    def din(name, shape, dt=F32):
        return nc.dram_tensor(name, list(shape), dt, kind="ExternalInput").ap()

    def dout(name, shape):
        return nc.dram_tensor(name, list(shape), F32, kind="ExternalOutput").ap()

    def dint(name, shape, dt):
        return nc.dram_tensor(name, list(shape), dt, kind="Internal").ap()

    xp = din("xp", [NTOK, D])
    xpre = din("xpre", [max(NPRE, 1) * 128, D])
    premask = din("premask", [4, max(NPRE, 1) * 128])
    premask2 = din("premask2", [4, max(NPRE, 1) * 128])
    pmg_d = din("pmg", [4, 16])
    pmg2_d = din("pmg2", [4, 16])
    xhalo = din("xhalo", [128, D])
    halov = din("halov", [128, 1])
    w_in = din("w_in", [D, DIN])
    w_a = din("w_a", [512, D])
    w_b = din("w_b", [512, D])
    w_out = din("w_out", [D, D])
    w_gate = din("w_gate", [D, DFF])
    w_up = din("w_up", [D, DFF])
    w_down = din("w_down", [DFF, D])
    nmw_t = din("nmw_t", [128, 8])
    nfw_t = din("nfw_t", [128, 8])
    ibias = din("ibias", [4, 1])
    fbias = din("fbias", [4, 1])
    mnw = din("mnw", [1, 512])
    qnw_dup = din("qnw_dup", [128, 1])
    knw_dup = din("knw_dup", [128, 1])
    knw_row = din("knw_row", [1, 128])
    sinks = din("sinks", [1, 8])
    if SAMPLE:
        xs_d = din("xs", [128, D])
        sC = din("sC", [16, 4, 64, 128])
        sn_t = din("sn_t", [128, 32])
        sm_t = din("sm_t", [4, 16])
        ckd = din("ck", [16, 128, 128])
        cvd = din("cv", [16, 128, 128])

    yp = dout("yp", [NTOK, D])
    Cp = dout("Cp", [4, 64, 128])
    np_t = dout("np_t", [128, 2])
    mp = dout("mp", [4, 1])
    kwp = dout("kwp", [128, 128])
    vwp = dout("vwp", [128, 128])
    if SAMPLE:
        ys = dout("ys", [128, D])
        Cs = dout("Cs", [16, 4, 64, 128])
        ns_t = dout("ns_t", [128, 32])
        ms_t = dout("ms_t", [4, 16])
        kws = dout("kws", [16, 128, 128])
        vws = dout("vws", [16, 128, 128])

    s5_scr = dint("s5_scr", [8, 128, 3072], BF16)
    gu_scr = dint("gu_scr", [NFF, 128, 2048], BF16)
    wd_scr = dint("wd_scr", [DFF, D], BF16)

    st = contextlib.ExitStack()
    SB_BYTES = 212800
    sb_t = st.enter_context(nc.sbuf_tensor("arena", [128, SB_BYTES // 4], F32))
    ps_t = st.enter_context(nc.psum_tensor("psum", [128, 4096], F32))
    SB = Arena(sb_t, "sb", SB_BYTES)
    PS = Arena(ps_t, "ps", 16384)
    P = Prog(nc)
    outs = []

    def bank(b, shape=(512,), dt=F32, off=0):
        return PS.at(b * 2048 + off, shape, dt)

    win = SB.alloc([8, WRES], BF16)
    wout = SB.alloc([8, D], BF16)
    s5ring = [SB.alloc([3072], BF16) for _ in range(2)]
    guring = [SB.alloc([2048], BF16) for _ in range(3)]
    wdring = [SB.alloc([1024], BF16) for _ in range(2)]
    wdring = wdring + [SB.at(guring[j].lo + h * 2048, [1024], BF16) for j in range(3) for h in range(2)]
    NWD = len(wdring)
    NSLOT = 8
    xslot = [SB.alloc([D], F32) for _ in range(NSLOT)]
    xsb = [SB.alloc([D], BF16) for _ in range(2)]
    hnT = SB.alloc([8, 512], BF16)
    hT = SB.alloc([NFF, 512], BF16)
    qT = SB.at(hT.lo, [2, 512], BF16)
    kT = SB.at(hT.lo + 2048, [2, 512], BF16)
    hmT = SB.at(hT.lo + 4096, [4, 512], BF16)
    haT = SB.at(hT.lo + 8192, [4, 512], BF16)
    mixT = SB.at(hT.lo + 12288, [8, 512], BF16)
    ident = SB.alloc([128], BF16)
    onesb = SB.alloc([128], BF16)
    maskU = SB.alloc([128], BF16)
    maskL = SB.alloc([128], BF16)
    i4 = SB.alloc([4], F32)
    halfm = SB.alloc([128], F32)
    psel = SB.alloc([2], F32)
    ones4 = SB.alloc([512], F32)
    zeros4 = SB.alloc([512], F32)
    mhalf = SB.alloc([8], F32)
    nmw = SB.alloc([8], F32)
    nfw = SB.alloc([8], F32)
    nmwb = SB.alloc([8], F32)
    ib = SB.alloc([1], F32)
    fbn = SB.alloc([1], F32)
    mnwh = SB.alloc([512], F32)
    qw = SB.alloc([1], F32)
    kw = SB.alloc([1], F32)
    kwrow = SB.alloc([128], F32)
    sinke = SB.alloc([8], F32)
    hvalid = SB.alloc([1], F32)
    pmg = SB.alloc([16], F32)
    pmg2 = SB.alloc([16], F32)
    gX1 = SB.alloc([512], F32)
    gX2 = SB.alloc([512], F32)
    gB = SB.alloc([512], F32)
    gM = SB.alloc([512], F32)
    g8 = SB.alloc([512], F32)
    gcar = SB.alloc([8], F32)
    gg = SB.alloc([16], F32)
    gsel = SB.alloc([128], F32)
    utok = [SB.alloc([8], F32) for _ in range(2)]
    gdk = [SB.alloc([2], F32) for _ in range(2)]
    stat = [SB.alloc([32], F32) for _ in range(2)]
    vext = [SB.alloc([4, 130], BF16) for _ in range(2)]
    tho = [SB.alloc([512], F32) for _ in range(2)]
    ku = [SB.alloc([256], BF16) for _ in range(2)]
    qn = [SB.alloc([512], BF16) for _ in range(2)]
    kdup = [SB.alloc([2, 2, 64], BF16) for _ in range(2)]
    vaext = [SB.alloc([2, 66], BF16) for _ in range(4)]
    kaT = [SB.alloc([2, 128], BF16) for _ in range(4)]
    rkr = [SB.alloc([2], F32) for _ in range(4)]
    qaT = [SB.alloc([4, 128], BF16) for _ in range(2)]
    PT = SB.alloc([2, 8, 128], BF16)
    PTm = SB.alloc([4, 128], BF16)
    S = SB.alloc([2, 130], F32)
    Sp = SB.alloc([2, 130], F32)
    Spb = SB.alloc([2, 130], BF16)
    hm = [SB.alloc([512], BF16) for _ in range(2)]
    ha = [SB.alloc([512], BF16) for _ in range(2)]
    tmpA = SB.alloc([512], F32)
    tmpB = SB.alloc([512], F32)
    tmpC = SB.alloc([512], F32)
    tmpD = SB.alloc([512], F32)
    kwout = SB.alloc([128], F32)
    vwout = SB.alloc([128], F32)
    ncol = SB.alloc([2], F32)
    junk = SB.alloc([D], BF16)
    vcext_s = SB.alloc([16, 2, 65], BF16)
    print("SBUF bytes used", SB.top)

    hnT2 = SB.at(hT.lo, [8, 512], BF16)
    gX1b = SB.at(hT.lo + 8192, [512], F32)
    gX2b = SB.at(hT.lo + 8192 + 2048, [512], F32)
    gmask2 = SB.at(hT.lo + 8192 + 4096, [512], F32)
    gmask = SB.at(hT.lo + 8192 + 6144, [512], F32)
    bufsets = [dict(X1=gX1, X2=gX2, ggo=0, hn=hnT), dict(X1=gX1b, X2=gX2b, ggo=8, hn=hnT2)]
    cur = dict(bufsets[0])
    def A(buf):
        return buf.ap

    def dve(fn, r, w):
        return P.dve(fn, r=r, w=w)

    def gpo(fn, r, w):
        return P.pool(fn, r=r, w=w)

    def mm(out_ap, lhsT, rhs, start, stop, r, w):
        return P.pe(lambda e: e.matmul(out_ap, lhsT=lhsT, rhs=rhs, start=start, stop=stop), r=r, w=w)

    def tr(out_ap, in_ap, r, w):
        return P.pe(lambda e: e.transpose(out=out_ap, in_=in_ap, identity=A(ident)[:, :]), r=r + [ident.k], w=w)

    def pow_mhalf(ap_io, key, n):
        return P.pool(lambda e: e.tensor_tensor(out=ap_io, in0=ap_io, in1=A(mhalf)[0:ap_io.shape[0], 0:n], op=ALU.pow),
                      r=[key, mhalf.k], w=[key])

    P.pool(lambda e: e.memset(A(onesb)[:, :], 1.0), w=[onesb.k])
    P.pool(lambda e: e.affine_select(out=A(ident)[:, :], in_=A(onesb)[:, :], pattern=[[1, 128]], compare_op=ALU.is_equal,
                                     fill=0.0, base=0, channel_multiplier=-1), r=[onesb.k], w=[ident.k])
    P.pool(lambda e: e.affine_select(out=A(maskU)[:, :], in_=A(onesb)[:, :], pattern=[[1, 128]], compare_op=ALU.is_ge,
                                     fill=0.0, base=0, channel_multiplier=-1), r=[onesb.k], w=[maskU.k])
    P.pool(lambda e: e.affine_select(out=A(maskL)[:, :], in_=A(onesb)[:, :], pattern=[[-1, 128]], compare_op=ALU.is_gt,
                                     fill=0.0, base=0, channel_multiplier=1), r=[onesb.k], w=[maskL.k])
    P.pool(lambda e: e.memset(A(ones4)[:, :], 1.0), w=[ones4.k])
    P.pool(lambda e: e.memset(A(zeros4)[:, :], 0.0), w=[zeros4.k])
    P.pool(lambda e: e.memset(A(mhalf)[:, :], -0.5), w=[mhalf.k])
    P.pool(lambda e: e.affine_select(out=A(i4)[0:4, 0:4], in_=A(ones4)[0:4, 0:4], pattern=[[1, 4]], compare_op=ALU.is_equal,
                                     fill=0.0, base=0, channel_multiplier=-1), r=[ones4.k], w=[i4.k])
    P.pool(lambda e: e.memset(A(halfm)[0:4, :], 0.0), w=[halfm.k])
    P.dve(lambda e: e.tensor_tensor(out=A(gcar)[0:4, 2:3], in0=A(i4)[0:4, 0:1], in1=A(i4)[0:4, 2:3], op=ALU.add), r=[i4.k], w=[gcar.k])
    P.dve(lambda e: e.tensor_tensor(out=A(gcar)[0:4, 3:4], in0=A(i4)[0:4, 1:2], in1=A(i4)[0:4, 3:4], op=ALU.add), r=[i4.k, gcar.k], w=[gcar.k])
    P.dve(lambda e: e.tensor_scalar(out=A(halfm)[0:4, 0:64], in0=A(ones4)[0:4, 0:64], scalar1=A(gcar)[0:4, 2:3], scalar2=None, op0=ALU.mult),
          r=[ones4.k, gcar.k], w=[halfm.k])
    P.dve(lambda e: e.tensor_scalar(out=A(halfm)[0:4, 64:128], in0=A(ones4)[0:4, 0:64], scalar1=A(gcar)[0:4, 3:4], scalar2=None, op0=ALU.mult),
          r=[ones4.k, gcar.k, halfm.k], w=[halfm.k])
    P.dve(lambda e: e.tensor_tensor(out=A(psel)[0:4, 0:1], in0=A(i4)[0:4, 0:1], in1=A(i4)[0:4, 1:2], op=ALU.add), r=[i4.k], w=[psel.k])
    P.dve(lambda e: e.tensor_tensor(out=A(psel)[0:4, 1:2], in0=A(i4)[0:4, 2:3], in1=A(i4)[0:4, 3:4], op=ALU.add), r=[i4.k, psel.k], w=[psel.k])

    def small_load(buf, src, parts, n):
        return P.dma("sp", "setupS", lambda e: e.dma_start(out=A(buf)[0:parts, 0:n], in_=src), w=[buf.k], total=True)

    small_load(nmw, nmw_t, 128, 8)
    small_load(nfw, nfw_t, 128, 8)
    small_load(nmwb, nmw_t, 128, 8)
    small_load(ib, ibias, 4, 1)
    small_load(fbn, fbias, 4, 1)
    small_load(mnwh, mnw.partition_broadcast(128), 128, 512)
    small_load(qw, qnw_dup, 128, 1)
    small_load(kw, knw_dup, 128, 1)
    small_load(kwrow, knw_row.partition_broadcast(128), 128, 128)
    small_load(sinke, sinks.partition_broadcast(128), 128, 8)
    small_load(hvalid, halov, 128, 1)
    small_load(pmg, pmg_d, 4, 16)
    small_load(pmg2, pmg2_d, 4, 16)
    P.dve(lambda e: e.tensor_scalar(out=A(fbn)[0:4, :], in0=A(fbn)[0:4, :], scalar1=-1.0, scalar2=None, op0=ALU.mult), r=[fbn.k], w=[fbn.k])
    P.dve(lambda e: e.tensor_scalar(out=A(mnwh)[:, :], in0=A(mnwh)[:, :], scalar1=0.5, scalar2=None, op0=ALU.mult), r=[mnwh.k], w=[mnwh.k])
    P.dve(lambda e: e.tensor_scalar(out=A(qw)[:, :], in0=A(qw)[:, :], scalar1=0.125, scalar2=None, op0=ALU.mult), r=[qw.k], w=[qw.k])
    P.act(lambda e: e.activation(out=A(sinke)[:, :], in_=A(sinke)[:, :], func=AF.Exp), r=[sinke.k], w=[sinke.k])
    for b_ in vext:
        P.pool(lambda e, b_=b_: e.memset(A(b_)[:, :, 128:130], 1.0), w=[b_.k])
    for b_ in vaext:
        P.pool(lambda e, b_=b_: e.memset(A(b_)[:, :, 64:66], 1.0), w=[b_.k])
    P.pool(lambda e: e.memset(A(S)[:, :, :], 0.0), w=[S.k])
    P.pool(lambda e: e.memset(A(gcar)[0:4, 0:2], 0.0), r=[gcar.k], w=[gcar.k])

    for k in range(8):
        P.dma("pool", "setupW", lambda e, k=k: e.dma_start(out=A(win)[:, k, :], in_=w_in[k * 128:(k + 1) * 128, 0:WRES]),
              w=[win.sub(k, 8)], total=True)
    for k in range(8):
        P.dma("pool", "setupW2", lambda e, k=k: e.dma_start(out=A(wout)[:, k, :], in_=w_out[k * 128:(k + 1) * 128, :]),
              w=[wout.sub(k, 8)], total=True)

    for k in range(8):
        P.dve(lambda e, k=k: e.tensor_scalar(out=A(win)[:, k, :], in0=A(win)[:, k, :], scalar1=A(nmw)[:, k:k + 1], scalar2=None, op0=ALU.mult),
              r=[win.sub(k, 8), nmw.k], w=[win.sub(k, 8)])

    scratch_jobs = []
    scr_ctr = [0]
    NTHR = 6

    def SCR(fn, w):
        i = scr_ctr[0] % NTHR
        scr_ctr[0] += 1
        return P.dma("pool", "scr%d" % i, fn, w=list(w) + [("scrthr", i, i + 1)])
    for c in range(8):
        def j_ga(c=c):
            SCR(lambda e: e.dma_start(
                out=s5_scr[c, :, 0:1024].rearrange("p (k n) -> p k n", k=8),
                in_=w_in[:, C_GA + c * 128:C_GA + (c + 1) * 128].rearrange("(k p) n -> p k n", p=128)), w=[("s5scr", c, c + 1)])
            SCR(lambda e: e.dma_start(
                out=s5_scr[c, :, 1024:2048].rearrange("p (k n) -> p k n", k=8),
                in_=w_in[:, C_GB + c * 128:C_GB + (c + 1) * 128].rearrange("(k p) n -> p k n", p=128)), w=[("s5scr", c, c + 1)])
            SCR(lambda e: e.dma_start(
                out=s5_scr[c, :, 2048:2560].rearrange("p (k n) -> p k n", k=4),
                in_=w_a[:, c * 128:(c + 1) * 128].rearrange("(k p) n -> p k n", p=128)), w=[("s5scr", c, c + 1)])
            SCR(lambda e: e.dma_start(
                out=s5_scr[c, :, 2560:3072].rearrange("p (k n) -> p k n", k=4),
                in_=w_b[:, c * 128:(c + 1) * 128].rearrange("(k p) n -> p k n", p=128)), w=[("s5scr", c, c + 1)])
        scratch_jobs.append(j_ga)
    for c in range(NFF):
        def j_gu(c=c):
            SCR(lambda e: e.dma_start(
                out=gu_scr[c, :, 0:1024].rearrange("p (k n) -> p k n", k=8),
                in_=w_gate[:, c * 128:(c + 1) * 128].rearrange("(k p) n -> p k n", p=128)), w=[("guscr", c, c + 1)])
            SCR(lambda e: e.dma_start(
                out=gu_scr[c, :, 1024:2048].rearrange("p (k n) -> p k n", k=8),
                in_=w_up[:, c * 128:(c + 1) * 128].rearrange("(k p) n -> p k n", p=128)), w=[("guscr", c, c + 1)])
        scratch_jobs.append(j_gu)
    for q in range(4):
        def j_wd(q=q):
            SCR(lambda e: e.dma_start(out=wd_scr[q * 704:(q + 1) * 704, :], in_=w_down[q * 704:(q + 1) * 704, :]),
                  w=[("wdscr", q, q + 1)])
        scratch_jobs.append(j_wd)

    def s5_fixup(c):
        rs = s5ring[c % 2]
        P.dma("sp", "s5r%d" % (c % 2), lambda e: e.dma_start(out=A(rs)[:, 0:2048], in_=s5_scr[c, :, 0:2048]), r=[("s5scr", c, c + 1)], w=[rs.k])
        dve(lambda e: e.tensor_tensor(out=A(rs)[:, 0:2048].rearrange("p (a k n) -> p a k n", a=2, k=8),
                                      in0=A(rs)[:, 0:2048].rearrange("p (a k n) -> p a k n", a=2, k=8),
                                      in1=A(nmwb)[:, 0:8].unsqueeze(1).unsqueeze(3).to_broadcast([128, 2, 8, 128]), op=ALU.mult),
            [rs.k, nmwb.k], [rs.k])
        P.dma("sp", "s5r%d" % (c % 2), lambda e: e.dma_start(out=s5_scr[c, :, 0:2048], in_=A(rs)[:, 0:2048]), r=[rs.k], w=[("s5scr", c, c + 1)])

    def pump_scratch(n):
        for _ in range(n):
            if scratch_jobs:
                scratch_jobs.pop(0)()

    slot_ctr = [0]
    nslot_active = [NSLOT]

    def load_tile(src_rows):
        s = slot_ctr[0] % nslot_active[0]
        slot_ctr[0] += 1
        P.dma("sp", "x%d" % s, lambda e: e.dma_start(out=A(xslot[s])[:, :], in_=src_rows), w=[xslot[s].k])
        return s

    xsb_ctr = [0]

    def hk(buf, i):
        return ("v%d" % buf.lo, i, i + 1)

    def hk_all(buf):
        return ("v%d" % buf.lo, 0, 4)

    def norm_a1(s, act_rstd=False, col=0, xb=None):
        if xb is None:
            xb = xsb[xsb_ctr[0] % 2]
            xsb_ctr[0] += 1
        sq = stat[0]
        c0 = 2 * col
        k0, k1 = sq.rng(c0 * 4, c0 * 4 + 4), sq.rng(c0 * 4 + 4, c0 * 4 + 8)
        P.act(lambda e: e.activation(out=A(junk)[:, :], in_=A(xslot[s])[:, :], func=AF.Square, accum_out=A(sq)[:, c0:c0 + 1]),
              r=[xslot[s].k], w=[junk.k, k0])
        if act_rstd:
            P.act(lambda e: e.activation(out=A(sq)[:, c0 + 1:c0 + 2], in_=A(sq)[:, c0:c0 + 1], func=AF.Ln, scale=1.0 / D, bias=EPS), r=[k0], w=[k1])
            P.act(lambda e: e.activation(out=A(sq)[:, c0 + 1:c0 + 2], in_=A(sq)[:, c0 + 1:c0 + 2], func=AF.Exp, scale=-0.5), r=[k1], w=[k1])
        else:
            dve(lambda e: e.tensor_scalar(out=A(sq)[:, c0 + 1:c0 + 2], in0=A(sq)[:, c0:c0 + 1], scalar1=1.0 / D, scalar2=EPS, op0=ALU.mult, op1=ALU.add),
                [k0], [k1])
            pow_mhalf(A(sq)[:, c0 + 1:c0 + 2], k1, 1)
        return xb, sq, c0

    def norm_scale(s, xb, sq, c0):
        dve(lambda e: e.tensor_scalar(out=A(xb)[:, :], in0=A(xslot[s])[:, :], scalar1=A(sq)[:, c0 + 1:c0 + 2], scalar2=None, op0=ALU.mult),
            [xslot[s].k, sq.rng(c0 * 4 + 4, c0 * 4 + 8)], [xb.k])

    def norm_a2(xb, dstT, ti, wts, act_evac=False):
        col0 = ti * 128
        tb = bank(0, (8, 128), BF16)
        for k in range(8):
            tr(A(tb)[:, k, :], A(xb)[:, k * 128:(k + 1) * 128], [xb.k], [tb.sub(k, 8)])
        if wts is None:
            P.act(lambda e: e.activation(out=dstT.ap[:, :, col0:col0 + 128], in_=A(tb)[:, :, :], func=AF.Copy), r=[tb.k], w=[hk(dstT, ti)])
        elif act_evac:
            for k in range(8):
                P.act(lambda e, k=k: e.activation(out=dstT.ap[:, k, col0:col0 + 128], in_=A(tb)[:, k, :], func=AF.Identity, scale=A(wts)[:, k:k + 1]),
                      r=[tb.k, wts.k], w=[hk(dstT, ti)])
        else:
            dve(lambda e: e.tensor_tensor(out=dstT.ap[:, :, col0:col0 + 128], in0=A(tb)[:, :, :],
                                          in1=A(wts)[:, 0:8].unsqueeze(2).to_broadcast([128, 8, 128]), op=ALU.mult),
                [tb.k, wts.k], [hk(dstT, ti)])

    def norm_group_pre(slots_, act_rstd=False):
        pend = [norm_a1(s_, col=t_, act_rstd=act_rstd) for t_, s_ in enumerate(slots_)]
        norm_scale(slots_[0], *pend[0])
        if len(slots_) > 1:
            norm_scale(slots_[1], *pend[1])
        return pend

    def norm_group_post(slots_, pend, dstT, wts):
        for t_, s_ in enumerate(slots_):
            norm_a2(pend[t_][0], dstT, t_, wts)
            if t_ + 2 < len(slots_):
                norm_scale(slots_[t_ + 2], *pend[t_ + 2])

    def norm_group(slots_, dstT, wts):
        norm_group_post(slots_, norm_group_pre(slots_), dstT, wts)

    def norm_transpose(s, dstT, col0, wts):
        xb, sq, c0 = norm_a1(s)
        norm_scale(s, xb, sq, c0)
        norm_a2(xb, dstT, col0 // 128, wts)

    def proj_tok(colsT, col_lo, ncols, pbank):
        for k in range(8):
            mm(A(pbank)[:, 0:ncols], cur["hn"].ap[:, k, colsT:colsT + 128], A(win)[:, k, col_lo:col_lo + ncols], k == 0, k == 7,
               [hk(cur["hn"], colsT // 128), win.k], [pbank.rng(0, ncols * 4)])

    tile_ctr = [0]
    att_ctr = [0]

    def gates_a(N, c0, mask_cols, groups, sample_m0=None):
        gX1_, gX2_, ggo, hn_ = cur["X1"], cur["X2"], cur["ggo"], cur["hn"]
        pif = bank(2)
        for k in range(8):
            mm(A(pif)[0:8, 0:N], A(win)[:, k, C_I:C_I + 8], hn_.ap[:, k, c0:c0 + N], k == 0, k == 7, [win.k, hk_all(hn_)], [pif.k])
        X1, X2, Bn, M = A(gX1_)[0:4, 0:N], A(gX2_)[0:4, 0:N], A(gB)[0:4, 0:N], A(gM)[0:4, 0:N]
        P.act(lambda e: e.activation(out=A(g8)[0:8, 0:N], in_=A(pif)[0:8, 0:N], func=AF.Copy), r=[pif.k], w=[g8.k])
        P.dma("sp", "g8f", lambda e: e.dma_start(out=X2, in_=A(g8)[4:8, 0:N]), r=[g8.k], w=[gX2_.k])

    def gates_b_gen(N, c0, mask_cols, groups, sample_m0=None, defer=False):
        gX1_, gX2_, ggo, hn_ = cur["X1"], cur["X2"], cur["ggo"], cur["hn"]
        X1, X2, Bn, M = A(gX1_)[0:4, 0:N], A(gX2_)[0:4, 0:N], A(gB)[0:4, 0:N], A(gM)[0:4, 0:N]
        deferred = []
        if mask_cols is None:
            P.act(lambda e: e.activation(out=X1, in_=A(g8)[0:4, 0:N], func=AF.Identity, bias=A(ib)[0:4, 0:1]), r=[g8.k, ib.k], w=[gX1_.k])
        P.act(lambda e: e.activation(out=X2, in_=X2, func=AF.Exp, bias=A(fbn)[0:4, 0:1], scale=-1.0), r=[gX2_.k, fbn.k], w=[gX2_.k])
        P.act(lambda e: e.activation(out=X2, in_=X2, func=AF.Ln, bias=1.0), r=[gX2_.k], w=[gX2_.k])
        if mask_cols is not None:
            gq = mask_cols // 512
            dve(lambda e: e.tensor_scalar(out=X2, in0=X2, scalar1=A(pmg)[0:4, gq:gq + 1], scalar2=None, op0=ALU.mult), [gX2_.k, pmg.k], [gX2_.k])
            dve(lambda e: e.tensor_scalar(out=X1, in0=A(g8)[0:4, 0:N], scalar1=A(ib)[0:4, 0:1], scalar2=A(pmg2)[0:4, gq:gq + 1], op0=ALU.add, op1=ALU.add),
                [g8.k, ib.k, pmg2.k], [gX1_.k])
        if sample_m0 is None:
            dve(lambda e: e.tensor_tensor_scan(out=Bn, data0=A(ones4)[0:4, 0:N], data1=X2, initial=A(gcar)[0:4, 0:1], op0=ALU.mult, op1=ALU.add),
                [ones4.k, gX2_.k, gcar.k], [gB.k])
            dve(lambda e: e.tensor_tensor(out=X1, in0=X1, in1=Bn, op=ALU.add), [gX1_.k, gB.k], [gX1_.k])
            yield None
            dve(lambda e: e.tensor_tensor_scan(out=M, data0=A(zeros4)[0:4, 0:N], data1=X1, initial=A(gcar)[0:4, 1:2], op0=ALU.add, op1=ALU.max),
                [zeros4.k, gX1_.k, gcar.k], [gM.k])
        else:
            rmul, radd, m0row = sample_m0[0:3]
            dve(lambda e: e.tensor_tensor_scan(out=Bn, data0=rmul, data1=X2, initial=0.0, op0=ALU.mult, op1=ALU.add),
                [gX2_.k, tmpA.k], [gB.k])
            dve(lambda e: e.tensor_tensor(out=X1, in0=X1, in1=Bn, op=ALU.add), [gX1_.k, gB.k], [gX1_.k])
            dve(lambda e: e.tensor_tensor(out=A(tmpC)[0:4, 0:N], in0=X1, in1=m0row, op=ALU.max), [gX1_.k, tmpB.k], [tmpC.k])
            dve(lambda e: e.tensor_tensor_scan(out=M, data0=radd, data1=A(tmpC)[0:4, 0:N], initial=0.0, op0=ALU.add, op1=ALU.max),
                [tmpC.k, tmpA.k], [gM.k])
        G, GS = groups
        Mv = M.rearrange("p (g s) -> p g s", g=G)
        Me = Mv[:, :, GS - 1:GS].to_broadcast([4, G, GS])
        if sample_m0 is None:
            gs = A(gcar)[0:4, 2:2 + G]
            dve(lambda e: e.tensor_copy(out=gs[:, 0:1], in_=A(gcar)[0:4, 1:2]), [gcar.k], [gcar.k])
            if G > 1:
                dve(lambda e: e.tensor_copy(out=gs[:, 1:G], in_=Mv[:, 0:G - 1, GS - 1]), [gM.k, gcar.k], [gcar.k])
            dve(lambda e: e.tensor_tensor(out=gs, in0=gs, in1=Mv[:, :, GS - 1], op=ALU.subtract), [gM.k, gcar.k], [gcar.k])
            deferred.append(lambda: P.act(lambda e: e.activation(out=A(gg)[0:4, ggo:ggo + G], in_=gs, func=AF.Exp), r=[gcar.k], w=[gg.rng(ggo * 4, (ggo + G) * 4)]))
            if not defer:
                deferred.pop()()
        else:
            m0c = sample_m0[3]
            dve(lambda e: e.tensor_tensor(out=A(gg)[0:4, ggo:ggo + G], in0=m0c, in1=Mv[:, :, GS - 1], op=ALU.subtract), [gM.k, tmpD.k], [gg.rng(ggo * 4, (ggo + G) * 4)])
            P.act(lambda e: e.activation(out=A(gg)[0:4, ggo:ggo + G], in_=A(gg)[0:4, ggo:ggo + G], func=AF.Exp), r=[gg.rng(ggo * 4, (ggo + G) * 4)], w=[gg.rng(ggo * 4, (ggo + G) * 4)])
        dve(lambda e: e.tensor_tensor(out=X2.rearrange("p (g s) -> p g s", g=G), in0=X1.rearrange("p (g s) -> p g s", g=G), in1=Me, op=ALU.subtract),
            [gX1_.k, gM.k], [gX2_.k])
        deferred.append(lambda: P.act(lambda e: e.activation(out=X2, in_=X2, func=AF.Exp, bias=math.log(0.125)), r=[gX2_.k], w=[gX2_.k]))
        if not defer:
            deferred.pop()()
        dve(lambda e: e.tensor_tensor(out=X1.rearrange("p (g s) -> p g s", g=G), in0=Bn.rearrange("p (g s) -> p g s", g=G), in1=Me, op=ALU.subtract),
            [gB.k, gM.k], [gX1_.k])
        deferred.append(lambda: P.act(lambda e: e.activation(out=X1, in_=X1, func=AF.Exp), r=[gX1_.k], w=[gX1_.k]))
        if not defer:
            deferred.pop()()
        if sample_m0 is None:
            dve(lambda e: e.tensor_copy(out=A(gcar)[0:4, 0:1], in_=Bn[:, N - 1:N]), [gB.k, gcar.k], [gcar.k])
            dve(lambda e: e.tensor_copy(out=A(gcar)[0:4, 1:2], in_=M[:, N - 1:N]), [gM.k, gcar.k], [gcar.k])
        yield deferred


    def gates_b(N, c0, mask_cols, groups, sample_m0=None, defer=False):
        out = None
        for out in gates_b_gen(N, c0, mask_cols, groups, sample_m0, defer):
            pass
        return out

    def gates(N, c0, mask_cols, groups, sample_m0=None):
        gates_a(N, c0, mask_cols, groups, sample_m0)
        gates_b(N, c0, mask_cols, groups, sample_m0)

    def gate_tok(tcol, gi, with_gdk=True):
        par = tile_ctr[0] % 2
        gX1_, gX2_, ggo = cur["X1"], cur["X2"], cur["ggo"]
        pg = bank(3)
        mm(A(pg)[:, 0:4], A(gX2_)[0:4, tcol:tcol + 128], A(i4)[0:4, 0:4], True, True, [gX2_.k, i4.k], [pg.rng(0, 16)])
        mm(A(pg)[:, 4:8], A(gX1_)[0:4, tcol:tcol + 128], A(i4)[0:4, 0:4], True, True, [gX1_.k, i4.k], [pg.rng(16, 32)])
        if with_gdk:
            dve(lambda e: e.tensor_scalar(out=A(gsel)[0:4, :], in0=A(halfm)[0:4, :], scalar1=A(gg)[0:4, ggo + gi:ggo + gi + 1], scalar2=None, op0=ALU.mult),
                [halfm.k, gg.rng((ggo + gi) * 4, (ggo + gi + 1) * 4)], [gsel.k])
            mm(A(pg)[:, 8:10], A(gsel)[0:4, :], A(psel)[0:4, 0:2], True, True, [gsel.k, psel.k], [pg.rng(32, 40)])
        dve(lambda e: e.tensor_copy(out=A(utok[par])[:, 0:8], in_=A(pg)[:, 0:8]), [pg.rng(0, 32)], [utok[par].k])
        if with_gdk:
            dve(lambda e: e.tensor_copy(out=A(gdk[par])[:, 0:2], in_=A(pg)[:, 8:10]), [pg.rng(32, 40)], [gdk[par].k])

    def state_update(par, pk_ap, pk_key, decay=True):
        dve(lambda e: e.tensor_tensor(out=A(ku[par])[:, :].rearrange("p (h d) -> p h d", h=4), in0=pk_ap[:, 0:256].rearrange("p (h d) -> p h d", h=4),
                                      in1=A(utok[par])[:, 0:4].unsqueeze(2).to_broadcast([128, 4, 64]), op=ALU.mult),
            [pk_key, utok[par].k], [ku[par].k])
        if decay:
            dve(lambda e: e.tensor_tensor(out=A(Sp)[:, :, 0:129], in0=A(S)[:, :, 0:129], in1=A(gdk[par])[:, 0:2].unsqueeze(2).to_broadcast([128, 2, 129]), op=ALU.mult),
                [S.k, gdk[par].k], [Sp.k])

    def state_update_fused(par):
        pu0 = bank(6, (2, 129))
        pu1 = bank(7, (2, 129))
        for h in range(4):
            pr = h // 2
            pu = pu0 if pr == 0 else pu1
            mm(A(pu)[:, h % 2, :], A(ku[par])[:, pr * 128:(pr + 1) * 128], A(vext[par])[:, h, 0:129], True, True,
               [ku[par].k, vext[par].k], [pu.sub(h % 2, 2)])
        for pr, pu in ((0, pu0), (1, pu1)):
            for hh in range(2):
                r0 = hh * 64
                dve(lambda e, pr=pr, pu=pu, hh=hh, r0=r0: e.scalar_tensor_tensor(
                    out=A(S)[r0:r0 + 64, pr, 0:129], in0=A(S)[r0:r0 + 64, pr, 0:129], scalar=A(gdk[par])[r0:r0 + 64, pr:pr + 1],
                    in1=A(pu)[r0:r0 + 64, hh, :], op0=ALU.mult, op1=ALU.add), [S.k, gdk[par].k, pu.k], [S.k])

    def state_update2(par):
        pu0 = bank(4, (2, 129))
        pu1 = bank(5, (2, 129))
        for h in range(4):
            pr = h // 2
            pu = pu0 if pr == 0 else pu1
            mm(A(pu)[:, h % 2, :], A(ku[par])[:, pr * 128:(pr + 1) * 128], A(vext[par])[:, h, 0:129], True, True,
               [ku[par].k, vext[par].k], [pu.sub(h % 2, 2)])
        for pr, pu in ((0, pu0), (1, pu1)):
            dve(lambda e, pr=pr, pu=pu: e.tensor_tensor(out=A(S)[0:64, pr, 0:129], in0=A(Sp)[0:64, pr, 0:129], in1=A(pu)[0:64, 0, :], op=ALU.add),
                [Sp.k, pu.k], [S.k])
            dve(lambda e, pr=pr, pu=pu: e.tensor_tensor(out=A(S)[64:128, pr, 0:129], in0=A(Sp)[64:128, pr, 0:129], in1=A(pu)[64:128, 1, :], op=ALU.add),
                [Sp.k, pu.k], [S.k])

    def prefix_B1(i, par):
        tile_ctr[0] = par
        pv = bank(1)
        proj_tok(i * 128, C_VM, 512, pv)
        P.act(lambda e: e.activation(out=A(vext[par])[:, :, 0:128], in_=A(pv)[:, :].rearrange("p (h v) -> p h v", h=4), func=AF.Copy),
              r=[pv.k], w=[vext[par].k])
        pk = bank(3, (256,), F32, 512)
        proj_tok(i * 128, C_KM, 256, pk)
        gate_tok(i * 128, 0, with_gdk=(i == 3))
        state_update(par, A(pk), pk.rng(0, 1024), decay=False)

    def prefix_B2(t):
        par, i = t % 2, t % 4
        for h in range(4):
            pr = h // 2
            pu = bank(4 + h, (129,))
            mm(A(pu)[:, 0:129], A(ku[par])[:, pr * 128:(pr + 1) * 128], A(vext[par])[:, h, 0:129], i == 0, i == 3,
               [ku[par].k, vext[par].k], [pu.k])
        if i == 3:
            for h in range(4):
                pr, r0 = h // 2, (h % 2) * 64
                pu = bank(4 + h, (129,))
                dve(lambda e, pr=pr, r0=r0, pu=pu: e.scalar_tensor_tensor(
                    out=A(S)[r0:r0 + 64, pr, 0:129], in0=A(S)[r0:r0 + 64, pr, 0:129], scalar=A(gdk[par])[r0:r0 + 64, pr:pr + 1],
                    in1=A(pu)[r0:r0 + 64, 0:129], op0=ALU.mult, op1=ALU.add), [S.k, gdk[par].k, pu.k], [S.k])

    halo_state = {}
    prefetched = {}

    def halo_tile():
        halo_T = SB.at(hT.lo + 16384, [8, 128], BF16)
        hs = load_tile(xhalo)
        xbh = SB.at(junk.lo, [D], BF16)
        xb_, sq_, c0_ = norm_a1(hs, xb=xbh, act_rstd=True)
        norm_scale(hs, xb_, sq_, c0_)
        norm_a2(xb_, halo_T, 0, None)
        save = cur["hn"]
        cur["hn"] = halo_T
        halo_state["sl"] = attn_kv(0, halo=True, pbank=2)
        cur["hn"] = save

    def prefix_pass():
        ngr = NPRE // 4
        assert NPRE % 4 == 0
        if ngr == 0:
            return
        rows = lambda u: xpre[u * 128:(u + 1) * 128, :]
        pend = {}

        loaded = {}
        LOAD_AHEAD = 4

        def ld(u):
            if u < NPRE and u not in loaded:
                loaded[u] = load_tile(rows(u))

        def a1(u):
            ld(u)
            s_ = loaded.pop(u)
            xb, sq, c0 = norm_a1(s_, act_rstd=True, col=u % 4)
            pend[u] = (s_, xb, sq, c0)

        def sc(u):
            s_, xb, sq, c0 = pend[u]
            norm_scale(s_, xb, sq, c0)

        nslot_active[0] = NSLOT - 2
        hnT3 = SB.at(xslot[NSLOT - 2].lo, [8, 512], BF16)
        gX1c = SB.at(hT.lo + 18432, [512], F32)
        gX2c = SB.at(hT.lo + 20480, [512], F32)
        bs3 = [bufsets[0], bufsets[1], dict(X1=gX1c, X2=gX2c, ggo=4, hn=hnT3)]
        pump_scratch(1000)
        for u_ in range(LOAD_AHEAD):
            ld(u_)
        a1(0)
        if NPRE > 1:
            a1(1)
        sc(0)
        LAG = 11
        gexp = {}
        for u in range(NPRE + LAG):
            g, i = u // 4, u % 4
            ld(u + LOAD_AHEAD)
            if u + 2 < NPRE:
                a1(u + 2)
            if u + 1 < NPRE:
                sc(u + 1)
            if u < NPRE:
                s_, xb, sq, c0 = pend.pop(u)
                norm_a2(xb, bs3[g % 3]["hn"], i, None)
                if i == 3:
                    cur.update(bs3[g % 3])
                    gates_a(512, 0, g * 512, (1, 512))
            if u >= 4 and i == 0 and (g - 1) * 4 + 3 < NPRE:
                cur.update(bs3[(g - 1) % 3])
                gexp["gen"] = gates_b_gen(512, 0, (g - 1) * 512, (1, 512), defer=True)
                next(gexp["gen"])
            if u >= 4 and i == 1 and "gen" in gexp:
                cur.update(bs3[(g - 1) % 3])
                gexp["d"] = next(gexp.pop("gen"))
            if u >= 4 and i == 2 and gexp.get("d"):
                for f_ in gexp.pop("d"):
                    f_()
            if u == max(9, NPRE - 5):
                halo_tile()
            if u == NPRE and NSUP > 0:
                nslot_active[0] = NSLOT
                slot_ctr[0] = 0
                nslots = [load_tile(xp[i_ * 128:(i_ + 1) * 128, :]) for i_ in range(4)]
                prefetched["p"] = (nslots, norm_group_pre(nslots, act_rstd=True))
            if u >= LAG:
                ub = u - LAG
                cur.update(bs3[(ub // 4) % 3])
                prefix_B1(ub % 4, ub % 2)
                if ub >= 1:
                    prefix_B2(ub - 1)
        nslot_active[0] = NSLOT
        prefix_B2(NPRE - 1)
        cur.update(bufsets[0])

    def attn_kv_a(colsT, last=False, halo=False, pbank=5):
        if halo:
            sl = 3
        else:
            sl = att_ctr[0] % 3
            att_ctr[0] += 1
        pkv = bank(pbank)
        proj_tok(colsT, C_KA, 256, pkv)
        sq = stat[sl % 2]
        P.act(lambda e: e.activation(out=A(tmpA)[:, 0:128], in_=A(pkv)[:, 0:128], func=AF.Square), r=[pkv.k], w=[tmpA.k])
        dve(lambda e: e.tensor_reduce(out=A(rkr[sl])[:, 0:2], in_=A(tmpA)[:, 0:128].rearrange("p (g d) -> p g d", g=2), axis=AX.X, op=ALU.add),
            [tmpA.k], [rkr[sl].k])
        dve(lambda e: e.tensor_scalar(out=A(rkr[sl])[:, 0:2], in0=A(rkr[sl])[:, 0:2], scalar1=1.0 / 64, scalar2=EPS, op0=ALU.mult, op1=ALU.add),
            [rkr[sl].k], [rkr[sl].k])
        pow_mhalf(A(rkr[sl])[:, 0:2], rkr[sl].k, 2)
        for dup in range(2):
            P.act(lambda e, dup=dup: e.activation(out=A(kdup[sl % 2])[:, :, dup, :], in_=A(pkv)[:, 0:128].rearrange("p (g d) -> p g d", g=2), func=AF.Copy),
                  r=[pkv.k], w=[kdup[sl % 2].k])
        if halo:
            dve(lambda e: e.tensor_scalar(out=A(vaext[sl])[:, :, 0:64], in0=A(pkv)[:, 128:256].rearrange("p (g d) -> p g d", g=2),
                                          scalar1=A(hvalid)[:, 0:1], scalar2=None, op0=ALU.mult), [pkv.k, hvalid.k], [vaext[sl].k])
            dve(lambda e: e.tensor_scalar(out=A(vaext[sl])[:, :, 64:66], in0=A(vaext[sl])[:, :, 64:66], scalar1=A(hvalid)[:, 0:1], scalar2=None, op0=ALU.mult),
                [vaext[sl].k, hvalid.k], [vaext[sl].k])
        else:
            P.act(lambda e: e.activation(out=A(vaext[sl])[:, :, 0:64], in_=A(pkv)[:, 128:256].rearrange("p (g d) -> p g d", g=2), func=AF.Copy),
                  r=[pkv.k], w=[vaext[sl].k])
        if last:
            dve(lambda e: e.tensor_tensor(out=A(kwout)[:, :].rearrange("p (g d) -> p g d", g=2), in0=A(pkv)[:, 0:128].rearrange("p (g d) -> p g d", g=2),
                                          in1=A(rkr[sl])[:, 0:2].unsqueeze(2).to_broadcast([128, 2, 64]), op=ALU.mult), [pkv.k, rkr[sl].k], [kwout.k])
            dve(lambda e: e.tensor_tensor(out=A(kwout)[:, :], in0=A(kwout)[:, :], in1=A(kwrow)[:, :], op=ALU.mult), [kwout.k, kwrow.k], [kwout.k])
            dve(lambda e: e.tensor_copy(out=A(vwout)[:, :], in_=A(pkv)[:, 128:256]), [pkv.k], [vwout.k])
        return sl

    def attn_kv_b(sl):
        tb = bank(0, (8, 128), BF16)
        for g in range(2):
            tr(A(tb)[:, g, :], A(kdup[sl % 2])[:, g, :, :].rearrange("p a d -> p (a d)"), [kdup[sl % 2].k], [tb.sub(g, 8)])
        dve(lambda e: e.tensor_scalar(out=A(kaT[sl])[:, :, :], in0=A(tb)[:, 0:2, :], scalar1=A(kw)[:, 0:1], scalar2=None, op0=ALU.mult),
            [tb.rng(0, 512), kw.k], [kaT[sl].k])

    def attn_kv(colsT, last=False, halo=False, pbank=5):
        sl = attn_kv_a(colsT, last=last, halo=halo, pbank=pbank)
        attn_kv_b(sl)
        return sl

    def tail(slots, dst, t0, dkey, ckoff=0, before_pass2=None, s5_slots=None, gu_slots=None, chunk_hook=None):
        nt = len(slots)
        NT = nt * 128
        s5_slots = s5_slots or s5ring
        gu_slots = gu_slots or guring
        dve(lambda e: e.tensor_tensor(out=A(hnT)[:, :, 0:NT], in0=A(hnT)[:, :, 0:NT], in1=A(nmwb)[:, 0:8].unsqueeze(2).to_broadcast([128, 8, NT]), op=ALU.mult),
            [hk_all(hnT), nmwb.k], [hk_all(hnT)])
        for c in range(8):
            rs = s5_slots[c % len(s5_slots)]
            P.dma("sp", "s5r%d" % (c % len(s5_slots)), lambda e, c=c, rs=rs: e.dma_start(out=A(rs)[:, :], in_=s5_scr[c, :, :]), r=[("s5scr", c, c + 1)], w=[rs.k])
            wga = A(rs)[:, 0:1024].rearrange("p (k n) -> p k n", k=8)
            wgb = A(rs)[:, 1024:2048].rearrange("p (k n) -> p k n", k=8)
            wac = A(rs)[:, 2048:2560].rearrange("p (k n) -> p k n", k=4)
            wbc = A(rs)[:, 2560:3072].rearrange("p (k n) -> p k n", k=4)
            pga, pgb, pa, pb_ = bank(4), bank(5), bank(6), bank(7)
            for k in range(8):
                mm(A(pga)[:, 0:NT], wga[:, k, :], A(hnT)[:, k, 0:NT], k == 0, k == 7, [rs.k, hk_all(hnT)], [pga.k])
            for k in range(8):
                mm(A(pgb)[:, 0:NT], wgb[:, k, :], A(hnT)[:, k, 0:NT], k == 0, k == 7, [rs.k, hk_all(hnT)], [pgb.k])
            for k in range(4):
                mm(A(pa)[:, 0:NT], wac[:, k, :], A(hmT)[:, k, 0:NT], k == 0, k == 3, [rs.k, hmT.k], [pa.k])
            for k in range(4):
                mm(A(pb_)[:, 0:NT], wbc[:, k, :], A(haT)[:, k, 0:NT], k == 0, k == 3, [rs.k, haT.k], [pb_.k])
            P.act(lambda e, pga=pga: e.activation(out=A(tmpA)[:, 0:NT], in_=A(pga)[:, 0:NT], func=AF.Tanh, scale=0.5), r=[pga.k], w=[tmpA.k])
            P.act(lambda e, pgb=pgb: e.activation(out=A(tmpB)[:, 0:NT], in_=A(pgb)[:, 0:NT], func=AF.Tanh, scale=0.5), r=[pgb.k], w=[tmpB.k])
            dve(lambda e, pa=pa: e.scalar_tensor_tensor(out=A(tmpA)[:, 0:NT], in0=A(tmpA)[:, 0:NT], scalar=1.0, in1=A(pa)[:, 0:NT], op0=ALU.add, op1=ALU.mult),
                [tmpA.k, pa.k], [tmpA.k])
            dve(lambda e, pb_=pb_: e.scalar_tensor_tensor(out=A(tmpB)[:, 0:NT], in0=A(tmpB)[:, 0:NT], scalar=1.0, in1=A(pb_)[:, 0:NT], op0=ALU.add, op1=ALU.mult),
                [tmpB.k, pb_.k], [tmpB.k])
            dve(lambda e, c=c: e.tensor_tensor(out=A(mixT)[:, c, 0:NT], in0=A(tmpA)[:, 0:NT], in1=A(tmpB)[:, 0:NT], op=ALU.add), [tmpA.k, tmpB.k], [mixT.sub(c, 8)])
        ck(10 + ckoff)
        for i, s in enumerate(slots):
            for hh in range(2):
                px = bank(1 + hh)
                for k in range(8):
                    mm(A(px)[:, :], A(mixT)[:, k, i * 128:(i + 1) * 128], A(wout)[:, k, hh * 512:(hh + 1) * 512], k == 0, k == 7, [mixT.k, wout.k], [px.k])
                dve(lambda e, s=s, hh=hh, px=px: e.scalar_tensor_tensor(out=A(xslot[s])[:, hh * 512:(hh + 1) * 512], in0=A(px)[:, :], scalar=0.5,
                                                                      in1=A(xslot[s])[:, hh * 512:(hh + 1) * 512], op0=ALU.mult, op1=ALU.add),
                    [px.k, xslot[s].k], [xslot[s].k])
        norm_group(slots, hnT, nfw)
        ck(11 + ckoff)
        for c in range(NFF):
            rs = gu_slots[c % len(gu_slots)]
            P.dma("sp", "gur%d" % (c % len(gu_slots)), lambda e, c=c, rs=rs: e.dma_start(out=A(rs)[:, :], in_=gu_scr[c, :, :]), r=[("guscr", c, c + 1)], w=[rs.k])
            wg = A(rs)[:, 0:1024].rearrange("p (k n) -> p k n", k=8)
            wu = A(rs)[:, 1024:2048].rearrange("p (k n) -> p k n", k=8)
            pg_, pu_ = bank((c % 2) * 2), bank((c % 2) * 2 + 1)
            for k in range(8):
                mm(A(pg_)[:, 0:NT], wg[:, k, :], A(hnT)[:, k, 0:NT], k == 0, k == 7, [rs.k, hk_all(hnT)], [pg_.k])
            for k in range(8):
                mm(A(pu_)[:, 0:NT], wu[:, k, :], A(hnT)[:, k, 0:NT], k == 0, k == 7, [rs.k, hk_all(hnT)], [pu_.k])
            ta = tmpC if c % 2 == 0 else tmpD
            P.act(lambda e, pg_=pg_, ta=ta: e.activation(out=A(ta)[:, 0:NT], in_=A(pg_)[:, 0:NT], func=AF.Tanh, scale=0.5), r=[pg_.k], w=[ta.k])
            dve(lambda e, pg_=pg_, ta=ta: e.scalar_tensor_tensor(out=A(ta)[:, 0:NT], in0=A(ta)[:, 0:NT], scalar=1.0, in1=A(pg_)[:, 0:NT], op0=ALU.add, op1=ALU.mult),
                [ta.k, pg_.k], [ta.k])
            dve(lambda e, pu_=pu_, ta=ta, c=c: e.scalar_tensor_tensor(out=A(hT)[:, c, 0:NT], in0=A(ta)[:, 0:NT], scalar=0.5, in1=A(pu_)[:, 0:NT], op0=ALU.mult, op1=ALU.mult),
                [ta.k, pu_.k], [hT.sub(c, NFF)])
            if chunk_hook is not None:
                chunk_hook(c)
        ck(12 + ckoff)
        if before_pass2 is not None:
            before_pass2()
        for c in range(NFF):
            rs = wdring[c % NWD]
            P.dma("sp", "wdr%d" % (c % NWD), lambda e, c=c, rs=rs: e.dma_start(out=A(rs)[:, :], in_=wd_scr[c * 128:(c + 1) * 128, :]), r=[("wdscr", 0, 4)], w=[rs.k])
            for i in range(nt):
                for hh in range(2):
                    py = bank(i * 2 + hh)
                    mm(A(py)[:, :], A(hT)[:, c, i * 128:(i + 1) * 128], A(rs)[:, hh * 512:(hh + 1) * 512], c == 0, c == NFF - 1, [hT.sub(c, NFF), rs.k], [py.k])
        for i, s in enumerate(slots):
            for hh in range(2):
                py = bank(i * 2 + hh)
                dve(lambda e, s=s, hh=hh, py=py: e.tensor_tensor(out=A(xslot[s])[:, hh * 512:(hh + 1) * 512], in0=A(py)[:, :], in1=A(xslot[s])[:, hh * 512:(hh + 1) * 512], op=ALU.add),
                    [py.k, xslot[s].k], [xslot[s].k])
            outs.append(P.dma("sp", "x%d" % s, lambda e, s=s, i=i: e.dma_start(out=dst[(t0 + i) * 128:(t0 + i + 1) * 128, :], in_=A(xslot[s])[:, :]),
                              r=[xslot[s].k], w=[(dkey, t0 + i, t0 + i + 1)]))

    def seg_A(i, cT, is_last_tile, sm=None):
        par = i % 2
        pv = bank(1)
        proj_tok(cT, C_VM, 512, pv)
        P.act(lambda e: e.activation(out=A(vext[par])[:, :, 0:128], in_=A(pv)[:, :].rearrange("p (h v) -> p h v", h=4), func=AF.Copy),
              r=[pv.k], w=[vext[par].k])
        po = bank(2)
        proj_tok(cT, C_OM, 512, po)
        P.act(lambda e: e.activation(out=A(tho[par])[:, :], in_=A(po)[:, :], func=AF.Tanh, scale=0.5), r=[po.k], w=[tho[par].k])
        dve(lambda e: e.scalar_tensor_tensor(out=A(tho[par])[:, :], in0=A(tho[par])[:, :], scalar=1.0, in1=A(mnwh)[:, :], op0=ALU.add, op1=ALU.mult),
            [tho[par].k, mnwh.k], [tho[par].k])
        ck(7 if sm is None else 27)
        pq = bank(1)
        proj_tok(cT, C_QA, 512, pq)
        sq = stat[par]
        P.act(lambda e: e.activation(out=A(tmpA)[:, :], in_=A(pq)[:, :], func=AF.Square), r=[pq.k], w=[tmpA.k])
        dve(lambda e: e.tensor_reduce(out=A(sq)[:, 8:16], in_=A(tmpA)[:, :].rearrange("p (h d) -> p h d", h=8), axis=AX.X, op=ALU.add),
            [tmpA.k], [sq.rng(32, 64)])
        dve(lambda e: e.tensor_scalar(out=A(sq)[:, 8:16], in0=A(sq)[:, 8:16], scalar1=1.0 / 64, scalar2=EPS, op0=ALU.mult, op1=ALU.add),
            [sq.rng(32, 64)], [sq.rng(32, 64)])
        pow_mhalf(A(sq)[:, 8:16], sq.rng(32, 64), 8)
        dve(lambda e: e.tensor_tensor(out=A(qn[par])[:, :].rearrange("p (h d) -> p h d", h=8), in0=A(pq)[:, :].rearrange("p (h d) -> p h d", h=8),
                                      in1=A(sq)[:, 8:16].unsqueeze(2).to_broadcast([128, 8, 64]), op=ALU.mult),
            [pq.k, sq.rng(32, 64)], [qn[par].k])
        cur_sl = attn_kv_a(cT, last=is_last_tile, pbank=2)
        return dict(i=i, cT=cT, par=par, sq=sq, cur_sl=cur_sl, sm=sm)

    def seg_B(tc):
        par, cur_sl = tc["par"], tc["cur_sl"]
        tb = bank(0, (8, 128), BF16)
        for c in range(4):
            tr(A(tb)[:, 4 + c, :], A(qn[par])[:, c * 128:(c + 1) * 128], [qn[par].k], [tb.sub(4 + c, 8)])
        dve(lambda e: e.tensor_scalar(out=A(qaT[par])[:, :, :], in0=A(tb)[:, 4:8, :], scalar1=A(qw)[:, 0:1], scalar2=None, op0=ALU.mult),
            [tb.rng(1024, 2048), qw.k], [qaT[par].k])
        attn_kv_b(cur_sl)

    def seg_C(tc, prev_sl):
        i, cT, par, sq, cur_sl, sm = tc["i"], tc["cT"], tc["par"], tc["sq"], tc["cur_sl"], tc["sm"]
        for blk in range(2):
            pbk = (bank(4), bank(5))
            sl = prev_sl if blk == 0 else cur_sl
            for hq in range(8):
                p0 = (hq % 2) * 64
                idx = hq // 2
                if sm is not None and blk == 0:
                    for b in range(16):
                        mm(A(pbk[hq % 2])[:, idx * 128 + 8 * b:idx * 128 + 8 * b + 8], A(sm["kcT"])[p0:p0 + 64, b, hq // 4, :],
                           A(qaT[par])[p0:p0 + 64, hq // 2, 8 * b:8 * b + 8], True, True, [sm["kcT"].k, qaT[par].k], [pbk[hq % 2].sub(idx, 4)])
                else:
                    mm(A(pbk[hq % 2])[:, idx * 128:(idx + 1) * 128], A(kaT[sl])[p0:p0 + 64, hq // 4, :], A(qaT[par])[p0:p0 + 64, hq // 2, :], True, True,
                       [kaT[sl].k, qaT[par].k], [pbk[hq % 2].sub(idx, 4)])
            PTv = A(PT)[:, blk, :, :].rearrange("p (g i two) t -> p g i two t", g=2, i=2, two=2)
            for par2 in range(2):
                for g in range(2):
                    if sm is not None and blk == 0:
                        P.act(lambda e, par2=par2, g=g, PTv=PTv, pbk=pbk: e.activation(
                            out=PTv[:, g, :, par2, :], in_=A(pbk[par2])[:, g * 256:(g + 1) * 256].rearrange("p (i t) -> p i t", i=2), func=AF.Exp),
                            r=[pbk[par2].k], w=[PT.sub(blk, 2)])
                    else:
                        P.act(lambda e, par2=par2, g=g, sl=sl, PTv=PTv, pbk=pbk: e.activation(
                            out=PTv[:, g, :, par2, :], in_=A(pbk[par2])[:, g * 256:(g + 1) * 256].rearrange("p (i t) -> p i t", i=2),
                            func=AF.Exp, scale=A(rkr[sl])[:, g:g + 1]),
                            r=[pbk[par2].k, rkr[sl].k], w=[PT.sub(blk, 2)])
            if sm is None:
                mk = maskL if blk == 0 else maskU
            else:
                mk = sm["maskC"] if blk == 0 else sm["maskS"]
            dve(lambda e, blk=blk, mk=mk: e.tensor_tensor(out=A(PT)[:, blk, :, :], in0=A(PT)[:, blk, :, :],
                                                          in1=A(mk)[:, :].unsqueeze(1).to_broadcast([128, 8, 128]), op=ALU.mult),
                [PT.sub(blk, 2), mk.k], [PT.sub(blk, 2)])

    def seg_D(tc, prev_sl):
        i, cT, par, sq, cur_sl, sm = tc["i"], tc["cT"], tc["par"], tc["sq"], tc["cur_sl"], tc["sm"]
        pzc = 0
        for hb in range(2):
            po_ = bank(6 + hb, (4, 66))
            for hh in range(4):
                hq = hb * 4 + hh
                if sm is None:
                    mm(A(po_)[:, hh, 0:65], A(PT)[:, 0, hq, :], A(vaext[prev_sl])[:, hb, 0:65], True, False, [PT.sub(0, 2), vaext[prev_sl].k], [po_.sub(hh, 4)])
                else:
                    PZ = sm["PZ"][pzc % 2]
                    pzc += 1
                    dve(lambda e, hq=hq, PZ=PZ: e.tensor_tensor(
                        out=A(PZ)[:, :, :].rearrange("p b (c j) -> p b c j", j=8),
                        in0=A(PT)[:, 0, hq, :].rearrange("p (c j) -> p c j", j=8).unsqueeze(1).to_broadcast([128, 16, 16, 8]),
                        in1=A(sm["eye16"])[:, :, :].unsqueeze(3).to_broadcast([128, 16, 16, 8]), op=ALU.mult),
                        [PT.sub(0, 2), sm["eye16"].k], [PZ.k])
                    for b in range(16):
                        mm(A(po_)[:, hh, 0:65], A(PZ)[:, b, :], A(sm["vcext"])[:, b, hb, 0:65], b == 0, False, [PZ.k, sm["vcext"].k], [po_.sub(hh, 4)])
                mm(A(po_)[:, hh, 0:65], A(PT)[:, 1, hq, :], A(vaext[cur_sl])[:, hb, 0:65], False, True, [PT.sub(1, 2), vaext[cur_sl].k], [po_.sub(hh, 4)])
            dve(lambda e, hb=hb, po_=po_: e.tensor_tensor(out=A(sq)[:, 16 + hb * 4:20 + hb * 4], in0=A(po_)[:, :, 64], in1=A(sinke)[:, hb * 4:(hb + 1) * 4], op=ALU.add),
                [po_.k, sinke.k], [sq.rng(64 + hb * 16, 80 + hb * 16)])
            dve(lambda e, hb=hb: e.reciprocal(out=A(sq)[:, 16 + hb * 4:20 + hb * 4], in_=A(sq)[:, 16 + hb * 4:20 + hb * 4]),
                [sq.rng(64 + hb * 16, 80 + hb * 16)], [sq.rng(64 + hb * 16, 80 + hb * 16)])
            dve(lambda e, hb=hb, po_=po_: e.tensor_tensor(
                out=A(ha[par])[:, hb * 256:(hb + 1) * 256].rearrange("p (h d) -> p h d", h=4), in0=A(po_)[:, :, 0:64],
                in1=A(sq)[:, 16 + hb * 4:20 + hb * 4].unsqueeze(2).to_broadcast([128, 4, 64]), op=ALU.mult),
                [po_.k, sq.rng(64 + hb * 16, 80 + hb * 16)], [ha[par].k])
        ck(8 if sm is None else 28)

    def sample_pre_e(sm, cT=0):
        SS, SSb, QZ, Z = sm["SS"], sm["SSb"], sm["QZ"], sm["Z"]
        dve(lambda e: e.tensor_tensor(out=A(sm["Rg"])[0:4, :].rearrange("p (b r) -> p b r", r=2), in0=A(gg)[0:4, 0:16].unsqueeze(2).to_broadcast([4, 16, 2]),
                                      in1=A(psel)[0:4, 0:2].unsqueeze(1).to_broadcast([4, 16, 2]), op=ALU.mult), [gg.k, psel.k], [sm["Rg"].k])
        pgs = bank(3)
        mm(A(pgs)[:, 32:64], A(halfm)[0:4, :], A(sm["Rg"])[0:4, 0:32], True, True, [halfm.k, sm["Rg"].k], [pgs.rng(128, 256)])
        dve(lambda e: e.tensor_copy(out=A(sm["gdkS"])[:, 0:32], in_=A(pgs)[:, 32:64]), [pgs.rng(128, 256)], [sm["gdkS"].k])
        dve(lambda e: e.tensor_tensor(out=A(SS)[:, :, :, 0:129], in0=A(SS)[:, :, :, 0:129],
                                      in1=A(sm["gdkS"])[:, 0:32].rearrange("p (b r) -> p b r", r=2).unsqueeze(3).to_broadcast([128, 16, 2, 129]), op=ALU.mult),
            [SS.k, sm["gdkS"].k], [SS.k])
        dve(lambda e: e.tensor_copy(out=A(SSb)[:, :, :, 0:129], in_=A(SS)[:, :, :, 0:129]), [SS.k], [SSb.k])
        for pr in range(2):
            dve(lambda e, pr=pr: e.tensor_tensor(
                out=A(QZ)[:, pr, :, :].rearrange("p b (c j) -> p b c j", j=8),
                in0=A(qT)[:, pr, cT:cT + 128].rearrange("p (c j) -> p c j", j=8).unsqueeze(1).to_broadcast([128, 16, 16, 8]),
                in1=A(sm["eye16"])[:, :, :].unsqueeze(3).to_broadcast([128, 16, 16, 8]), op=ALU.mult),
                [qT.k, sm["eye16"].k], [QZ.sub(pr, 2)])


    def seg_E(tc):
        i, cT, par, sq, cur_sl, sm = tc["i"], tc["cT"], tc["par"], tc["sq"], tc["cur_sl"], tc["sm"]
        tile_ctr[0] = par
        pk = bank(3, (256,), F32, 512)
        proj_tok(cT, C_KM, 256, pk)
        gate_tok(cT, i)
        state_update(par, A(pk), pk.rng(0, 1024), decay=(sm is None))
        if sm is None:
            dve(lambda e: e.tensor_copy(out=A(Spb)[:, :, 0:129], in_=A(Sp)[:, :, 0:129]), [Sp.k], [Spb.k])
        else:
            Z = sm["Z"]
            dve(lambda e: e.tensor_tensor(out=A(Z)[:, :, :], in0=A(ku[par])[:, :].unsqueeze(1).to_broadcast([128, 16, 256]),
                                          in1=A(sm["seqind"])[:, :].unsqueeze(2).to_broadcast([128, 16, 256]), op=ALU.mult),
                [ku[par].k, sm["seqind"].k], [Z.k])
        for h in range(4):
            p0 = (h % 2) * 64
            pst = bank(4 + h % 2)
            mm(A(pst)[:, (h // 2) * 128:(h // 2 + 1) * 128], A(kT)[p0:p0 + 64, h // 2, cT:cT + 128], A(qT)[p0:p0 + 64, h // 2, cT:cT + 128], True, True,
               [kT.k, qT.k], [pst.sub(h // 2, 4)])
        mkm = maskU if sm is None else sm["maskS"]
        for h in range(4):
            pst = bank(4 + h % 2)
            dve(lambda e, h=h, pst=pst: e.scalar_tensor_tensor(out=A(PTm)[:, h, :], in0=A(pst)[:, (h // 2) * 128:(h // 2 + 1) * 128], scalar=A(utok[par])[:, h:h + 1],
                                                              in1=A(mkm)[:, :], op0=ALU.mult, op1=ALU.mult),
                [pst.sub(h // 2, 4), utok[par].k, mkm.k], [PTm.sub(h, 4)])

    def seg_F(tc):
        i, cT, par, sq, cur_sl, sm = tc["i"], tc["cT"], tc["par"], tc["sq"], tc["cur_sl"], tc["sm"]
        tile_ctr[0] = par
        if sm is not None:
            SS, SSb, QZ, Z = sm["SS"], sm["SSb"], sm["QZ"], sm["Z"]
        pm0 = bank(6, (2, 129))
        pm1 = bank(7, (2, 129))
        for h in range(4):
            pm = pm0 if h < 2 else pm1
            p0 = (h % 2) * 64
            mm(A(pm)[:, h % 2, :], A(PTm)[:, h, :], A(vext[par])[:, h, 0:129], True, False, [PTm.sub(h, 4), vext[par].k], [pm.sub(h % 2, 2)])
            if sm is None:
                mm(A(pm)[:, h % 2, :], A(qT)[p0:p0 + 64, h // 2, cT:cT + 128], A(Spb)[p0:p0 + 64, h // 2, 0:129], False, True, [qT.k, Spb.k], [pm.sub(h % 2, 2)])
            else:
                for b in range(16):
                    mm(A(pm)[:, h % 2, :], A(QZ)[p0:p0 + 64, h // 2, b, :], A(SSb)[p0:p0 + 64, b, h // 2, 0:129], False, b == 15, [QZ.k, SSb.k], [pm.sub(h % 2, 2)])
        for pr, pm in ((0, pm0), (1, pm1)):
            c8 = 24 + pr * 2
            P.act(lambda e, pm=pm, c8=c8: e.activation(out=A(sq)[:, c8:c8 + 2], in_=A(pm)[:, :, 128], func=AF.Abs),
                  r=[pm.k], w=[sq.rng(96 + pr * 8, 104 + pr * 8)])
            for hh in range(2):
                h = pr * 2 + hh
                P.act(lambda e, pm=pm, hh=hh, h=h: e.activation(out=A(junk)[:, 0:128], in_=A(pm)[:, hh, 0:128], func=AF.Square, accum_out=A(sq)[:, 28 + h:29 + h]),
                      r=[pm.k], w=[junk.k, sq.rng(112 + h * 4, 116 + h * 4)])
        dve(lambda e: e.tensor_tensor(out=A(sq)[:, 24:28], in0=A(sq)[:, 24:28], in1=A(utok[par])[:, 4:8], op=ALU.max),
            [sq.rng(96, 112), utok[par].k], [sq.rng(96, 112)])
        dve(lambda e: e.reciprocal(out=A(sq)[:, 24:28], in_=A(sq)[:, 24:28]), [sq.rng(96, 112)], [sq.rng(96, 112)])
        dve(lambda e: e.tensor_tensor(out=A(sq)[:, 28:32], in0=A(sq)[:, 28:32], in1=A(sq)[:, 24:28], op=ALU.mult), [sq.rng(96, 128)], [sq.rng(112, 128)])
        dve(lambda e: e.tensor_tensor(out=A(sq)[:, 28:32], in0=A(sq)[:, 28:32], in1=A(sq)[:, 24:28], op=ALU.mult), [sq.rng(96, 128)], [sq.rng(112, 128)])
        dve(lambda e: e.tensor_scalar(out=A(sq)[:, 28:32], in0=A(sq)[:, 28:32], scalar1=1.0 / 128, scalar2=EPS, op0=ALU.mult, op1=ALU.add),
            [sq.rng(112, 128)], [sq.rng(112, 128)])
        pow_mhalf(A(sq)[:, 28:32], sq.rng(112, 128), 4)
        dve(lambda e: e.tensor_tensor(out=A(sq)[:, 28:32], in0=A(sq)[:, 28:32], in1=A(sq)[:, 24:28], op=ALU.mult), [sq.rng(96, 128)], [sq.rng(112, 128)])
        for h in range(4):
            pm = pm0 if h < 2 else pm1
            dve(lambda e, h=h, pm=pm: e.scalar_tensor_tensor(out=A(hm[par])[:, h * 128:(h + 1) * 128], in0=A(pm)[:, h % 2, 0:128], scalar=A(sq)[:, 28 + h:29 + h],
                                                            in1=A(tho[par])[:, h * 128:(h + 1) * 128], op0=ALU.mult, op1=ALU.mult),
                [pm.k, sq.rng(112, 128), tho[par].k], [hm[par].k])
        if sm is None:
            state_update2(par)
        else:
            cnt = 0
            for pr in range(2):
                for b in range(16):
                    pu = bank(4 + cnt % 4, (2, 129))
                    cnt += 1
                    for hh in range(2):
                        mm(A(pu)[:, hh, :], A(Z)[:, b, pr * 128:(pr + 1) * 128], A(vext[par])[:, 2 * pr + hh, 0:129], True, True, [Z.k, vext[par].k], [pu.sub(hh, 2)])
                    dve(lambda e, pr=pr, b=b, pu=pu: e.tensor_tensor(out=A(SS)[0:64, b, pr, 0:129], in0=A(SS)[0:64, b, pr, 0:129], in1=A(pu)[0:64, 0, :], op=ALU.add),
                        [SS.k, pu.k], [SS.k])
                    dve(lambda e, pr=pr, b=b, pu=pu: e.tensor_tensor(out=A(SS)[64:128, b, pr, 0:129], in0=A(SS)[64:128, b, pr, 0:129], in1=A(pu)[64:128, 1, :], op=ALU.add),
                        [SS.k, pu.k], [SS.k])

    def seg_G(tc):
        i, cT, par, sq, cur_sl, sm = tc["i"], tc["cT"], tc["par"], tc["sq"], tc["cur_sl"], tc["sm"]
        tb2 = bank(0, (8, 128), BF16)
        for c in range(4):
            tr(A(tb2)[:, c, :], A(hm[par])[:, c * 128:(c + 1) * 128], [hm[par].k], [tb2.sub(c, 8)])
        for c in range(4):
            tr(A(tb2)[:, 4 + c, :], A(ha[par])[:, c * 128:(c + 1) * 128], [ha[par].k], [tb2.sub(4 + c, 8)])
        P.act(lambda e: e.activation(out=A(hmT)[:, :, cT:cT + 128], in_=A(tb2)[:, 0:4, :], func=AF.Copy), r=[tb2.rng(0, 1024)], w=[hmT.k])
        P.act(lambda e: e.activation(out=A(haT)[:, :, cT:cT + 128], in_=A(tb2)[:, 4:8, :], func=AF.Copy), r=[tb2.rng(1024, 2048)], w=[haT.k])
        ck(9 if sm is None else 29)

    def mixer_tile(i, cT, prev_sl, is_last_tile, sm=None):
        tc = seg_A(i, cT, is_last_tile, sm)
        seg_B(tc)
        seg_C(tc, prev_sl)
        seg_D(tc, prev_sl)
        seg_E(tc)
        seg_F(tc)
        seg_G(tc)
        return tc["cur_sl"]

    def qk_feature(NT):
        for blk, (dstb, c_lo) in enumerate(((qT, C_QM), (qT, C_QM + 128), (kT, C_KM), (kT, C_KM + 128))):
            pb = bank(1 + blk % 2)
            for k in range(8):
                mm(A(pb)[:, 0:NT], A(win)[:, k, c_lo:c_lo + 128], hnT.ap[:, k, 0:NT], k == 0, k == 7, [win.k, hk_all(hnT)], [pb.k])
            P.act(lambda e, pb=pb, dstb=dstb, blk=blk: e.activation(out=dstb.ap[:, blk % 2, 0:NT], in_=A(pb)[:, 0:NT], func=AF.Copy),
                  r=[pb.k], w=[dstb.sub(blk % 2, 2)])


    def super_tile(src, dst, t0, first, last_sup, prev_sl, chunk_hook=None):
        if "p" in prefetched:
            slots, pend = prefetched.pop("p")
        else:
            slots = [load_tile(src[(t0 + i) * 128:(t0 + i + 1) * 128, :]) for i in range(4)]
            pend = norm_group_pre(slots)
        norm_group_post(slots, pend, hnT, None)
        qk_feature(512)
        ck(5)
        gates_a(512, 0, None, (4, 128))
        ck(6)
        tcs = {}
        prevs = {0: prev_sl}

        def sA(i):
            tcs[i] = seg_A(i, i * 128, last_sup and i == 3)
            prevs[i + 1] = tcs[i]["cur_sl"]

        sA(0)
        seg_B(tcs[0])
        ggen = gates_b_gen(512, 0, None, (4, 128), defer=True)
        next(ggen)
        for i in range(4):
            if i + 1 < 4:
                sA(i + 1)
            if i == 0:
                for f_ in next(ggen):
                    f_()
            if i >= 1:
                seg_G(tcs[i - 1])
            seg_C(tcs[i], prevs[i])
            seg_E(tcs[i])
            if i + 1 < 4:
                seg_B(tcs[i + 1])
            seg_D(tcs[i], prevs[i])
            seg_F(tcs[i])
        seg_G(tcs[3])
        def pre_next():
            nslots = [load_tile(src[(t0 + 4 + i) * 128:(t0 + 5 + i) * 128, :]) for i in range(4)]
            prefetched["p"] = (nslots, norm_group_pre(nslots))

        tail(slots, dst, t0, "ydst", before_pass2=(None if last_sup else pre_next), chunk_hook=chunk_hook)
        return prevs[4]

    def sample_setup():
        s7 = NSLOT - 1
        base = xslot[0].lo
        SS = SB.at(base, [16, 2, 130], F32)
        SSb = SB.at(base + 16640, [16, 2, 130], BF16)
        off = base + 16640 + 8320
        def nxt(shape, dt, nbytes):
            nonlocal off
            b_ = SB.at(off, shape, dt)
            off += (nbytes + 31) // 32 * 32
            return b_
        snT = nxt([32], F32, 128)
        gdkS = nxt([32], F32, 128)
        seqind = nxt([16], BF16, 32)
        eye16 = nxt([16, 16], BF16, 512)
        maskS = nxt([128], BF16, 256)
        maskC = nxt([128], BF16, 256)
        Rg = nxt([32], F32, 128)
        msr = nxt([16], F32, 64)
        assert off <= xslot[s7].lo
        Z = SB.at(guring[0].lo, [16, 256], BF16)
        PZ = [SB.at(guring[2].lo, [16, 128], BF16), SB.at(wdring[0].lo, [16, 128], BF16)]
        QZ = SB.at(mixT.lo, [2, 16, 128], BF16)
        stg = [SB.at(s5ring[0].lo + 8192 + q * 1024, [256], F32) for q in range(2)]
        kcT = SB.at(s5ring[0].lo, [16, 2, 128], BF16)
        sm = dict(SS=SS, SSb=SSb, QZ=QZ, Z=Z, PZ=PZ, kcT=kcT, vcext=vcext_s, eye16=eye16, seqind=seqind, maskS=maskS, maskC=maskC, Rg=Rg, gdkS=gdkS, cached=set())
        sm.update(snT=snT, msr=msr, stg=stg, s7=s7)
        return sm

    def sample_cache(b, sm):
        stg, kcT = sm["stg"], sm["kcT"]
        sg = stg[b % 2]
        P.dma("sp", "stg%d" % (b % 2), lambda e: e.dma_start(out=A(sg)[:, 0:128], in_=ckd[b]), w=[sg.rng(0, 512)])
        P.dma("sp", "stv%d" % (b % 2), lambda e: e.dma_start(out=A(sg)[:, 128:256], in_=cvd[b]), w=[sg.rng(512, 1024)])
        kd = kdup[b % 2]
        P.act(lambda e: e.activation(out=A(kd)[:, :, :, :], in_=A(sg)[:, 0:128].rearrange("p (g d) -> p g d", g=2).unsqueeze(2).to_broadcast([128, 2, 2, 64]), func=AF.Copy),
              r=[sg.rng(0, 512)], w=[kd.k])
        tbc = bank(4 + b % 2, (8, 128), BF16)
        for g in range(2):
            tr(A(tbc)[:, g, :], A(kd)[:, g, :, :].rearrange("p a d -> p (a d)"), [kd.k], [tbc.sub(g, 8)])
        dve(lambda e: e.tensor_copy(out=A(kcT)[:, b, :, :], in_=A(tbc)[:, 0:2, :]), [tbc.rng(0, 512)], [kcT.sub(b, 16)])
        P.act(lambda e: e.activation(out=A(vcext_s)[:, b, :, 0:64], in_=A(sg)[:, 128:256].rearrange("p (g d) -> p g d", g=2), func=AF.Copy),
              r=[sg.rng(512, 1024)], w=[vcext_s.k])
        outs.append(P.dma("sp", "okws", lambda e: e.dma_start(out=kws[b, 0:120, :], in_=ckd[b, 8:128, :]), w=[("okws", b, b + 1)], total=True))
        outs.append(P.dma("sp", "ovws", lambda e: e.dma_start(out=vws[b, 0:120, :], in_=cvd[b, 8:128, :]), w=[("ovws", b, b + 1)], total=True))

    def sample_tile(sm):
        s7 = sm["s7"]
        P.dma("sp", "x%d" % s7, lambda e: e.dma_start(out=A(xslot[s7])[:, :], in_=xs_d), w=[xslot[s7].k])
        SS, SSb, QZ, Z, PZ, kcT = sm["SS"], sm["SSb"], sm["QZ"], sm["Z"], sm["PZ"], sm["kcT"]
        eye16, seqind, maskS, maskC, Rg, gdkS, snT, msr = sm["eye16"], sm["seqind"], sm["maskS"], sm["maskC"], sm["Rg"], sm["gdkS"], sm["snT"], sm["msr"]
        P.pool(lambda e: e.tensor_copy(out=A(seqind)[:, :], in_=A(onesb)[:, 0:16]), r=[onesb.k], w=[seqind.k])
        P.pool(lambda e: e.affine_select(out=A(seqind)[:, :], in_=A(seqind)[:, :], pattern=[[-8, 16]], compare_op=ALU.is_ge, fill=0.0, base=0, channel_multiplier=1),
               r=[seqind.k], w=[seqind.k])
        P.pool(lambda e: e.affine_select(out=A(seqind)[:, :], in_=A(seqind)[:, :], pattern=[[8, 16]], compare_op=ALU.is_ge, fill=0.0, base=7, channel_multiplier=-1),
               r=[seqind.k], w=[seqind.k])
        P.pool(lambda e: e.memset(A(eye16)[:, :, :], 1.0), w=[eye16.k])
        P.pool(lambda e: e.affine_select(out=A(eye16)[:, :, :], in_=A(eye16)[:, :, :], pattern=[[1, 16], [-1, 16]], compare_op=ALU.is_equal, fill=0.0, base=0, channel_multiplier=0),
               r=[eye16.k], w=[eye16.k])
        P.pool(lambda e: e.affine_select(out=A(maskS)[:, :], in_=A(maskU)[:, :], pattern=[[-8, 16], [0, 8]], compare_op=ALU.is_ge, fill=0.0, base=0, channel_multiplier=1),
               r=[maskU.k], w=[maskS.k])
        P.pool(lambda e: e.affine_select(out=A(maskS)[:, :], in_=A(maskS)[:, :], pattern=[[8, 16], [0, 8]], compare_op=ALU.is_ge, fill=0.0, base=7, channel_multiplier=-1),
               r=[maskS.k], w=[maskS.k])
        P.pool(lambda e: e.affine_select(out=A(maskC)[:, :], in_=A(onesb)[:, :], pattern=[[0, 16], [-1, 8]], compare_op=ALU.is_gt, fill=0.0, base=0, channel_multiplier=1),
               r=[onesb.k], w=[maskC.k])
        P.pool(lambda e: e.memset(A(vcext_s)[:, :, :, 64:65], 1.0), w=[vcext_s.k])
        rmul, radd = A(tmpA)[0:4, 0:128], A(tmpA)[0:4, 128:256]
        m0row, m0c = A(tmpB)[0:4, 0:128], A(tmpD)[0:4, 0:16]
        v8 = lambda ap_: ap_.rearrange("p (b t) -> p b t", t=8)
        P.dma("sp", "sm0", lambda e: e.dma_start(out=m0c, in_=sm_t), w=[tmpD.k])
        P.pool(lambda e: e.memset(rmul, 1.0), w=[tmpA.k])
        P.pool(lambda e: e.memset(v8(rmul)[:, :, 0:1], 0.0), r=[tmpA.k], w=[tmpA.k])
        P.pool(lambda e: e.memset(radd, 0.0), r=[tmpA.k], w=[tmpA.k])
        P.pool(lambda e: e.memset(v8(radd)[:, :, 0:1], -BIG), r=[tmpA.k], w=[tmpA.k])
        P.pool(lambda e: e.memset(m0row, -BIG), w=[tmpB.k])
        dve(lambda e: e.tensor_copy(out=v8(m0row)[:, :, 0], in_=m0c), [tmpD.k, tmpB.k], [tmpB.k])
        ck(20)
        P.dma("sp", "sSS", lambda e: e.dma_start(out=A(SS)[:, :, :, 0:128], in_=sC.rearrange("b (pr hh) k v -> (hh k) b pr v", hh=2)), w=[SS.k])
        P.dma("sp", "ssn", lambda e: e.dma_start(out=A(snT)[:, 0:32], in_=sn_t), w=[snT.k])
        dve(lambda e: e.tensor_copy(out=A(SS)[:, :, :, 128], in_=A(snT)[:, 0:32].rearrange("p (b r) -> p b r", r=2)), [snT.k, SS.k], [SS.k])
        ck(21)
        norm_transpose(s7, hnT, 0, None)
        qk_feature(128)
        gates(128, 0, None, (16, 8), sample_m0=(rmul, radd, m0row, m0c))
        dve(lambda e: e.tensor_tensor(out=A(msr)[0:4, 0:16], in0=A(gM)[0:4, 0:128].rearrange("p (b t) -> p b t", t=8)[:, :, 7],
                                      in1=A(gB)[0:4, 0:128].rearrange("p (b t) -> p b t", t=8)[:, :, 7], op=ALU.subtract), [gM.k, gB.k], [msr.k])
        outs.append(P.dma("sp", "oms", lambda e: e.dma_start(out=ms_t, in_=A(msr)[0:4, 0:16]), r=[msr.k], w=[("oms", 0, 1)]))
        sample_pre_e(sm)
        ck(22)
        for b in range(16):
            if b not in sm["cached"]:
                sample_cache(b, sm)
        ck(23)
        mixer_tile(0, 0, None, True, sm=sm)
        ck(30)
        for b in range(16):
            outs.append(P.dma("sp", "okws2", lambda e, b=b: e.dma_start(out=kws[b, 120:128, :], in_=A(kwout)[8 * b:8 * b + 8, :]), r=[kwout.k], w=[("okws2", b, b + 1)], total=True))
            outs.append(P.dma("sp", "ovws2", lambda e, b=b: e.dma_start(out=vws[b, 120:128, :], in_=A(vwout)[8 * b:8 * b + 8, :]), r=[vwout.k], w=[("ovws2", b, b + 1)], total=True))
        outs.append(P.dma("sp", "oCs", lambda e: e.dma_start(out=Cs.rearrange("b (pr hh) k v -> (hh k) b pr v", hh=2), in_=A(SS)[:, :, :, 0:128]), r=[SS.k], w=[("oCs", 0, 1)]))
        dve(lambda e: e.tensor_copy(out=A(snT)[:, 0:32].rearrange("p (b r) -> p b r", r=2), in_=A(SS)[:, :, :, 128]), [SS.k], [snT.k])
        outs.append(P.dma("sp", "ons", lambda e: e.dma_start(out=ns_t, in_=A(snT)[:, 0:32]), r=[snT.k], w=[("ons", 0, 1)]))
        ck(31)
        gu_ext = guring + [SB.at(xslot[j].lo, [2048], BF16) for j in range(NSLOT - 1)]
        s5_ext = s5ring + [SB.at(xslot[2 * j].lo, [3072], BF16) for j in range((NSLOT - 1) // 2)]
        tail([s7], ys, 0, "ysdst", ckoff=30, s5_slots=s5_ext, gu_slots=gu_ext)

    try:
      ck(1)
      prefix_pass()
      ck(2)
      pump_scratch(1000)
      ck(3)
      if "sl" not in halo_state:
          halo_tile()
      prev_sl = halo_state["sl"]
      ck(4)
      smS = sample_setup() if SAMPLE else None

      def cache_hook(c):
          if c < 16:
              sample_cache(c, smS)
              smS["cached"].add(c)

      for sup in range(NSUP):
          prev_sl = super_tile(xp, yp, sup * 4, sup == 0, sup == NSUP - 1, prev_sl,
                               chunk_hook=(cache_hook if (SAMPLE and sup == NSUP - 1) else None))

      for h in range(4):
          p0 = (h % 2) * 64
          outs.append(P.dma("sp", "oC", lambda e, h=h, p0=p0: e.dma_start(out=Cp[h], in_=A(S)[p0:p0 + 64, h // 2, 0:128]), r=[S.k], w=[("oCp", h, h + 1)], total=True))
      dve(lambda e: e.tensor_copy(out=A(ncol)[:, 0:2], in_=A(S)[:, :, 128]), [S.k], [ncol.k])
      outs.append(P.dma("sp", "oN", lambda e: e.dma_start(out=np_t, in_=A(ncol)[:, 0:2]), r=[ncol.k], w=[("onp", 0, 1)]))
      dve(lambda e: e.tensor_tensor(out=A(gcar)[0:4, 2:3], in0=A(gcar)[0:4, 1:2], in1=A(gcar)[0:4, 0:1], op=ALU.subtract), [gcar.k], [gcar.k])
      outs.append(P.dma("sp", "oM", lambda e: e.dma_start(out=mp, in_=A(gcar)[0:4, 2:3]), r=[gcar.k], w=[("omp", 0, 1)]))
      outs.append(P.dma("sp", "oK", lambda e: e.dma_start(out=kwp, in_=A(kwout)[:, :]), r=[kwout.k], w=[("okw", 0, 1)]))
      outs.append(P.dma("sp", "oV", lambda e: e.dma_start(out=vwp, in_=A(vwout)[:, :]), r=[vwout.k], w=[("ovw", 0, 1)]))

      if SAMPLE:
          sample_tile(smS)
    except _Stop:
        pass
    if not outs:
        outs.append(P.dma("sp", "oM", lambda e: e.dma_start(out=mp, in_=A(gcar)[0:4, 2:3]), r=[gcar.k], w=[("omp", 0, 1)]))
    P.emit(final_wait_ops=outs)
    st.close()
    print("ops", len(P.ops), "waits", P.n_waits, "sems", P.n_sems)
    return nc


_CACHE = {}


def _get_program(key):
    if key not in _CACHE:
        _CACHE[key] = build_program(*key)
    return _CACHE[key]


def _common_inputs(norm_mix_w, w_in, mlstm_i_bias, mlstm_f_bias, mlstm_norm_w, q_norm_w, k_norm_w, attn_sinks,
                   w_branch_a, w_branch_b, w_out, norm_ffn_w, w_gate, w_up, w_down):
    f = lambda a: np.ascontiguousarray(np.asarray(a, dtype=np.float32))
    return {
        "w_in": f(w_in[0]), "w_a": f(w_branch_a[0]), "w_b": f(w_branch_b[0]), "w_out": f(w_out[0]),
        "w_gate": f(w_gate[0]), "w_up": f(w_up[0]), "w_down": f(w_down[0]),
        "nmw_t": f(np.asarray(norm_mix_w[0]).reshape(8, 128).T), "nfw_t": f(np.asarray(norm_ffn_w[0]).reshape(8, 128).T),
        "ibias": f(np.asarray(mlstm_i_bias[0]).reshape(4, 1)), "fbias": f(np.asarray(mlstm_f_bias[0]).reshape(4, 1)),
        "mnw": f(np.asarray(mlstm_norm_w[0]).reshape(1, 512)),
        "qnw_dup": f(np.tile(np.asarray(q_norm_w[0]), 2).reshape(128, 1)), "knw_dup": f(np.tile(np.asarray(k_norm_w[0]), 2).reshape(128, 1)),
        "knw_row": f(np.tile(np.asarray(k_norm_w[0]), 2).reshape(1, 128)), "sinks": f(np.asarray(attn_sinks[0]).reshape(1, 8)),
    }


def _sample_inputs(c, x_sample, state_mlstm_C, state_mlstm_n, state_mlstm_m, cache_swa_k, cache_swa_v):
    f = lambda a: np.ascontiguousarray(np.asarray(a, dtype=np.float32))
    sl = slice(16 * c, 16 * c + 16)
    n = np.asarray(state_mlstm_n[0, sl]).reshape(16, 2, 2, 64)
    return {
        "xs": f(np.asarray(x_sample[sl]).reshape(128, D)),
        "sC": f(state_mlstm_C[0, sl]),
        "sn_t": f(n.transpose(2, 3, 0, 1).reshape(128, 32)),
        "sm_t": f(np.asarray(state_mlstm_m[0, sl]).T),
        "ck": f(np.asarray(cache_swa_k[0, sl]).reshape(16, 128, 128)),
        "cv": f(np.asarray(cache_swa_v[0, sl]).reshape(16, 128, 128)),
    }


def _sample_outputs(r):
    ys = r["ys"].reshape(16, 8, D)
    ns = r["ns_t"].reshape(2, 64, 16, 2).transpose(2, 3, 0, 1).reshape(16, 4, 64)
    ms = r["ms_t"].T
    return ys, r["Cs"], ns, ms, r["kws"].reshape(16, 128, 2, 64), r["vws"].reshape(16, 128, 2, 64)


def kernel(x_prompt, x_sample, state_mlstm_C, state_mlstm_n, state_mlstm_m, cache_swa_k, cache_swa_v,
           norm_mix_w, w_in, mlstm_i_bias, mlstm_f_bias, mlstm_norm_w, q_norm_w, k_norm_w, attn_sinks,
           w_branch_a, w_branch_b, w_out, norm_ffn_w, w_gate, w_up, w_down):
    f = lambda a: np.ascontiguousarray(np.asarray(a, dtype=np.float32))
    x_prompt = f(x_prompt)
    NSUP, NPRE, SAMPLE = 4, NPRE_FULL, True
    nc = _get_program((NSUP, NPRE, SAMPLE))
    common = _common_inputs(norm_mix_w, w_in, mlstm_i_bias, mlstm_f_bias, mlstm_norm_w, q_norm_w, k_norm_w, attn_sinks,
                            w_branch_a, w_branch_b, w_out, norm_ffn_w, w_gate, w_up, w_down)
    in_maps = []
    for c in range(8):
        b, j = c // 4, c % 4
        m = dict(common)
        m["xp"] = f(x_prompt[b, j * SEG:(j + 1) * SEG])
        m["xpre"] = f(x_prompt[b, 0:NPRE * 128])
        pm = np.zeros((4, NPRE * 128), np.float32)
        pm[:, :j * SEG] = 1.0
        m["premask"] = pm
        m["premask2"] = ((pm - 1.0) * np.float32(BIG)).astype(np.float32)
        pg = np.zeros((4, 16), np.float32)
        pg[:, :(j * SEG) // 512] = 1.0
        m["pmg"] = pg
        m["pmg2"] = ((pg - 1.0) * np.float32(BIG)).astype(np.float32)
        if j > 0:
            m["xhalo"] = f(x_prompt[b, j * SEG - 128:j * SEG])
            m["halov"] = np.ones((128, 1), np.float32)
        else:
            m["xhalo"] = np.zeros((128, D), np.float32)
            m["halov"] = np.zeros((128, 1), np.float32)
        m.update(_sample_inputs(c, x_sample, state_mlstm_C, state_mlstm_n, state_mlstm_m, cache_swa_k, cache_swa_v))
        in_maps.append(m)
    res = run_bass_kernel_spmd(nc, in_maps, core_ids=list(range(8)))
    R = res.results
    yp = np.stack([np.concatenate([R[b * 4 + j]["yp"] for j in range(4)], axis=0) for b in range(2)])
    Cp = np.stack([R[b * 4 + 3]["Cp"] for b in range(2)])[None]
    npo = np.stack([R[b * 4 + 3]["np_t"].reshape(2, 64, 2).transpose(2, 0, 1).reshape(4, 64) for b in range(2)])[None]
    mpo = np.stack([R[b * 4 + 3]["mp"].reshape(4) for b in range(2)])[None]
    kwp = np.stack([R[b * 4 + 3]["kwp"].reshape(128, 2, 64) for b in range(2)])[None]
    vwp = np.stack([R[b * 4 + 3]["vwp"].reshape(128, 2, 64) for b in range(2)])[None]
    so = [_sample_outputs(R[c]) for c in range(8)]
    cat = lambda i: np.concatenate([o[i] for o in so], axis=0)
    return (yp, cat(0), Cp, npo, mpo, kwp, vwp, cat(1)[None], cat(2)[None], cat(3)[None], cat(4)[None], cat(5)[None])
```

```python
import contextlib
import math
import numpy as np
import concourse.bass as bass
import concourse.mybir as mybir
from concourse.bass_utils import run_bass_kernel_spmd

F32 = mybir.dt.float32
BF16 = mybir.dt.bfloat16
AF = mybir.ActivationFunctionType
ALU = mybir.AluOpType
AX = mybir.AxisListType

D = 1024
DIN = 4360
DFF = 2816
NFF = 22
C_QM, C_KM, C_VM, C_OM, C_I, C_F, C_QA, C_KA, C_VA, C_GA, C_GB = 0, 256, 512, 1024, 1536, 1540, 1544, 2056, 2184, 2312, 3336
WRES = 2312
EPS = 1e-6
BIG = 1.0e30
SEG = 2048
NPRE_FULL = 48


class Op:
    __slots__ = ("eng", "fn", "deps", "skey", "is_dma", "signal", "sem", "val", "total")

    def __init__(self, eng, fn, skey, is_dma, total):
        self.eng = eng
        self.fn = fn
        self.deps = []
        self.skey = skey
        self.is_dma = is_dma
        self.signal = False
        self.sem = None
        self.val = 0
        self.total = total


class Prog:
    ENGS = ("pe", "act", "dve", "pool", "sp")

    def __init__(self, nc):
        self.nc = nc
        self.ops = []
        self.recs = {}

    def _track(self, op, r, w):
        deps = {}
        for (sp, lo, hi) in r:
            for rec in self.recs.get(sp, ()):
                if rec[3] == "w" and rec[0] < hi and lo < rec[1]:
                    deps[id(rec[2])] = rec[2]
        for (sp, lo, hi) in w:
            for rec in self.recs.get(sp, ()):
                if rec[0] < hi and lo < rec[1]:
                    deps[id(rec[2])] = rec[2]
        for d in deps.values():
            if d is op:
                continue
            if d.eng == "pe" and op.eng == "pe" and not d.is_dma:
                continue
            op.deps.append(d)
        for (sp, lo, hi) in w:
            lst = self.recs.setdefault(sp, [])
            lst[:] = [rec for rec in lst if not (lo <= rec[0] and rec[1] <= hi)]
            lst.append([lo, hi, op, "w"])
        for (sp, lo, hi) in r:
            lst = self.recs.setdefault(sp, [])
            lst[:] = [rec for rec in lst if not (rec[3] == "r" and rec[0] == lo and rec[1] == hi and rec[2].eng == op.eng and not rec[2].is_dma and not op.is_dma)]
            lst.append([lo, hi, op, "r"])

    def add(self, eng, fn, r=(), w=(), skey=None, is_dma=False, total=False):
        op = Op(eng, fn, skey, is_dma, total)
        r2, w2 = [], []
        for (sp, lo, hi) in r:
            if sp == "ps":
                w2.append((sp, lo // 2048 * 2048, (hi + 2047) // 2048 * 2048))
            else:
                r2.append((sp, lo, hi))
        for (sp, lo, hi) in w:
            if sp == "ps":
                w2.append((sp, lo // 2048 * 2048, (hi + 2047) // 2048 * 2048))
            else:
                w2.append((sp, lo, hi))
        self._track(op, r2, w2)
        self.ops.append(op)
        return op

    def pe(self, fn, r=(), w=()):
        return self.add("pe", fn, r, w)

    def act(self, fn, r=(), w=()):
        return self.add("act", fn, r, w)

    def dve(self, fn, r=(), w=()):
        return self.add("dve", fn, r, w)

    def pool(self, fn, r=(), w=()):
        return self.add("pool", fn, r, w)

    def dma(self, q, skey, fn, r=(), w=(), total=False):
        return self.add(q, fn, r, w, skey=skey, is_dma=True, total=total)

    def emit(self, final_wait_ops=()):
        nc = self.nc
        ops = self.ops
        for op in ops:
            for d in op.deps:
                d.signal = True
        for op in final_wait_ops:
            op.signal = True
        for op in ops:
            if op.is_dma:
                op.signal = True
        stack = contextlib.ExitStack()
        sems = {}

        def get_sem(name):
            if name not in sems:
                sems[name] = stack.enter_context(nc.semaphore(name))
            return sems[name]

        counts = {}
        for op in ops:
            if not op.signal:
                continue
            if op.is_dma:
                sname = "d_" + str(op.skey)
                inc = 16
            else:
                sname = "e_" + op.eng
                inc = 1
            counts[sname] = counts.get(sname, 0) + inc
            op.sem = get_sem(sname)
            op.val = counts[sname]
        for op in ops:
            if op.signal and op.is_dma and op.total:
                op.val = counts["d_" + str(op.skey)]
        self.n_sems = len(sems)
        per_eng = {e: [o for o in ops if o.eng == e] for e in self.ENGS}
        engobj = {"pe": "tensor", "act": "scalar", "dve": "vector", "pool": "gpsimd", "sp": "sync"}
        self.n_waits = 0
        with stack:
            with nc.Block() as block:
                def make(ename):
                    def body(eng):
                        waited = {}
                        for op in per_eng[ename]:
                            need = {}
                            for d in op.deps:
                                k = id(d.sem)
                                if need.get(k, (None, 0))[1] < d.val:
                                    need[k] = (d.sem, d.val)
                            for k, (s, v) in need.items():
                                if waited.get(k, 0) >= v:
                                    continue
                                eng.wait_ge(s, v)
                                waited[k] = v
                                self.n_waits += 1
                            ins = op.fn(eng)
                            if op.signal:
                                ins.then_inc(op.sem, 16 if op.is_dma else 1)
                        if ename == "sp":
                            for sname, cnt in counts.items():
                                if sname.startswith("d_"):
                                    eng.wait_ge(sems[sname], cnt)
                    return body

                for ename in self.ENGS:
                    getattr(block, engobj[ename])(make(ename))


class Buf:
    def __init__(self, ap, space, lo, hi):
        self.ap = ap
        self.space = space
        self.lo = lo
        self.hi = hi

    @property
    def k(self):
        return (self.space, self.lo, self.hi)

    def sub(self, i, n):
        sz = (self.hi - self.lo) // n
        return (self.space, self.lo + i * sz, self.lo + (i + 1) * sz)

    def rng(self, lo_b, hi_b):
        return (self.space, self.lo + lo_b, self.lo + hi_b)


def _shape_view(ap, shape):
    if len(shape) == 1:
        return ap
    if len(shape) == 2:
        return ap.rearrange("p (a b) -> p a b", a=shape[0])
    if len(shape) == 3:
        return ap.rearrange("p (a b c) -> p a b c", a=shape[0], b=shape[1])
    raise ValueError(shape)


class Arena:
    def __init__(self, tensor, space, nbytes):
        self.t = tensor
        self.space = space
        self.nbytes = nbytes
        self.top = 0

    def at(self, lo, shape, dt):
        n = int(np.prod(shape))
        esz = 2 if dt == BF16 else 4
        hi = lo + n * esz
        assert hi <= self.nbytes, (self.space, hi, self.nbytes)
        v = self.t[:, lo // 4:(hi + 3) // 4]
        if dt != F32:
            v = v.bitcast(dt)[:, 0:n]
        return Buf(_shape_view(v, shape), self.space, lo, hi)

    def alloc(self, shape, dt):
        lo = (self.top + 31) // 32 * 32
        b = self.at(lo, shape, dt)
        self.top = b.hi
        return b


class _Stop(Exception):
    pass


def build_program(NSUP=4, NPRE=NPRE_FULL, SAMPLE=True, STOP=None):
    nc = bass.Bass("TRN2", target_bir_lowering=False)

    def ck(n):
        if STOP is not None and n >= STOP:
            raise _Stop()
    NTOK = NSUP * 512

    def din(name, shape, dt=F32):
        return nc.dram_tensor(name, list(shape), dt, kind="ExternalInput").ap()

    def dout(name, shape):
        return nc.dram_tensor(name, list(shape), F32, kind="ExternalOutput").ap()

    def dint(name, shape, dt):
        return nc.dram_tensor(name, list(shape), dt, kind="Internal").ap()

    xp = din("xp", [NTOK, D])
    xpre = din("xpre", [max(NPRE, 1) * 128, D])
    premask = din("premask", [4, max(NPRE, 1) * 128])
    premask2 = din("premask2", [4, max(NPRE, 1) * 128])
    pmg_d = din("pmg", [4, 16])
    pmg2_d = din("pmg2", [4, 16])
    xhalo = din("xhalo", [128, D])
    halov = din("halov", [128, 1])
    w_in = din("w_in", [D, DIN])
    w_a = din("w_a", [512, D])
    w_b = din("w_b", [512, D])
    w_out = din("w_out", [D, D])
    w_gate = din("w_gate", [D, DFF])
    w_up = din("w_up", [D, DFF])
    w_down = din("w_down", [DFF, D])
    nmw_t = din("nmw_t", [128, 8])
    nfw_t = din("nfw_t", [128, 8])
    ibias = din("ibias", [4, 1])
    fbias = din("fbias", [4, 1])
    mnw = din("mnw", [1, 512])
    qnw_dup = din("qnw_dup", [128, 1])
    knw_dup = din("knw_dup", [128, 1])
    knw_row = din("knw_row", [1, 128])
    sinks = din("sinks", [1, 8])
    if SAMPLE:
        xs_d = din("xs", [128, D])
        sC = din("sC", [16, 4, 64, 128])
        sn_t = din("sn_t", [128, 32])
        sm_t = din("sm_t", [4, 16])
        ckd = din("ck", [16, 128, 128])
        cvd = din("cv", [16, 128, 128])

    yp = dout("yp", [NTOK, D])
    Cp = dout("Cp", [4, 64, 128])
    np_t = dout("np_t", [128, 2])
    mp = dout("mp", [4, 1])
    kwp = dout("kwp", [128, 128])
    vwp = dout("vwp", [128, 128])
    if SAMPLE:
        ys = dout("ys", [128, D])
        Cs = dout("Cs", [16, 4, 64, 128])
        ns_t = dout("ns_t", [128, 32])
        ms_t = dout("ms_t", [4, 16])
        kws = dout("kws", [16, 128, 128])
        vws = dout("vws", [16, 128, 128])

    s5_scr = dint("s5_scr", [8, 128, 3072], BF16)
    gu_scr = dint("gu_scr", [NFF, 128, 2048], BF16)
    wd_scr = dint("wd_scr", [DFF, D], BF16)

    st = contextlib.ExitStack()
    SB_BYTES = 212800
    sb_t = st.enter_context(nc.sbuf_tensor("arena", [128, SB_BYTES // 4], F32))
    ps_t = st.enter_context(nc.psum_tensor("psum", [128, 4096], F32))
    SB = Arena(sb_t, "sb", SB_BYTES)
    PS = Arena(ps_t, "ps", 16384)
    P = Prog(nc)
    outs = []

    def bank(b, shape=(512,), dt=F32, off=0):
        return PS.at(b * 2048 + off, shape, dt)

    win = SB.alloc([8, WRES], BF16)
    wout = SB.alloc([8, D], BF16)
    s5ring = [SB.alloc([3072], BF16) for _ in range(2)]
    guring = [SB.alloc([2048], BF16) for _ in range(3)]
    wdring = [SB.alloc([1024], BF16) for _ in range(2)]
    wdring = wdring + [SB.at(guring[j].lo + h * 2048, [1024], BF16) for j in range(3) for h in range(2)]
    NWD = len(wdring)
    NSLOT = 8
    xslot = [SB.alloc([D], F32) for _ in range(NSLOT)]
    xsb = [SB.alloc([D], BF16) for _ in range(2)]
    hnT = SB.alloc([8, 512], BF16)
    hT = SB.alloc([NFF, 512], BF16)
    qT = SB.at(hT.lo, [2, 512], BF16)
    kT = SB.at(hT.lo + 2048, [2, 512], BF16)
    hmT = SB.at(hT.lo + 4096, [4, 512], BF16)
    haT = SB.at(hT.lo + 8192, [4, 512], BF16)
    mixT = SB.at(hT.lo + 12288, [8, 512], BF16)
    ident = SB.alloc([128], BF16)
    onesb = SB.alloc([128], BF16)
    maskU = SB.alloc([128], BF16)
    maskL = SB.alloc([128], BF16)
    i4 = SB.alloc([4], F32)
    halfm = SB.alloc([128], F32)
    psel = SB.alloc([2], F32)
    ones4 = SB.alloc([512], F32)
    zeros4 = SB.alloc([512], F32)
    mhalf = SB.alloc([8], F32)
    nmw = SB.alloc([8], F32)
    nfw = SB.alloc([8], F32)
    nmwb = SB.alloc([8], F32)
    ib = SB.alloc([1], F32)
    fbn = SB.alloc([1], F32)
    mnwh = SB.alloc([512], F32)
    qw = SB.alloc([1], F32)
    kw = SB.alloc([1], F32)
    kwrow = SB.alloc([128], F32)
    sinke = SB.alloc([8], F32)
    hvalid = SB.alloc([1], F32)
    pmg = SB.alloc([16], F32)
    pmg2 = SB.alloc([16], F32)
    gX1 = SB.alloc([512], F32)
    gX2 = SB.alloc([512], F32)
    gB = SB.alloc([512], F32)
    gM = SB.alloc([512], F32)
    g8 = SB.alloc([512], F32)
    gcar = SB.alloc([8], F32)
    gg = SB.alloc([16], F32)
    gsel = SB.alloc([128], F32)
    utok = [SB.alloc([8], F32) for _ in range(2)]
    gdk = [SB.alloc([2], F32) for _ in range(2)]
    stat = [SB.alloc([32], F32) for _ in range(2)]
    vext = [SB.alloc([4, 130], BF16) for _ in range(2)]
    tho = [SB.alloc([512], F32) for _ in range(2)]
    ku = [SB.alloc([256], BF16) for _ in range(2)]
    qn = [SB.alloc([512], BF16) for _ in range(2)]
    kdup = [SB.alloc([2, 2, 64], BF16) for _ in range(2)]
    vaext = [SB.alloc([2, 66], BF16) for _ in range(4)]
    kaT = [SB.alloc([2, 128], BF16) for _ in range(4)]
    rkr = [SB.alloc([2], F32) for _ in range(4)]
    qaT = [SB.alloc([4, 128], BF16) for _ in range(2)]
    PT = SB.alloc([2, 8, 128], BF16)
    PTm = SB.alloc([4, 128], BF16)
    S = SB.alloc([2, 130], F32)
    Sp = SB.alloc([2, 130], F32)
    Spb = SB.alloc([2, 130], BF16)
    hm = [SB.alloc([512], BF16) for _ in range(2)]
    ha = [SB.alloc([512], BF16) for _ in range(2)]
    tmpA = SB.alloc([512], F32)
    tmpB = SB.alloc([512], F32)
    tmpC = SB.alloc([512], F32)
    tmpD = SB.alloc([512], F32)
    kwout = SB.alloc([128], F32)
    vwout = SB.alloc([128], F32)
    ncol = SB.alloc([2], F32)
    junk = SB.alloc([D], BF16)
    vcext_s = SB.alloc([16, 2, 65], BF16)
    print("SBUF bytes used", SB.top)

    hnT2 = SB.at(hT.lo, [8, 512], BF16)
    gX1b = SB.at(hT.lo + 8192, [512], F32)
    gX2b = SB.at(hT.lo + 8192 + 2048, [512], F32)
    gmask2 = SB.at(hT.lo + 8192 + 4096, [512], F32)
    gmask = SB.at(hT.lo + 8192 + 6144, [512], F32)
    bufsets = [dict(X1=gX1, X2=gX2, ggo=0, hn=hnT), dict(X1=gX1b, X2=gX2b, ggo=8, hn=hnT2)]
    cur = dict(bufsets[0])
    def A(buf):
        return buf.ap

    def dve(fn, r, w):
        return P.dve(fn, r=r, w=w)

    def gpo(fn, r, w):
        return P.pool(fn, r=r, w=w)

    def mm(out_ap, lhsT, rhs, start, stop, r, w):
        return P.pe(lambda e: e.matmul(out_ap, lhsT=lhsT, rhs=rhs, start=start, stop=stop), r=r, w=w)

    def tr(out_ap, in_ap, r, w):
        return P.pe(lambda e: e.transpose(out=out_ap, in_=in_ap, identity=A(ident)[:, :]), r=r + [ident.k], w=w)

    def pow_mhalf(ap_io, key, n):
        return P.pool(lambda e: e.tensor_tensor(out=ap_io, in0=ap_io, in1=A(mhalf)[0:ap_io.shape[0], 0:n], op=ALU.pow),
                      r=[key, mhalf.k], w=[key])

    P.pool(lambda e: e.memset(A(onesb)[:, :], 1.0), w=[onesb.k])
    P.pool(lambda e: e.affine_select(out=A(ident)[:, :], in_=A(onesb)[:, :], pattern=[[1, 128]], compare_op=ALU.is_equal,
                                     fill=0.0, base=0, channel_multiplier=-1), r=[onesb.k], w=[ident.k])
    P.pool(lambda e: e.affine_select(out=A(maskU)[:, :], in_=A(onesb)[:, :], pattern=[[1, 128]], compare_op=ALU.is_ge,
                                     fill=0.0, base=0, channel_multiplier=-1), r=[onesb.k], w=[maskU.k])
    P.pool(lambda e: e.affine_select(out=A(maskL)[:, :], in_=A(onesb)[:, :], pattern=[[-1, 128]], compare_op=ALU.is_gt,
                                     fill=0.0, base=0, channel_multiplier=1), r=[onesb.k], w=[maskL.k])
    P.pool(lambda e: e.memset(A(ones4)[:, :], 1.0), w=[ones4.k])
    P.pool(lambda e: e.memset(A(zeros4)[:, :], 0.0), w=[zeros4.k])
    P.pool(lambda e: e.memset(A(mhalf)[:, :], -0.5), w=[mhalf.k])
    P.pool(lambda e: e.affine_select(out=A(i4)[0:4, 0:4], in_=A(ones4)[0:4, 0:4], pattern=[[1, 4]], compare_op=ALU.is_equal,
                                     fill=0.0, base=0, channel_multiplier=-1), r=[ones4.k], w=[i4.k])
    P.pool(lambda e: e.memset(A(halfm)[0:4, :], 0.0), w=[halfm.k])
    P.dve(lambda e: e.tensor_tensor(out=A(gcar)[0:4, 2:3], in0=A(i4)[0:4, 0:1], in1=A(i4)[0:4, 2:3], op=ALU.add), r=[i4.k], w=[gcar.k])
    P.dve(lambda e: e.tensor_tensor(out=A(gcar)[0:4, 3:4], in0=A(i4)[0:4, 1:2], in1=A(i4)[0:4, 3:4], op=ALU.add), r=[i4.k, gcar.k], w=[gcar.k])
    P.dve(lambda e: e.tensor_scalar(out=A(halfm)[0:4, 0:64], in0=A(ones4)[0:4, 0:64], scalar1=A(gcar)[0:4, 2:3], scalar2=None, op0=ALU.mult),
          r=[ones4.k, gcar.k], w=[halfm.k])
    P.dve(lambda e: e.tensor_scalar(out=A(halfm)[0:4, 64:128], in0=A(ones4)[0:4, 0:64], scalar1=A(gcar)[0:4, 3:4], scalar2=None, op0=ALU.mult),
          r=[ones4.k, gcar.k, halfm.k], w=[halfm.k])
    P.dve(lambda e: e.tensor_tensor(out=A(psel)[0:4, 0:1], in0=A(i4)[0:4, 0:1], in1=A(i4)[0:4, 1:2], op=ALU.add), r=[i4.k], w=[psel.k])
    P.dve(lambda e: e.tensor_tensor(out=A(psel)[0:4, 1:2], in0=A(i4)[0:4, 2:3], in1=A(i4)[0:4, 3:4], op=ALU.add), r=[i4.k, psel.k], w=[psel.k])

    def small_load(buf, src, parts, n):
        return P.dma("sp", "setupS", lambda e: e.dma_start(out=A(buf)[0:parts, 0:n], in_=src), w=[buf.k], total=True)

    small_load(nmw, nmw_t, 128, 8)
    small_load(nfw, nfw_t, 128, 8)
    small_load(nmwb, nmw_t, 128, 8)
    small_load(ib, ibias, 4, 1)
    small_load(fbn, fbias, 4, 1)
    small_load(mnwh, mnw.partition_broadcast(128), 128, 512)
    small_load(qw, qnw_dup, 128, 1)
    small_load(kw, knw_dup, 128, 1)
    small_load(kwrow, knw_row.partition_broadcast(128), 128, 128)
    small_load(sinke, sinks.partition_broadcast(128), 128, 8)
    small_load(hvalid, halov, 128, 1)
    small_load(pmg, pmg_d, 4, 16)
    small_load(pmg2, pmg2_d, 4, 16)
    P.dve(lambda e: e.tensor_scalar(out=A(fbn)[0:4, :], in0=A(fbn)[0:4, :], scalar1=-1.0, scalar2=None, op0=ALU.mult), r=[fbn.k], w=[fbn.k])
    P.dve(lambda e: e.tensor_scalar(out=A(mnwh)[:, :], in0=A(mnwh)[:, :], scalar1=0.5, scalar2=None, op0=ALU.mult), r=[mnwh.k], w=[mnwh.k])
    P.dve(lambda e: e.tensor_scalar(out=A(qw)[:, :], in0=A(qw)[:, :], scalar1=0.125, scalar2=None, op0=ALU.mult), r=[qw.k], w=[qw.k])
    P.act(lambda e: e.activation(out=A(sinke)[:, :], in_=A(sinke)[:, :], func=AF.Exp), r=[sinke.k], w=[sinke.k])
    for b_ in vext:
        P.pool(lambda e, b_=b_: e.memset(A(b_)[:, :, 128:130], 1.0), w=[b_.k])
    for b_ in vaext:
        P.pool(lambda e, b_=b_: e.memset(A(b_)[:, :, 64:66], 1.0), w=[b_.k])
    P.pool(lambda e: e.memset(A(S)[:, :, :], 0.0), w=[S.k])
    P.pool(lambda e: e.memset(A(gcar)[0:4, 0:2], 0.0), r=[gcar.k], w=[gcar.k])

    for k in range(8):
        P.dma("pool", "setupW", lambda e, k=k: e.dma_start(out=A(win)[:, k, :], in_=w_in[k * 128:(k + 1) * 128, 0:WRES]),
              w=[win.sub(k, 8)], total=True)
    for k in range(8):
        P.dma("pool", "setupW2", lambda e, k=k: e.dma_start(out=A(wout)[:, k, :], in_=w_out[k * 128:(k + 1) * 128, :]),
              w=[wout.sub(k, 8)], total=True)

    for k in range(8):
        P.dve(lambda e, k=k: e.tensor_scalar(out=A(win)[:, k, :], in0=A(win)[:, k, :], scalar1=A(nmw)[:, k:k + 1], scalar2=None, op0=ALU.mult),
              r=[win.sub(k, 8), nmw.k], w=[win.sub(k, 8)])

    scratch_jobs = []
    scr_ctr = [0]
    NTHR = 6

    def SCR(fn, w):
        i = scr_ctr[0] % NTHR
        scr_ctr[0] += 1
        return P.dma("pool", "scr%d" % i, fn, w=list(w) + [("scrthr", i, i + 1)])
    for c in range(8):
        def j_ga(c=c):
            SCR(lambda e: e.dma_start(
                out=s5_scr[c, :, 0:1024].rearrange("p (k n) -> p k n", k=8),
                in_=w_in[:, C_GA + c * 128:C_GA + (c + 1) * 128].rearrange("(k p) n -> p k n", p=128)), w=[("s5scr", c, c + 1)])
            SCR(lambda e: e.dma_start(
                out=s5_scr[c, :, 1024:2048].rearrange("p (k n) -> p k n", k=8),
                in_=w_in[:, C_GB + c * 128:C_GB + (c + 1) * 128].rearrange("(k p) n -> p k n", p=128)), w=[("s5scr", c, c + 1)])
            SCR(lambda e: e.dma_start(
                out=s5_scr[c, :, 2048:2560].rearrange("p (k n) -> p k n", k=4),
                in_=w_a[:, c * 128:(c + 1) * 128].rearrange("(k p) n -> p k n", p=128)), w=[("s5scr", c, c + 1)])
            SCR(lambda e: e.dma_start(
                out=s5_scr[c, :, 2560:3072].rearrange("p (k n) -> p k n", k=4),
                in_=w_b[:, c * 128:(c + 1) * 128].rearrange("(k p) n -> p k n", p=128)), w=[("s5scr", c, c + 1)])
        scratch_jobs.append(j_ga)
    for c in range(NFF):
        def j_gu(c=c):
            SCR(lambda e: e.dma_start(
                out=gu_scr[c, :, 0:1024].rearrange("p (k n) -> p k n", k=8),
                in_=w_gate[:, c * 128:(c + 1) * 128].rearrange("(k p) n -> p k n", p=128)), w=[("guscr", c, c + 1)])
            SCR(lambda e: e.dma_start(
                out=gu_scr[c, :, 1024:2048].rearrange("p (k n) -> p k n", k=8),
                in_=w_up[:, c * 128:(c + 1) * 128].rearrange("(k p) n -> p k n", p=128)), w=[("guscr", c, c + 1)])
        scratch_jobs.append(j_gu)
    for q in range(4):
        def j_wd(q=q):
            SCR(lambda e: e.dma_start(out=wd_scr[q * 704:(q + 1) * 704, :], in_=w_down[q * 704:(q + 1) * 704, :]),
                  w=[("wdscr", q, q + 1)])
        scratch_jobs.append(j_wd)

    def s5_fixup(c):
        rs = s5ring[c % 2]
        P.dma("sp", "s5r%d" % (c % 2), lambda e: e.dma_start(out=A(rs)[:, 0:2048], in_=s5_scr[c, :, 0:2048]), r=[("s5scr", c, c + 1)], w=[rs.k])
        dve(lambda e: e.tensor_tensor(out=A(rs)[:, 0:2048].rearrange("p (a k n) -> p a k n", a=2, k=8),
                                      in0=A(rs)[:, 0:2048].rearrange("p (a k n) -> p a k n", a=2, k=8),
                                      in1=A(nmwb)[:, 0:8].unsqueeze(1).unsqueeze(3).to_broadcast([128, 2, 8, 128]), op=ALU.mult),
            [rs.k, nmwb.k], [rs.k])
        P.dma("sp", "s5r%d" % (c % 2), lambda e: e.dma_start(out=s5_scr[c, :, 0:2048], in_=A(rs)[:, 0:2048]), r=[rs.k], w=[("s5scr", c, c + 1)])

    def pump_scratch(n):
        for _ in range(n):
            if scratch_jobs:
                scratch_jobs.pop(0)()

    slot_ctr = [0]
    nslot_active = [NSLOT]

    def load_tile(src_rows):
        s = slot_ctr[0] % nslot_active[0]
        slot_ctr[0] += 1
        P.dma("sp", "x%d" % s, lambda e: e.dma_start(out=A(xslot[s])[:, :], in_=src_rows), w=[xslot[s].k])
        return s

    xsb_ctr = [0]

    def hk(buf, i):
        return ("v%d" % buf.lo, i, i + 1)

    def hk_all(buf):
        return ("v%d" % buf.lo, 0, 4)

    def norm_a1(s, act_rstd=False, col=0, xb=None):
        if xb is None:
            xb = xsb[xsb_ctr[0] % 2]
            xsb_ctr[0] += 1
        sq = stat[0]
        c0 = 2 * col
        k0, k1 = sq.rng(c0 * 4, c0 * 4 + 4), sq.rng(c0 * 4 + 4, c0 * 4 + 8)
        P.act(lambda e: e.activation(out=A(junk)[:, :], in_=A(xslot[s])[:, :], func=AF.Square, accum_out=A(sq)[:, c0:c0 + 1]),
              r=[xslot[s].k], w=[junk.k, k0])
        if act_rstd:
            P.act(lambda e: e.activation(out=A(sq)[:, c0 + 1:c0 + 2], in_=A(sq)[:, c0:c0 + 1], func=AF.Ln, scale=1.0 / D, bias=EPS), r=[k0], w=[k1])
            P.act(lambda e: e.activation(out=A(sq)[:, c0 + 1:c0 + 2], in_=A(sq)[:, c0 + 1:c0 + 2], func=AF.Exp, scale=-0.5), r=[k1], w=[k1])
        else:
            dve(lambda e: e.tensor_scalar(out=A(sq)[:, c0 + 1:c0 + 2], in0=A(sq)[:, c0:c0 + 1], scalar1=1.0 / D, scalar2=EPS, op0=ALU.mult, op1=ALU.add),
                [k0], [k1])
            pow_mhalf(A(sq)[:, c0 + 1:c0 + 2], k1, 1)
        return xb, sq, c0

    def norm_scale(s, xb, sq, c0):
        dve(lambda e: e.tensor_scalar(out=A(xb)[:, :], in0=A(xslot[s])[:, :], scalar1=A(sq)[:, c0 + 1:c0 + 2], scalar2=None, op0=ALU.mult),
            [xslot[s].k, sq.rng(c0 * 4 + 4, c0 * 4 + 8)], [xb.k])

    def norm_a2(xb, dstT, ti, wts, act_evac=False):
        col0 = ti * 128
        tb = bank(0, (8, 128), BF16)
        for k in range(8):
            tr(A(tb)[:, k, :], A(xb)[:, k * 128:(k + 1) * 128], [xb.k], [tb.sub(k, 8)])
        if wts is None:
            P.act(lambda e: e.activation(out=dstT.ap[:, :, col0:col0 + 128], in_=A(tb)[:, :, :], func=AF.Copy), r=[tb.k], w=[hk(dstT, ti)])
        elif act_evac:
            for k in range(8):
                P.act(lambda e, k=k: e.activation(out=dstT.ap[:, k, col0:col0 + 128], in_=A(tb)[:, k, :], func=AF.Identity, scale=A(wts)[:, k:k + 1]),
                      r=[tb.k, wts.k], w=[hk(dstT, ti)])
        else:
            dve(lambda e: e.tensor_tensor(out=dstT.ap[:, :, col0:col0 + 128], in0=A(tb)[:, :, :],
                                          in1=A(wts)[:, 0:8].unsqueeze(2).to_broadcast([128, 8, 128]), op=ALU.mult),
                [tb.k, wts.k], [hk(dstT, ti)])

    def norm_group_pre(slots_, act_rstd=False):
        pend = [norm_a1(s_, col=t_, act_rstd=act_rstd) for t_, s_ in enumerate(slots_)]
        norm_scale(slots_[0], *pend[0])
        if len(slots_) > 1:
            norm_scale(slots_[1], *pend[1])
        return pend

    def norm_group_post(slots_, pend, dstT, wts):
        for t_, s_ in enumerate(slots_):
            norm_a2(pend[t_][0], dstT, t_, wts)
            if t_ + 2 < len(slots_):
                norm_scale(slots_[t_ + 2], *pend[t_ + 2])

    def norm_group(slots_, dstT, wts):
        norm_group_post(slots_, norm_group_pre(slots_), dstT, wts)

    def norm_transpose(s, dstT, col0, wts):
        xb, sq, c0 = norm_a1(s)
        norm_scale(s, xb, sq, c0)
        norm_a2(xb, dstT, col0 // 128, wts)

    def proj_tok(colsT, col_lo, ncols, pbank):
        for k in range(8):
            mm(A(pbank)[:, 0:ncols], cur["hn"].ap[:, k, colsT:colsT + 128], A(win)[:, k, col_lo:col_lo + ncols], k == 0, k == 7,
               [hk(cur["hn"], colsT // 128), win.k], [pbank.rng(0, ncols * 4)])

    tile_ctr = [0]
    att_ctr = [0]

    def gates_a(N, c0, mask_cols, groups, sample_m0=None):
        gX1_, gX2_, ggo, hn_ = cur["X1"], cur["X2"], cur["ggo"], cur["hn"]
        pif = bank(2)
        for k in range(8):
            mm(A(pif)[0:8, 0:N], A(win)[:, k, C_I:C_I + 8], hn_.ap[:, k, c0:c0 + N], k == 0, k == 7, [win.k, hk_all(hn_)], [pif.k])
        X1, X2, Bn, M = A(gX1_)[0:4, 0:N], A(gX2_)[0:4, 0:N], A(gB)[0:4, 0:N], A(gM)[0:4, 0:N]
        P.act(lambda e: e.activation(out=A(g8)[0:8, 0:N], in_=A(pif)[0:8, 0:N], func=AF.Copy), r=[pif.k], w=[g8.k])
        P.dma("sp", "g8f", lambda e: e.dma_start(out=X2, in_=A(g8)[4:8, 0:N]), r=[g8.k], w=[gX2_.k])

    def gates_b_gen(N, c0, mask_cols, groups, sample_m0=None, defer=False):
        gX1_, gX2_, ggo, hn_ = cur["X1"], cur["X2"], cur["ggo"], cur["hn"]
        X1, X2, Bn, M = A(gX1_)[0:4, 0:N], A(gX2_)[0:4, 0:N], A(gB)[0:4, 0:N], A(gM)[0:4, 0:N]
        deferred = []
        if mask_cols is None:
            P.act(lambda e: e.activation(out=X1, in_=A(g8)[0:4, 0:N], func=AF.Identity, bias=A(ib)[0:4, 0:1]), r=[g8.k, ib.k], w=[gX1_.k])
        P.act(lambda e: e.activation(out=X2, in_=X2, func=AF.Exp, bias=A(fbn)[0:4, 0:1], scale=-1.0), r=[gX2_.k, fbn.k], w=[gX2_.k])
        P.act(lambda e: e.activation(out=X2, in_=X2, func=AF.Ln, bias=1.0), r=[gX2_.k], w=[gX2_.k])
        if mask_cols is not None:
            gq = mask_cols // 512
            dve(lambda e: e.tensor_scalar(out=X2, in0=X2, scalar1=A(pmg)[0:4, gq:gq + 1], scalar2=None, op0=ALU.mult), [gX2_.k, pmg.k], [gX2_.k])
            dve(lambda e: e.tensor_scalar(out=X1, in0=A(g8)[0:4, 0:N], scalar1=A(ib)[0:4, 0:1], scalar2=A(pmg2)[0:4, gq:gq + 1], op0=ALU.add, op1=ALU.add),
                [g8.k, ib.k, pmg2.k], [gX1_.k])
        if sample_m0 is None:
            dve(lambda e: e.tensor_tensor_scan(out=Bn, data0=A(ones4)[0:4, 0:N], data1=X2, initial=A(gcar)[0:4, 0:1], op0=ALU.mult, op1=ALU.add),
                [ones4.k, gX2_.k, gcar.k], [gB.k])
            dve(lambda e: e.tensor_tensor(out=X1, in0=X1, in1=Bn, op=ALU.add), [gX1_.k, gB.k], [gX1_.k])
            yield None
            dve(lambda e: e.tensor_tensor_scan(out=M, data0=A(zeros4)[0:4, 0:N], data1=X1, initial=A(gcar)[0:4, 1:2], op0=ALU.add, op1=ALU.max),
                [zeros4.k, gX1_.k, gcar.k], [gM.k])
        else:
            rmul, radd, m0row = sample_m0[0:3]
            dve(lambda e: e.tensor_tensor_scan(out=Bn, data0=rmul, data1=X2, initial=0.0, op0=ALU.mult, op1=ALU.add),
                [gX2_.k, tmpA.k], [gB.k])
            dve(lambda e: e.tensor_tensor(out=X1, in0=X1, in1=Bn, op=ALU.add), [gX1_.k, gB.k], [gX1_.k])
            dve(lambda e: e.tensor_tensor(out=A(tmpC)[0:4, 0:N], in0=X1, in1=m0row, op=ALU.max), [gX1_.k, tmpB.k], [tmpC.k])
            dve(lambda e: e.tensor_tensor_scan(out=M, data0=radd, data1=A(tmpC)[0:4, 0:N], initial=0.0, op0=ALU.add, op1=ALU.max),
                [tmpC.k, tmpA.k], [gM.k])
        G, GS = groups
        Mv = M.rearrange("p (g s) -> p g s", g=G)
        Me = Mv[:, :, GS - 1:GS].to_broadcast([4, G, GS])
        if sample_m0 is None:
            gs = A(gcar)[0:4, 2:2 + G]
            dve(lambda e: e.tensor_copy(out=gs[:, 0:1], in_=A(gcar)[0:4, 1:2]), [gcar.k], [gcar.k])
            if G > 1:
                dve(lambda e: e.tensor_copy(out=gs[:, 1:G], in_=Mv[:, 0:G - 1, GS - 1]), [gM.k, gcar.k], [gcar.k])
            dve(lambda e: e.tensor_tensor(out=gs, in0=gs, in1=Mv[:, :, GS - 1], op=ALU.subtract), [gM.k, gcar.k], [gcar.k])
            deferred.append(lambda: P.act(lambda e: e.activation(out=A(gg)[0:4, ggo:ggo + G], in_=gs, func=AF.Exp), r=[gcar.k], w=[gg.rng(ggo * 4, (ggo + G) * 4)]))
            if not defer:
                deferred.pop()()
        else:
            m0c = sample_m0[3]
            dve(lambda e: e.tensor_tensor(out=A(gg)[0:4, ggo:ggo + G], in0=m0c, in1=Mv[:, :, GS - 1], op=ALU.subtract), [gM.k, tmpD.k], [gg.rng(ggo * 4, (ggo + G) * 4)])
            P.act(lambda e: e.activation(out=A(gg)[0:4, ggo:ggo + G], in_=A(gg)[0:4, ggo:ggo + G], func=AF.Exp), r=[gg.rng(ggo * 4, (ggo + G) * 4)], w=[gg.rng(ggo * 4, (ggo + G) * 4)])
        if mask_cols is not None:
            assert G == 1
            nb = A(gcar)[0:4, 6:7]
            dve(lambda e: e.tensor_scalar(out=nb, in0=M[:, N - 1:N], scalar1=-1.0, scalar2=math.log(0.125), op0=ALU.mult, op1=ALU.add),
                [gM.k, gcar.k], [gcar.k])
            deferred.append(lambda: P.act(lambda e: e.activation(out=X2, in_=X1, func=AF.Exp, bias=nb), r=[gX1_.k, gcar.k], w=[gX2_.k]))
            if not defer:
                deferred.pop()()
        else:
            dve(lambda e: e.tensor_tensor(out=X2.rearrange("p (g s) -> p g s", g=G), in0=X1.rearrange("p (g s) -> p g s", g=G), in1=Me, op=ALU.subtract),
                [gX1_.k, gM.k], [gX2_.k])
            deferred.append(lambda: P.act(lambda e: e.activation(out=X2, in_=X2, func=AF.Exp, bias=math.log(0.125)), r=[gX2_.k], w=[gX2_.k]))
            if not defer:
                deferred.pop()()
            dve(lambda e: e.tensor_tensor(out=X1.rearrange("p (g s) -> p g s", g=G), in0=Bn.rearrange("p (g s) -> p g s", g=G), in1=Me, op=ALU.subtract),
                [gB.k, gM.k], [gX1_.k])
            deferred.append(lambda: P.act(lambda e: e.activation(out=X1, in_=X1, func=AF.Exp), r=[gX1_.k], w=[gX1_.k]))
            if not defer:
                deferred.pop()()
        if sample_m0 is None:
            dve(lambda e: e.tensor_copy(out=A(gcar)[0:4, 0:1], in_=Bn[:, N - 1:N]), [gB.k, gcar.k], [gcar.k])
            dve(lambda e: e.tensor_copy(out=A(gcar)[0:4, 1:2], in_=M[:, N - 1:N]), [gM.k, gcar.k], [gcar.k])
        yield deferred


    def gates_b(N, c0, mask_cols, groups, sample_m0=None, defer=False):
        out = None
        for out in gates_b_gen(N, c0, mask_cols, groups, sample_m0, defer):
            pass
        return out

    def gates(N, c0, mask_cols, groups, sample_m0=None):
        gates_a(N, c0, mask_cols, groups, sample_m0)
        gates_b(N, c0, mask_cols, groups, sample_m0)

    def gate_tok(tcol, gi, with_gdk=True):
        par = tile_ctr[0] % 2
        gX1_, gX2_, ggo = cur["X1"], cur["X2"], cur["ggo"]
        pg = bank(3)
        mm(A(pg)[:, 0:4], A(gX2_)[0:4, tcol:tcol + 128], A(i4)[0:4, 0:4], True, True, [gX2_.k, i4.k], [pg.rng(0, 16)])
        mm(A(pg)[:, 4:8], A(gX1_)[0:4, tcol:tcol + 128], A(i4)[0:4, 0:4], True, True, [gX1_.k, i4.k], [pg.rng(16, 32)])
        if with_gdk:
            dve(lambda e: e.tensor_scalar(out=A(gsel)[0:4, :], in0=A(halfm)[0:4, :], scalar1=A(gg)[0:4, ggo + gi:ggo + gi + 1], scalar2=None, op0=ALU.mult),
                [halfm.k, gg.rng((ggo + gi) * 4, (ggo + gi + 1) * 4)], [gsel.k])
            mm(A(pg)[:, 8:10], A(gsel)[0:4, :], A(psel)[0:4, 0:2], True, True, [gsel.k, psel.k], [pg.rng(32, 40)])
        dve(lambda e: e.tensor_copy(out=A(utok[par])[:, 0:8], in_=A(pg)[:, 0:8]), [pg.rng(0, 32)], [utok[par].k])
        if with_gdk:
            dve(lambda e: e.tensor_copy(out=A(gdk[par])[:, 0:2], in_=A(pg)[:, 8:10]), [pg.rng(32, 40)], [gdk[par].k])

    def state_update(par, pk_ap, pk_key, decay=True):
        dve(lambda e: e.tensor_tensor(out=A(ku[par])[:, :].rearrange("p (h d) -> p h d", h=4), in0=pk_ap[:, 0:256].rearrange("p (h d) -> p h d", h=4),
                                      in1=A(utok[par])[:, 0:4].unsqueeze(2).to_broadcast([128, 4, 64]), op=ALU.mult),
            [pk_key, utok[par].k], [ku[par].k])
        if decay:
            dve(lambda e: e.tensor_tensor(out=A(Sp)[:, :, 0:129], in0=A(S)[:, :, 0:129], in1=A(gdk[par])[:, 0:2].unsqueeze(2).to_broadcast([128, 2, 129]), op=ALU.mult),
                [S.k, gdk[par].k], [Sp.k])

    def state_update_fused(par):
        pu0 = bank(6, (2, 129))
        pu1 = bank(7, (2, 129))
        for h in range(4):
            pr = h // 2
            pu = pu0 if pr == 0 else pu1
            mm(A(pu)[:, h % 2, :], A(ku[par])[:, pr * 128:(pr + 1) * 128], A(vext[par])[:, h, 0:129], True, True,
               [ku[par].k, vext[par].k], [pu.sub(h % 2, 2)])
        for pr, pu in ((0, pu0), (1, pu1)):
            for hh in range(2):
                r0 = hh * 64
                dve(lambda e, pr=pr, pu=pu, hh=hh, r0=r0: e.scalar_tensor_tensor(
                    out=A(S)[r0:r0 + 64, pr, 0:129], in0=A(S)[r0:r0 + 64, pr, 0:129], scalar=A(gdk[par])[r0:r0 + 64, pr:pr + 1],
                    in1=A(pu)[r0:r0 + 64, hh, :], op0=ALU.mult, op1=ALU.add), [S.k, gdk[par].k, pu.k], [S.k])

    def state_update2(par):
        pu0 = bank(4, (2, 129))
        pu1 = bank(5, (2, 129))
        for h in range(4):
            pr = h // 2
            pu = pu0 if pr == 0 else pu1
            mm(A(pu)[:, h % 2, :], A(ku[par])[:, pr * 128:(pr + 1) * 128], A(vext[par])[:, h, 0:129], True, True,
               [ku[par].k, vext[par].k], [pu.sub(h % 2, 2)])
        for pr, pu in ((0, pu0), (1, pu1)):
            dve(lambda e, pr=pr, pu=pu: e.tensor_tensor(out=A(S)[0:64, pr, 0:129], in0=A(Sp)[0:64, pr, 0:129], in1=A(pu)[0:64, 0, :], op=ALU.add),
                [Sp.k, pu.k], [S.k])
            dve(lambda e, pr=pr, pu=pu: e.tensor_tensor(out=A(S)[64:128, pr, 0:129], in0=A(Sp)[64:128, pr, 0:129], in1=A(pu)[64:128, 1, :], op=ALU.add),
                [Sp.k, pu.k], [S.k])

    def prefix_B1(i, par):
        tile_ctr[0] = par
        pv = bank(1)
        proj_tok(i * 128, C_VM, 512, pv)
        P.act(lambda e: e.activation(out=A(vext[par])[:, :, 0:128], in_=A(pv)[:, :].rearrange("p (h v) -> p h v", h=4), func=AF.Copy),
              r=[pv.k], w=[vext[par].k])
        pk = bank(3, (256,), F32, 512)
        proj_tok(i * 128, C_KM, 256, pk)
        gate_tok(i * 128, 0, with_gdk=(i == 3))
        state_update(par, A(pk), pk.rng(0, 1024), decay=False)

    def prefix_B2(t):
        par, i = t % 2, t % 4
        for h in range(4):
            pr = h // 2
            pu = bank(4 + h, (129,))
            mm(A(pu)[:, 0:129], A(ku[par])[:, pr * 128:(pr + 1) * 128], A(vext[par])[:, h, 0:129], i == 0, i == 3,
               [ku[par].k, vext[par].k], [pu.k])
        if i == 3:
            for h in range(4):
                pr, r0 = h // 2, (h % 2) * 64
                pu = bank(4 + h, (129,))
                dve(lambda e, pr=pr, r0=r0, pu=pu: e.scalar_tensor_tensor(
                    out=A(S)[r0:r0 + 64, pr, 0:129], in0=A(S)[r0:r0 + 64, pr, 0:129], scalar=A(gdk[par])[r0:r0 + 64, pr:pr + 1],
                    in1=A(pu)[r0:r0 + 64, 0:129], op0=ALU.mult, op1=ALU.add), [S.k, gdk[par].k, pu.k], [S.k])

    halo_state = {}
    prefetched = {}

    def halo_tile():
        halo_T = SB.at(hT.lo + 16384, [8, 128], BF16)
        hs = load_tile(xhalo)
        xbh = SB.at(junk.lo, [D], BF16)
        xb_, sq_, c0_ = norm_a1(hs, xb=xbh, act_rstd=True)
        norm_scale(hs, xb_, sq_, c0_)
        norm_a2(xb_, halo_T, 0, None)
        save = cur["hn"]
        cur["hn"] = halo_T
        halo_state["sl"] = attn_kv(0, halo=True, pbank=2)
        cur["hn"] = save

    def prefix_pass():
        ngr = NPRE // 4
        assert NPRE % 4 == 0
        if ngr == 0:
            return
        rows = lambda u: xpre[u * 128:(u + 1) * 128, :]
        pend = {}

        loaded = {}
        LOAD_AHEAD = 4

        def ld(u):
            if u < NPRE and u not in loaded:
                loaded[u] = load_tile(rows(u))

        def a1(u):
            ld(u)
            s_ = loaded.pop(u)
            xb, sq, c0 = norm_a1(s_, act_rstd=True, col=u % 4)
            pend[u] = (s_, xb, sq, c0)

        def sc(u):
            s_, xb, sq, c0 = pend[u]
            norm_scale(s_, xb, sq, c0)

        nslot_active[0] = NSLOT - 2
        hnT3 = SB.at(xslot[NSLOT - 2].lo, [8, 512], BF16)
        gX1c = SB.at(hT.lo + 18432, [512], F32)
        gX2c = SB.at(hT.lo + 20480, [512], F32)
        bs3 = [bufsets[0], bufsets[1], dict(X1=gX1c, X2=gX2c, ggo=4, hn=hnT3)]
        pump_scratch(1000)
        for u_ in range(LOAD_AHEAD):
            ld(u_)
        a1(0)
        if NPRE > 1:
            a1(1)
        sc(0)
        LAG = 11
        gexp = {}
        for u in range(NPRE + LAG):
            g, i = u // 4, u % 4
            ld(u + LOAD_AHEAD)
            if u + 2 < NPRE:
                a1(u + 2)
            if u + 1 < NPRE:
                sc(u + 1)
            if u < NPRE:
                s_, xb, sq, c0 = pend.pop(u)
                norm_a2(xb, bs3[g % 3]["hn"], i, None)
                if i == 3:
                    cur.update(bs3[g % 3])
                    gates_a(512, 0, g * 512, (1, 512))
            if u >= 4 and i == 0 and (g - 1) * 4 + 3 < NPRE:
                cur.update(bs3[(g - 1) % 3])
                gexp["gen"] = gates_b_gen(512, 0, (g - 1) * 512, (1, 512), defer=True)
                next(gexp["gen"])
            if u >= 4 and i == 1 and "gen" in gexp:
                cur.update(bs3[(g - 1) % 3])
                gexp["d"] = next(gexp.pop("gen"))
            if u >= 4 and i == 2 and gexp.get("d"):
                for f_ in gexp.pop("d"):
                    f_()
            if u == max(9, NPRE - 5):
                halo_tile()
            if u == NPRE and NSUP > 0:
                nslot_active[0] = NSLOT
                slot_ctr[0] = 0
                nslots = [load_tile(xp[i_ * 128:(i_ + 1) * 128, :]) for i_ in range(4)]
                prefetched["p"] = (nslots, norm_group_pre(nslots, act_rstd=True))
            if u >= LAG:
                ub = u - LAG
                cur.update(bs3[(ub // 4) % 3])
                prefix_B1(ub % 4, ub % 2)
                if ub >= 1:
                    prefix_B2(ub - 1)
        nslot_active[0] = NSLOT
        prefix_B2(NPRE - 1)
        cur.update(bufsets[0])

    def attn_kv_a(colsT, last=False, halo=False, pbank=5):
        if halo:
            sl = 3
        else:
            sl = att_ctr[0] % 3
            att_ctr[0] += 1
        pkv = bank(pbank)
        proj_tok(colsT, C_KA, 256, pkv)
        sq = stat[sl % 2]
        P.act(lambda e: e.activation(out=A(tmpA)[:, 0:128], in_=A(pkv)[:, 0:128], func=AF.Square), r=[pkv.k], w=[tmpA.k])
        dve(lambda e: e.tensor_reduce(out=A(rkr[sl])[:, 0:2], in_=A(tmpA)[:, 0:128].rearrange("p (g d) -> p g d", g=2), axis=AX.X, op=ALU.add),
            [tmpA.k], [rkr[sl].k])
        dve(lambda e: e.tensor_scalar(out=A(rkr[sl])[:, 0:2], in0=A(rkr[sl])[:, 0:2], scalar1=1.0 / 64, scalar2=EPS, op0=ALU.mult, op1=ALU.add),
            [rkr[sl].k], [rkr[sl].k])
        pow_mhalf(A(rkr[sl])[:, 0:2], rkr[sl].k, 2)
        for dup in range(2):
            P.act(lambda e, dup=dup: e.activation(out=A(kdup[sl % 2])[:, :, dup, :], in_=A(pkv)[:, 0:128].rearrange("p (g d) -> p g d", g=2), func=AF.Copy),
                  r=[pkv.k], w=[kdup[sl % 2].k])
        if halo:
            dve(lambda e: e.tensor_scalar(out=A(vaext[sl])[:, :, 0:64], in0=A(pkv)[:, 128:256].rearrange("p (g d) -> p g d", g=2),
                                          scalar1=A(hvalid)[:, 0:1], scalar2=None, op0=ALU.mult), [pkv.k, hvalid.k], [vaext[sl].k])
            dve(lambda e: e.tensor_scalar(out=A(vaext[sl])[:, :, 64:66], in0=A(vaext[sl])[:, :, 64:66], scalar1=A(hvalid)[:, 0:1], scalar2=None, op0=ALU.mult),
                [vaext[sl].k, hvalid.k], [vaext[sl].k])
        else:
            P.act(lambda e: e.activation(out=A(vaext[sl])[:, :, 0:64], in_=A(pkv)[:, 128:256].rearrange("p (g d) -> p g d", g=2), func=AF.Copy),
                  r=[pkv.k], w=[vaext[sl].k])
        if last:
            dve(lambda e: e.tensor_tensor(out=A(kwout)[:, :].rearrange("p (g d) -> p g d", g=2), in0=A(pkv)[:, 0:128].rearrange("p (g d) -> p g d", g=2),
                                          in1=A(rkr[sl])[:, 0:2].unsqueeze(2).to_broadcast([128, 2, 64]), op=ALU.mult), [pkv.k, rkr[sl].k], [kwout.k])
            dve(lambda e: e.tensor_tensor(out=A(kwout)[:, :], in0=A(kwout)[:, :], in1=A(kwrow)[:, :], op=ALU.mult), [kwout.k, kwrow.k], [kwout.k])
            dve(lambda e: e.tensor_copy(out=A(vwout)[:, :], in_=A(pkv)[:, 128:256]), [pkv.k], [vwout.k])
        return sl

    def attn_kv_b(sl):
        tb = bank(0, (8, 128), BF16)
        for g in range(2):
            tr(A(tb)[:, g, :], A(kdup[sl % 2])[:, g, :, :].rearrange("p a d -> p (a d)"), [kdup[sl % 2].k], [tb.sub(g, 8)])
        dve(lambda e: e.tensor_scalar(out=A(kaT[sl])[:, :, :], in0=A(tb)[:, 0:2, :], scalar1=A(kw)[:, 0:1], scalar2=None, op0=ALU.mult),
            [tb.rng(0, 512), kw.k], [kaT[sl].k])

    def attn_kv(colsT, last=False, halo=False, pbank=5):
        sl = attn_kv_a(colsT, last=last, halo=halo, pbank=pbank)
        attn_kv_b(sl)
        return sl

    def tail(slots, dst, t0, dkey, ckoff=0, before_pass2=None, s5_slots=None, gu_slots=None, chunk_hook=None):
        nt = len(slots)
        NT = nt * 128
        s5_slots = s5_slots or s5ring
        gu_slots = gu_slots or guring
        dve(lambda e: e.tensor_tensor(out=A(hnT)[:, :, 0:NT], in0=A(hnT)[:, :, 0:NT], in1=A(nmwb)[:, 0:8].unsqueeze(2).to_broadcast([128, 8, NT]), op=ALU.mult),
            [hk_all(hnT), nmwb.k], [hk_all(hnT)])
        for c in range(8):
            rs = s5_slots[c % len(s5_slots)]
            P.dma("sp", "s5r%d" % (c % len(s5_slots)), lambda e, c=c, rs=rs: e.dma_start(out=A(rs)[:, :], in_=s5_scr[c, :, :]), r=[("s5scr", c, c + 1)], w=[rs.k])
            wga = A(rs)[:, 0:1024].rearrange("p (k n) -> p k n", k=8)
            wgb = A(rs)[:, 1024:2048].rearrange("p (k n) -> p k n", k=8)
            wac = A(rs)[:, 2048:2560].rearrange("p (k n) -> p k n", k=4)
            wbc = A(rs)[:, 2560:3072].rearrange("p (k n) -> p k n", k=4)
            pga, pgb, pa, pb_ = bank(4), bank(5), bank(6), bank(7)
            for k in range(8):
                mm(A(pga)[:, 0:NT], wga[:, k, :], A(hnT)[:, k, 0:NT], k == 0, k == 7, [rs.k, hk_all(hnT)], [pga.k])
            for k in range(8):
                mm(A(pgb)[:, 0:NT], wgb[:, k, :], A(hnT)[:, k, 0:NT], k == 0, k == 7, [rs.k, hk_all(hnT)], [pgb.k])
            for k in range(4):
                mm(A(pa)[:, 0:NT], wac[:, k, :], A(hmT)[:, k, 0:NT], k == 0, k == 3, [rs.k, hmT.k], [pa.k])
            for k in range(4):
                mm(A(pb_)[:, 0:NT], wbc[:, k, :], A(haT)[:, k, 0:NT], k == 0, k == 3, [rs.k, haT.k], [pb_.k])
            P.act(lambda e, pga=pga: e.activation(out=A(tmpA)[:, 0:NT], in_=A(pga)[:, 0:NT], func=AF.Tanh, scale=0.5), r=[pga.k], w=[tmpA.k])
            P.act(lambda e, pgb=pgb: e.activation(out=A(tmpB)[:, 0:NT], in_=A(pgb)[:, 0:NT], func=AF.Tanh, scale=0.5), r=[pgb.k], w=[tmpB.k])
            dve(lambda e, pa=pa: e.scalar_tensor_tensor(out=A(tmpA)[:, 0:NT], in0=A(tmpA)[:, 0:NT], scalar=1.0, in1=A(pa)[:, 0:NT], op0=ALU.add, op1=ALU.mult),
                [tmpA.k, pa.k], [tmpA.k])
            dve(lambda e, pb_=pb_: e.scalar_tensor_tensor(out=A(tmpB)[:, 0:NT], in0=A(tmpB)[:, 0:NT], scalar=1.0, in1=A(pb_)[:, 0:NT], op0=ALU.add, op1=ALU.mult),
                [tmpB.k, pb_.k], [tmpB.k])
            dve(lambda e, c=c: e.tensor_tensor(out=A(mixT)[:, c, 0:NT], in0=A(tmpA)[:, 0:NT], in1=A(tmpB)[:, 0:NT], op=ALU.add), [tmpA.k, tmpB.k], [mixT.sub(c, 8)])
        ck(10 + ckoff)
        for i, s in enumerate(slots):
            for hh in range(2):
                px = bank(1 + hh)
                for k in range(8):
                    mm(A(px)[:, :], A(mixT)[:, k, i * 128:(i + 1) * 128], A(wout)[:, k, hh * 512:(hh + 1) * 512], k == 0, k == 7, [mixT.k, wout.k], [px.k])
                dve(lambda e, s=s, hh=hh, px=px: e.scalar_tensor_tensor(out=A(xslot[s])[:, hh * 512:(hh + 1) * 512], in0=A(px)[:, :], scalar=0.5,
                                                                      in1=A(xslot[s])[:, hh * 512:(hh + 1) * 512], op0=ALU.mult, op1=ALU.add),
                    [px.k, xslot[s].k], [xslot[s].k])
        norm_group(slots, hnT, nfw)
        ck(11 + ckoff)
        for c in range(NFF):
            rs = gu_slots[c % len(gu_slots)]
            P.dma("sp", "gur%d" % (c % len(gu_slots)), lambda e, c=c, rs=rs: e.dma_start(out=A(rs)[:, :], in_=gu_scr[c, :, :]), r=[("guscr", c, c + 1)], w=[rs.k])
            wg = A(rs)[:, 0:1024].rearrange("p (k n) -> p k n", k=8)
            wu = A(rs)[:, 1024:2048].rearrange("p (k n) -> p k n", k=8)
            pg_, pu_ = bank((c % 2) * 2), bank((c % 2) * 2 + 1)
            for k in range(8):
                mm(A(pg_)[:, 0:NT], wg[:, k, :], A(hnT)[:, k, 0:NT], k == 0, k == 7, [rs.k, hk_all(hnT)], [pg_.k])
            for k in range(8):
                mm(A(pu_)[:, 0:NT], wu[:, k, :], A(hnT)[:, k, 0:NT], k == 0, k == 7, [rs.k, hk_all(hnT)], [pu_.k])
            ta = tmpC if c % 2 == 0 else tmpD
            P.act(lambda e, pg_=pg_, ta=ta: e.activation(out=A(ta)[:, 0:NT], in_=A(pg_)[:, 0:NT], func=AF.Tanh, scale=0.5), r=[pg_.k], w=[ta.k])
            dve(lambda e, pg_=pg_, ta=ta: e.scalar_tensor_tensor(out=A(ta)[:, 0:NT], in0=A(ta)[:, 0:NT], scalar=1.0, in1=A(pg_)[:, 0:NT], op0=ALU.add, op1=ALU.mult),
                [ta.k, pg_.k], [ta.k])
            dve(lambda e, pu_=pu_, ta=ta, c=c: e.scalar_tensor_tensor(out=A(hT)[:, c, 0:NT], in0=A(ta)[:, 0:NT], scalar=0.5, in1=A(pu_)[:, 0:NT], op0=ALU.mult, op1=ALU.mult),
                [ta.k, pu_.k], [hT.sub(c, NFF)])
            if chunk_hook is not None:
                chunk_hook(c)
        ck(12 + ckoff)
        if before_pass2 is not None:
            before_pass2()
        for c in range(NFF):
            rs = wdring[c % NWD]
            P.dma("sp", "wdr%d" % (c % NWD), lambda e, c=c, rs=rs: e.dma_start(out=A(rs)[:, :], in_=wd_scr[c * 128:(c + 1) * 128, :]), r=[("wdscr", 0, 4)], w=[rs.k])
            for i in range(nt):
                for hh in range(2):
                    py = bank(i * 2 + hh)
                    mm(A(py)[:, :], A(hT)[:, c, i * 128:(i + 1) * 128], A(rs)[:, hh * 512:(hh + 1) * 512], c == 0, c == NFF - 1, [hT.sub(c, NFF), rs.k], [py.k])
        for i, s in enumerate(slots):
            for hh in range(2):
                py = bank(i * 2 + hh)
                dve(lambda e, s=s, hh=hh, py=py: e.tensor_tensor(out=A(xslot[s])[:, hh * 512:(hh + 1) * 512], in0=A(py)[:, :], in1=A(xslot[s])[:, hh * 512:(hh + 1) * 512], op=ALU.add),
                    [py.k, xslot[s].k], [xslot[s].k])
            outs.append(P.dma("sp", "x%d" % s, lambda e, s=s, i=i: e.dma_start(out=dst[(t0 + i) * 128:(t0 + i + 1) * 128, :], in_=A(xslot[s])[:, :]),
                              r=[xslot[s].k], w=[(dkey, t0 + i, t0 + i + 1)]))

    def seg_A(i, cT, is_last_tile, sm=None):
        par = i % 2
        pv = bank(1)
        proj_tok(cT, C_VM, 512, pv)
        P.act(lambda e: e.activation(out=A(vext[par])[:, :, 0:128], in_=A(pv)[:, :].rearrange("p (h v) -> p h v", h=4), func=AF.Copy),
              r=[pv.k], w=[vext[par].k])
        po = bank(2)
        proj_tok(cT, C_OM, 512, po)
        P.act(lambda e: e.activation(out=A(tho[par])[:, :], in_=A(po)[:, :], func=AF.Tanh, scale=0.5), r=[po.k], w=[tho[par].k])
        dve(lambda e: e.scalar_tensor_tensor(out=A(tho[par])[:, :], in0=A(tho[par])[:, :], scalar=1.0, in1=A(mnwh)[:, :], op0=ALU.add, op1=ALU.mult),
            [tho[par].k, mnwh.k], [tho[par].k])
        ck(7 if sm is None else 27)
        pq = bank(1)
        proj_tok(cT, C_QA, 512, pq)
        sq = stat[par]
        P.act(lambda e: e.activation(out=A(tmpA)[:, :], in_=A(pq)[:, :], func=AF.Square), r=[pq.k], w=[tmpA.k])
        dve(lambda e: e.tensor_reduce(out=A(sq)[:, 8:16], in_=A(tmpA)[:, :].rearrange("p (h d) -> p h d", h=8), axis=AX.X, op=ALU.add),
            [tmpA.k], [sq.rng(32, 64)])
        dve(lambda e: e.tensor_scalar(out=A(sq)[:, 8:16], in0=A(sq)[:, 8:16], scalar1=1.0 / 64, scalar2=EPS, op0=ALU.mult, op1=ALU.add),
            [sq.rng(32, 64)], [sq.rng(32, 64)])
        pow_mhalf(A(sq)[:, 8:16], sq.rng(32, 64), 8)
        dve(lambda e: e.tensor_tensor(out=A(qn[par])[:, :].rearrange("p (h d) -> p h d", h=8), in0=A(pq)[:, :].rearrange("p (h d) -> p h d", h=8),
                                      in1=A(sq)[:, 8:16].unsqueeze(2).to_broadcast([128, 8, 64]), op=ALU.mult),
            [pq.k, sq.rng(32, 64)], [qn[par].k])
        cur_sl = attn_kv_a(cT, last=is_last_tile, pbank=2)
        return dict(i=i, cT=cT, par=par, sq=sq, cur_sl=cur_sl, sm=sm)

    def seg_B(tc):
        par, cur_sl = tc["par"], tc["cur_sl"]
        tb = bank(0, (8, 128), BF16)
        for c in range(4):
            tr(A(tb)[:, 4 + c, :], A(qn[par])[:, c * 128:(c + 1) * 128], [qn[par].k], [tb.sub(4 + c, 8)])
        dve(lambda e: e.tensor_scalar(out=A(qaT[par])[:, :, :], in0=A(tb)[:, 4:8, :], scalar1=A(qw)[:, 0:1], scalar2=None, op0=ALU.mult),
            [tb.rng(1024, 2048), qw.k], [qaT[par].k])
        attn_kv_b(cur_sl)

    def seg_C(tc, prev_sl):
        i, cT, par, sq, cur_sl, sm = tc["i"], tc["cT"], tc["par"], tc["sq"], tc["cur_sl"], tc["sm"]
        for blk in range(2):
            pbk = (bank(4), bank(5))
            sl = prev_sl if blk == 0 else cur_sl
            for hq in range(8):
                p0 = (hq % 2) * 64
                idx = hq // 2
                if sm is not None and blk == 0:
                    for b in range(16):
                        mm(A(pbk[hq % 2])[:, idx * 128 + 8 * b:idx * 128 + 8 * b + 8], A(sm["kcT"])[p0:p0 + 64, b, hq // 4, :],
                           A(qaT[par])[p0:p0 + 64, hq // 2, 8 * b:8 * b + 8], True, True, [sm["kcT"].k, qaT[par].k], [pbk[hq % 2].sub(idx, 4)])
                else:
                    mm(A(pbk[hq % 2])[:, idx * 128:(idx + 1) * 128], A(kaT[sl])[p0:p0 + 64, hq // 4, :], A(qaT[par])[p0:p0 + 64, hq // 2, :], True, True,
                       [kaT[sl].k, qaT[par].k], [pbk[hq % 2].sub(idx, 4)])
            PTv = A(PT)[:, blk, :, :].rearrange("p (g i two) t -> p g i two t", g=2, i=2, two=2)
            for par2 in range(2):
                for g in range(2):
                    if sm is not None and blk == 0:
                        P.act(lambda e, par2=par2, g=g, PTv=PTv, pbk=pbk: e.activation(
                            out=PTv[:, g, :, par2, :], in_=A(pbk[par2])[:, g * 256:(g + 1) * 256].rearrange("p (i t) -> p i t", i=2), func=AF.Exp),
                            r=[pbk[par2].k], w=[PT.sub(blk, 2)])
                    else:
                        P.act(lambda e, par2=par2, g=g, sl=sl, PTv=PTv, pbk=pbk: e.activation(
                            out=PTv[:, g, :, par2, :], in_=A(pbk[par2])[:, g * 256:(g + 1) * 256].rearrange("p (i t) -> p i t", i=2),
                            func=AF.Exp, scale=A(rkr[sl])[:, g:g + 1]),
                            r=[pbk[par2].k, rkr[sl].k], w=[PT.sub(blk, 2)])
            if sm is None:
                mk = maskL if blk == 0 else maskU
            else:
                mk = sm["maskC"] if blk == 0 else sm["maskS"]
            dve(lambda e, blk=blk, mk=mk: e.tensor_tensor(out=A(PT)[:, blk, :, :], in0=A(PT)[:, blk, :, :],
                                                          in1=A(mk)[:, :].unsqueeze(1).to_broadcast([128, 8, 128]), op=ALU.mult),
                [PT.sub(blk, 2), mk.k], [PT.sub(blk, 2)])

    def seg_D(tc, prev_sl):
        i, cT, par, sq, cur_sl, sm = tc["i"], tc["cT"], tc["par"], tc["sq"], tc["cur_sl"], tc["sm"]
        pzc = 0
        for hb in range(2):
            po_ = bank(6 + hb, (4, 66))
            for hh in range(4):
                hq = hb * 4 + hh
                if sm is None:
                    mm(A(po_)[:, hh, 0:65], A(PT)[:, 0, hq, :], A(vaext[prev_sl])[:, hb, 0:65], True, False, [PT.sub(0, 2), vaext[prev_sl].k], [po_.sub(hh, 4)])
                else:
                    PZ = sm["PZ"][pzc % 2]
                    pzc += 1
                    dve(lambda e, hq=hq, PZ=PZ: e.tensor_tensor(
                        out=A(PZ)[:, :, :].rearrange("p b (c j) -> p b c j", j=8),
                        in0=A(PT)[:, 0, hq, :].rearrange("p (c j) -> p c j", j=8).unsqueeze(1).to_broadcast([128, 16, 16, 8]),
                        in1=A(sm["eye16"])[:, :, :].unsqueeze(3).to_broadcast([128, 16, 16, 8]), op=ALU.mult),
                        [PT.sub(0, 2), sm["eye16"].k], [PZ.k])
                    for b in range(16):
                        mm(A(po_)[:, hh, 0:65], A(PZ)[:, b, :], A(sm["vcext"])[:, b, hb, 0:65], b == 0, False, [PZ.k, sm["vcext"].k], [po_.sub(hh, 4)])
                mm(A(po_)[:, hh, 0:65], A(PT)[:, 1, hq, :], A(vaext[cur_sl])[:, hb, 0:65], False, True, [PT.sub(1, 2), vaext[cur_sl].k], [po_.sub(hh, 4)])
            dve(lambda e, hb=hb, po_=po_: e.tensor_tensor(out=A(sq)[:, 16 + hb * 4:20 + hb * 4], in0=A(po_)[:, :, 64], in1=A(sinke)[:, hb * 4:(hb + 1) * 4], op=ALU.add),
                [po_.k, sinke.k], [sq.rng(64 + hb * 16, 80 + hb * 16)])
            dve(lambda e, hb=hb: e.reciprocal(out=A(sq)[:, 16 + hb * 4:20 + hb * 4], in_=A(sq)[:, 16 + hb * 4:20 + hb * 4]),
                [sq.rng(64 + hb * 16, 80 + hb * 16)], [sq.rng(64 + hb * 16, 80 + hb * 16)])
            dve(lambda e, hb=hb, po_=po_: e.tensor_tensor(
                out=A(ha[par])[:, hb * 256:(hb + 1) * 256].rearrange("p (h d) -> p h d", h=4), in0=A(po_)[:, :, 0:64],
                in1=A(sq)[:, 16 + hb * 4:20 + hb * 4].unsqueeze(2).to_broadcast([128, 4, 64]), op=ALU.mult),
                [po_.k, sq.rng(64 + hb * 16, 80 + hb * 16)], [ha[par].k])
        ck(8 if sm is None else 28)

    def sample_pre_e(sm, cT=0):
        SS, SSb, QZ, Z = sm["SS"], sm["SSb"], sm["QZ"], sm["Z"]
        dve(lambda e: e.tensor_tensor(out=A(sm["Rg"])[0:4, :].rearrange("p (b r) -> p b r", r=2), in0=A(gg)[0:4, 0:16].unsqueeze(2).to_broadcast([4, 16, 2]),
                                      in1=A(psel)[0:4, 0:2].unsqueeze(1).to_broadcast([4, 16, 2]), op=ALU.mult), [gg.k, psel.k], [sm["Rg"].k])
        pgs = bank(3)
        mm(A(pgs)[:, 32:64], A(halfm)[0:4, :], A(sm["Rg"])[0:4, 0:32], True, True, [halfm.k, sm["Rg"].k], [pgs.rng(128, 256)])
        dve(lambda e: e.tensor_copy(out=A(sm["gdkS"])[:, 0:32], in_=A(pgs)[:, 32:64]), [pgs.rng(128, 256)], [sm["gdkS"].k])
        dve(lambda e: e.tensor_tensor(out=A(SS)[:, :, :, 0:129], in0=A(SS)[:, :, :, 0:129],
                                      in1=A(sm["gdkS"])[:, 0:32].rearrange("p (b r) -> p b r", r=2).unsqueeze(3).to_broadcast([128, 16, 2, 129]), op=ALU.mult),
            [SS.k, sm["gdkS"].k], [SS.k])
        dve(lambda e: e.tensor_copy(out=A(SSb)[:, :, :, 0:129], in_=A(SS)[:, :, :, 0:129]), [SS.k], [SSb.k])
        for pr in range(2):
            dve(lambda e, pr=pr: e.tensor_tensor(
                out=A(QZ)[:, pr, :, :].rearrange("p b (c j) -> p b c j", j=8),
                in0=A(qT)[:, pr, cT:cT + 128].rearrange("p (c j) -> p c j", j=8).unsqueeze(1).to_broadcast([128, 16, 16, 8]),
                in1=A(sm["eye16"])[:, :, :].unsqueeze(3).to_broadcast([128, 16, 16, 8]), op=ALU.mult),
                [qT.k, sm["eye16"].k], [QZ.sub(pr, 2)])


    def seg_E(tc):
        i, cT, par, sq, cur_sl, sm = tc["i"], tc["cT"], tc["par"], tc["sq"], tc["cur_sl"], tc["sm"]
        tile_ctr[0] = par
        pk = bank(3, (256,), F32, 512)
        proj_tok(cT, C_KM, 256, pk)
        gate_tok(cT, i)
        state_update(par, A(pk), pk.rng(0, 1024), decay=(sm is None))
        if sm is None:
            dve(lambda e: e.tensor_copy(out=A(Spb)[:, :, 0:129], in_=A(Sp)[:, :, 0:129]), [Sp.k], [Spb.k])
        else:
            Z = sm["Z"]
            dve(lambda e: e.tensor_tensor(out=A(Z)[:, :, :], in0=A(ku[par])[:, :].unsqueeze(1).to_broadcast([128, 16, 256]),
                                          in1=A(sm["seqind"])[:, :].unsqueeze(2).to_broadcast([128, 16, 256]), op=ALU.mult),
                [ku[par].k, sm["seqind"].k], [Z.k])
        for h in range(4):
            p0 = (h % 2) * 64
            pst = bank(4 + h % 2)
            mm(A(pst)[:, (h // 2) * 128:(h // 2 + 1) * 128], A(kT)[p0:p0 + 64, h // 2, cT:cT + 128], A(qT)[p0:p0 + 64, h // 2, cT:cT + 128], True, True,
               [kT.k, qT.k], [pst.sub(h // 2, 4)])
        mkm = maskU if sm is None else sm["maskS"]
        for h in range(4):
            pst = bank(4 + h % 2)
            dve(lambda e, h=h, pst=pst: e.scalar_tensor_tensor(out=A(PTm)[:, h, :], in0=A(pst)[:, (h // 2) * 128:(h // 2 + 1) * 128], scalar=A(utok[par])[:, h:h + 1],
                                                              in1=A(mkm)[:, :], op0=ALU.mult, op1=ALU.mult),
                [pst.sub(h // 2, 4), utok[par].k, mkm.k], [PTm.sub(h, 4)])

    def seg_F(tc):
        i, cT, par, sq, cur_sl, sm = tc["i"], tc["cT"], tc["par"], tc["sq"], tc["cur_sl"], tc["sm"]
        tile_ctr[0] = par
        if sm is not None:
            SS, SSb, QZ, Z = sm["SS"], sm["SSb"], sm["QZ"], sm["Z"]
        pm0 = bank(6, (2, 129))
        pm1 = bank(7, (2, 129))
        for h in range(4):
            pm = pm0 if h < 2 else pm1
            p0 = (h % 2) * 64
            mm(A(pm)[:, h % 2, :], A(PTm)[:, h, :], A(vext[par])[:, h, 0:129], True, False, [PTm.sub(h, 4), vext[par].k], [pm.sub(h % 2, 2)])
            if sm is None:
                mm(A(pm)[:, h % 2, :], A(qT)[p0:p0 + 64, h // 2, cT:cT + 128], A(Spb)[p0:p0 + 64, h // 2, 0:129], False, True, [qT.k, Spb.k], [pm.sub(h % 2, 2)])
            else:
                for b in range(16):
                    mm(A(pm)[:, h % 2, :], A(QZ)[p0:p0 + 64, h // 2, b, :], A(SSb)[p0:p0 + 64, b, h // 2, 0:129], False, b == 15, [QZ.k, SSb.k], [pm.sub(h % 2, 2)])
        for pr, pm in ((0, pm0), (1, pm1)):
            c8 = 24 + pr * 2
            P.act(lambda e, pm=pm, c8=c8: e.activation(out=A(sq)[:, c8:c8 + 2], in_=A(pm)[:, :, 128], func=AF.Abs),
                  r=[pm.k], w=[sq.rng(96 + pr * 8, 104 + pr * 8)])
            for hh in range(2):
                h = pr * 2 + hh
                P.act(lambda e, pm=pm, hh=hh, h=h: e.activation(out=A(junk)[:, 0:128], in_=A(pm)[:, hh, 0:128], func=AF.Square, accum_out=A(sq)[:, 28 + h:29 + h]),
                      r=[pm.k], w=[junk.k, sq.rng(112 + h * 4, 116 + h * 4)])
        dve(lambda e: e.tensor_tensor(out=A(sq)[:, 24:28], in0=A(sq)[:, 24:28], in1=A(utok[par])[:, 4:8], op=ALU.max),
            [sq.rng(96, 112), utok[par].k], [sq.rng(96, 112)])
        dve(lambda e: e.reciprocal(out=A(sq)[:, 24:28], in_=A(sq)[:, 24:28]), [sq.rng(96, 112)], [sq.rng(96, 112)])
        dve(lambda e: e.tensor_tensor(out=A(sq)[:, 28:32], in0=A(sq)[:, 28:32], in1=A(sq)[:, 24:28], op=ALU.mult), [sq.rng(96, 128)], [sq.rng(112, 128)])
        dve(lambda e: e.tensor_tensor(out=A(sq)[:, 28:32], in0=A(sq)[:, 28:32], in1=A(sq)[:, 24:28], op=ALU.mult), [sq.rng(96, 128)], [sq.rng(112, 128)])
        dve(lambda e: e.tensor_scalar(out=A(sq)[:, 28:32], in0=A(sq)[:, 28:32], scalar1=1.0 / 128, scalar2=EPS, op0=ALU.mult, op1=ALU.add),
            [sq.rng(112, 128)], [sq.rng(112, 128)])
        pow_mhalf(A(sq)[:, 28:32], sq.rng(112, 128), 4)
        dve(lambda e: e.tensor_tensor(out=A(sq)[:, 28:32], in0=A(sq)[:, 28:32], in1=A(sq)[:, 24:28], op=ALU.mult), [sq.rng(96, 128)], [sq.rng(112, 128)])
        for h in range(4):
            pm = pm0 if h < 2 else pm1
            dve(lambda e, h=h, pm=pm: e.scalar_tensor_tensor(out=A(hm[par])[:, h * 128:(h + 1) * 128], in0=A(pm)[:, h % 2, 0:128], scalar=A(sq)[:, 28 + h:29 + h],
                                                            in1=A(tho[par])[:, h * 128:(h + 1) * 128], op0=ALU.mult, op1=ALU.mult),
                [pm.k, sq.rng(112, 128), tho[par].k], [hm[par].k])
        if sm is None:
            state_update2(par)
        else:
            cnt = 0
            for pr in range(2):
                for b in range(16):
                    pu = bank(4 + cnt % 4, (2, 129))
                    cnt += 1
                    for hh in range(2):
                        mm(A(pu)[:, hh, :], A(Z)[:, b, pr * 128:(pr + 1) * 128], A(vext[par])[:, 2 * pr + hh, 0:129], True, True, [Z.k, vext[par].k], [pu.sub(hh, 2)])
                    dve(lambda e, pr=pr, b=b, pu=pu: e.tensor_tensor(out=A(SS)[0:64, b, pr, 0:129], in0=A(SS)[0:64, b, pr, 0:129], in1=A(pu)[0:64, 0, :], op=ALU.add),
                        [SS.k, pu.k], [SS.k])
                    dve(lambda e, pr=pr, b=b, pu=pu: e.tensor_tensor(out=A(SS)[64:128, b, pr, 0:129], in0=A(SS)[64:128, b, pr, 0:129], in1=A(pu)[64:128, 1, :], op=ALU.add),
                        [SS.k, pu.k], [SS.k])

    def seg_G(tc):
        i, cT, par, sq, cur_sl, sm = tc["i"], tc["cT"], tc["par"], tc["sq"], tc["cur_sl"], tc["sm"]
        tb2 = bank(0, (8, 128), BF16)
        for c in range(4):
            tr(A(tb2)[:, c, :], A(hm[par])[:, c * 128:(c + 1) * 128], [hm[par].k], [tb2.sub(c, 8)])
        for c in range(4):
            tr(A(tb2)[:, 4 + c, :], A(ha[par])[:, c * 128:(c + 1) * 128], [ha[par].k], [tb2.sub(4 + c, 8)])
        P.act(lambda e: e.activation(out=A(hmT)[:, :, cT:cT + 128], in_=A(tb2)[:, 0:4, :], func=AF.Copy), r=[tb2.rng(0, 1024)], w=[hmT.k])
        P.act(lambda e: e.activation(out=A(haT)[:, :, cT:cT + 128], in_=A(tb2)[:, 4:8, :], func=AF.Copy), r=[tb2.rng(1024, 2048)], w=[haT.k])
        ck(9 if sm is None else 29)

    def mixer_tile(i, cT, prev_sl, is_last_tile, sm=None):
        tc = seg_A(i, cT, is_last_tile, sm)
        seg_B(tc)
        seg_C(tc, prev_sl)
        seg_D(tc, prev_sl)
        seg_E(tc)
        seg_F(tc)
        seg_G(tc)
        return tc["cur_sl"]

    def qk_feature(NT):
        for blk, (dstb, c_lo) in enumerate(((qT, C_QM), (qT, C_QM + 128), (kT, C_KM), (kT, C_KM + 128))):
            pb = bank(1 + blk % 2)
            for k in range(8):
                mm(A(pb)[:, 0:NT], A(win)[:, k, c_lo:c_lo + 128], hnT.ap[:, k, 0:NT], k == 0, k == 7, [win.k, hk_all(hnT)], [pb.k])
            P.act(lambda e, pb=pb, dstb=dstb, blk=blk: e.activation(out=dstb.ap[:, blk % 2, 0:NT], in_=A(pb)[:, 0:NT], func=AF.Copy),
                  r=[pb.k], w=[dstb.sub(blk % 2, 2)])


    def super_tile(src, dst, t0, first, last_sup, prev_sl, chunk_hook=None):
        if "p" in prefetched:
            slots, pend = prefetched.pop("p")
        else:
            slots = [load_tile(src[(t0 + i) * 128:(t0 + i + 1) * 128, :]) for i in range(4)]
            pend = norm_group_pre(slots)
        norm_group_post(slots, pend, hnT, None)
        qk_feature(512)
        ck(5)
        gates_a(512, 0, None, (4, 128))
        ck(6)
        tcs = {}
        prevs = {0: prev_sl}

        def sA(i):
            tcs[i] = seg_A(i, i * 128, last_sup and i == 3)
            prevs[i + 1] = tcs[i]["cur_sl"]

        sA(0)
        seg_B(tcs[0])
        gates_b(512, 0, None, (4, 128))
        for i in range(4):
            if i + 1 < 4:
                sA(i + 1)
            if i >= 1:
                seg_G(tcs[i - 1])
            seg_C(tcs[i], prevs[i])
            seg_E(tcs[i])
            if i + 1 < 4:
                seg_B(tcs[i + 1])
            seg_D(tcs[i], prevs[i])
            seg_F(tcs[i])
        seg_G(tcs[3])
        def pre_next():
            nslots = [load_tile(src[(t0 + 4 + i) * 128:(t0 + 5 + i) * 128, :]) for i in range(4)]
            prefetched["p"] = (nslots, norm_group_pre(nslots))

        tail(slots, dst, t0, "ydst", before_pass2=(None if last_sup else pre_next), chunk_hook=chunk_hook)
        return prevs[4]

    def sample_setup():
        s7 = NSLOT - 1
        base = xslot[0].lo
        SS = SB.at(base, [16, 2, 130], F32)
        SSb = SB.at(base + 16640, [16, 2, 130], BF16)
        off = base + 16640 + 8320
        def nxt(shape, dt, nbytes):
            nonlocal off
            b_ = SB.at(off, shape, dt)
            off += (nbytes + 31) // 32 * 32
            return b_
        snT = nxt([32], F32, 128)
        gdkS = nxt([32], F32, 128)
        seqind = nxt([16], BF16, 32)
        eye16 = nxt([16, 16], BF16, 512)
        maskS = nxt([128], BF16, 256)
        maskC = nxt([128], BF16, 256)
        Rg = nxt([32], F32, 128)
        msr = nxt([16], F32, 64)
        assert off <= xslot[s7].lo
        Z = SB.at(guring[0].lo, [16, 256], BF16)
        PZ = [SB.at(guring[2].lo, [16, 128], BF16), SB.at(wdring[0].lo, [16, 128], BF16)]
        QZ = SB.at(mixT.lo, [2, 16, 128], BF16)
        stg = [SB.at(s5ring[0].lo + 8192 + q * 1024, [256], F32) for q in range(2)]
        kcT = SB.at(s5ring[0].lo, [16, 2, 128], BF16)
        sm = dict(SS=SS, SSb=SSb, QZ=QZ, Z=Z, PZ=PZ, kcT=kcT, vcext=vcext_s, eye16=eye16, seqind=seqind, maskS=maskS, maskC=maskC, Rg=Rg, gdkS=gdkS, cached=set())
        sm.update(snT=snT, msr=msr, stg=stg, s7=s7)
        return sm

    def sample_cache(b, sm):
        stg, kcT = sm["stg"], sm["kcT"]
        sg = stg[b % 2]
        P.dma("sp", "stg%d" % (b % 2), lambda e: e.dma_start(out=A(sg)[:, 0:128], in_=ckd[b]), w=[sg.rng(0, 512)])
        P.dma("sp", "stv%d" % (b % 2), lambda e: e.dma_start(out=A(sg)[:, 128:256], in_=cvd[b]), w=[sg.rng(512, 1024)])
        kd = kdup[b % 2]
        P.act(lambda e: e.activation(out=A(kd)[:, :, :, :], in_=A(sg)[:, 0:128].rearrange("p (g d) -> p g d", g=2).unsqueeze(2).to_broadcast([128, 2, 2, 64]), func=AF.Copy),
              r=[sg.rng(0, 512)], w=[kd.k])
        tbc = bank(4 + b % 2, (8, 128), BF16)
        for g in range(2):
            tr(A(tbc)[:, g, :], A(kd)[:, g, :, :].rearrange("p a d -> p (a d)"), [kd.k], [tbc.sub(g, 8)])
        dve(lambda e: e.tensor_copy(out=A(kcT)[:, b, :, :], in_=A(tbc)[:, 0:2, :]), [tbc.rng(0, 512)], [kcT.sub(b, 16)])
        P.act(lambda e: e.activation(out=A(vcext_s)[:, b, :, 0:64], in_=A(sg)[:, 128:256].rearrange("p (g d) -> p g d", g=2), func=AF.Copy),
              r=[sg.rng(512, 1024)], w=[vcext_s.k])
        outs.append(P.dma("sp", "okws", lambda e: e.dma_start(out=kws[b, 0:120, :], in_=ckd[b, 8:128, :]), w=[("okws", b, b + 1)], total=True))
        outs.append(P.dma("sp", "ovws", lambda e: e.dma_start(out=vws[b, 0:120, :], in_=cvd[b, 8:128, :]), w=[("ovws", b, b + 1)], total=True))

    def sample_tile(sm):
        s7 = sm["s7"]
        P.dma("sp", "x%d" % s7, lambda e: e.dma_start(out=A(xslot[s7])[:, :], in_=xs_d), w=[xslot[s7].k])
        SS, SSb, QZ, Z, PZ, kcT = sm["SS"], sm["SSb"], sm["QZ"], sm["Z"], sm["PZ"], sm["kcT"]
        eye16, seqind, maskS, maskC, Rg, gdkS, snT, msr = sm["eye16"], sm["seqind"], sm["maskS"], sm["maskC"], sm["Rg"], sm["gdkS"], sm["snT"], sm["msr"]
        P.pool(lambda e: e.tensor_copy(out=A(seqind)[:, :], in_=A(onesb)[:, 0:16]), r=[onesb.k], w=[seqind.k])
        P.pool(lambda e: e.affine_select(out=A(seqind)[:, :], in_=A(seqind)[:, :], pattern=[[-8, 16]], compare_op=ALU.is_ge, fill=0.0, base=0, channel_multiplier=1),
               r=[seqind.k], w=[seqind.k])
        P.pool(lambda e: e.affine_select(out=A(seqind)[:, :], in_=A(seqind)[:, :], pattern=[[8, 16]], compare_op=ALU.is_ge, fill=0.0, base=7, channel_multiplier=-1),
               r=[seqind.k], w=[seqind.k])
        P.pool(lambda e: e.memset(A(eye16)[:, :, :], 1.0), w=[eye16.k])
        P.pool(lambda e: e.affine_select(out=A(eye16)[:, :, :], in_=A(eye16)[:, :, :], pattern=[[1, 16], [-1, 16]], compare_op=ALU.is_equal, fill=0.0, base=0, channel_multiplier=0),
               r=[eye16.k], w=[eye16.k])
        P.pool(lambda e: e.affine_select(out=A(maskS)[:, :], in_=A(maskU)[:, :], pattern=[[-8, 16], [0, 8]], compare_op=ALU.is_ge, fill=0.0, base=0, channel_multiplier=1),
               r=[maskU.k], w=[maskS.k])
        P.pool(lambda e: e.affine_select(out=A(maskS)[:, :], in_=A(maskS)[:, :], pattern=[[8, 16], [0, 8]], compare_op=ALU.is_ge, fill=0.0, base=7, channel_multiplier=-1),
               r=[maskS.k], w=[maskS.k])
        P.pool(lambda e: e.affine_select(out=A(maskC)[:, :], in_=A(onesb)[:, :], pattern=[[0, 16], [-1, 8]], compare_op=ALU.is_gt, fill=0.0, base=0, channel_multiplier=1),
               r=[onesb.k], w=[maskC.k])
        P.pool(lambda e: e.memset(A(vcext_s)[:, :, :, 64:65], 1.0), w=[vcext_s.k])
        rmul, radd = A(tmpA)[0:4, 0:128], A(tmpA)[0:4, 128:256]
        m0row, m0c = A(tmpB)[0:4, 0:128], A(tmpD)[0:4, 0:16]
        v8 = lambda ap_: ap_.rearrange("p (b t) -> p b t", t=8)
        P.dma("sp", "sm0", lambda e: e.dma_start(out=m0c, in_=sm_t), w=[tmpD.k])
        P.pool(lambda e: e.memset(rmul, 1.0), w=[tmpA.k])
        P.pool(lambda e: e.memset(v8(rmul)[:, :, 0:1], 0.0), r=[tmpA.k], w=[tmpA.k])
        P.pool(lambda e: e.memset(radd, 0.0), r=[tmpA.k], w=[tmpA.k])
        P.pool(lambda e: e.memset(v8(radd)[:, :, 0:1], -BIG), r=[tmpA.k], w=[tmpA.k])
        P.pool(lambda e: e.memset(m0row, -BIG), w=[tmpB.k])
        dve(lambda e: e.tensor_copy(out=v8(m0row)[:, :, 0], in_=m0c), [tmpD.k, tmpB.k], [tmpB.k])
        ck(20)
        P.dma("sp", "sSS", lambda e: e.dma_start(out=A(SS)[:, :, :, 0:128], in_=sC.rearrange("b (pr hh) k v -> (hh k) b pr v", hh=2)), w=[SS.k])
        P.dma("sp", "ssn", lambda e: e.dma_start(out=A(snT)[:, 0:32], in_=sn_t), w=[snT.k])
        dve(lambda e: e.tensor_copy(out=A(SS)[:, :, :, 128], in_=A(snT)[:, 0:32].rearrange("p (b r) -> p b r", r=2)), [snT.k, SS.k], [SS.k])
        ck(21)
        norm_transpose(s7, hnT, 0, None)
        qk_feature(128)
        gates(128, 0, None, (16, 8), sample_m0=(rmul, radd, m0row, m0c))
        dve(lambda e: e.tensor_tensor(out=A(msr)[0:4, 0:16], in0=A(gM)[0:4, 0:128].rearrange("p (b t) -> p b t", t=8)[:, :, 7],
                                      in1=A(gB)[0:4, 0:128].rearrange("p (b t) -> p b t", t=8)[:, :, 7], op=ALU.subtract), [gM.k, gB.k], [msr.k])
        outs.append(P.dma("sp", "oms", lambda e: e.dma_start(out=ms_t, in_=A(msr)[0:4, 0:16]), r=[msr.k], w=[("oms", 0, 1)]))
        sample_pre_e(sm)
        ck(22)
        for b in range(16):
            if b not in sm["cached"]:
                sample_cache(b, sm)
        ck(23)
        mixer_tile(0, 0, None, True, sm=sm)
        ck(30)
        for b in range(16):
            outs.append(P.dma("sp", "okws2", lambda e, b=b: e.dma_start(out=kws[b, 120:128, :], in_=A(kwout)[8 * b:8 * b + 8, :]), r=[kwout.k], w=[("okws2", b, b + 1)], total=True))
            outs.append(P.dma("sp", "ovws2", lambda e, b=b: e.dma_start(out=vws[b, 120:128, :], in_=A(vwout)[8 * b:8 * b + 8, :]), r=[vwout.k], w=[("ovws2", b, b + 1)], total=True))
        outs.append(P.dma("sp", "oCs", lambda e: e.dma_start(out=Cs.rearrange("b (pr hh) k v -> (hh k) b pr v", hh=2), in_=A(SS)[:, :, :, 0:128]), r=[SS.k], w=[("oCs", 0, 1)]))
        dve(lambda e: e.tensor_copy(out=A(snT)[:, 0:32].rearrange("p (b r) -> p b r", r=2), in_=A(SS)[:, :, :, 128]), [SS.k], [snT.k])
        outs.append(P.dma("sp", "ons", lambda e: e.dma_start(out=ns_t, in_=A(snT)[:, 0:32]), r=[snT.k], w=[("ons", 0, 1)]))
        ck(31)
        gu_ext = guring + [SB.at(xslot[j].lo, [2048], BF16) for j in range(NSLOT - 1)]
        s5_ext = s5ring + [SB.at(xslot[2 * j].lo, [3072], BF16) for j in range((NSLOT - 1) // 2)]
        tail([s7], ys, 0, "ysdst", ckoff=30, s5_slots=s5_ext, gu_slots=gu_ext)

    try:
      ck(1)
      prefix_pass()
      ck(2)
      pump_scratch(1000)
      ck(3)
      if "sl" not in halo_state:
          halo_tile()
      prev_sl = halo_state["sl"]
      ck(4)
      smS = sample_setup() if SAMPLE else None

      def cache_hook(c):
          if c < 16:
              sample_cache(c, smS)
              smS["cached"].add(c)

      for sup in range(NSUP):
          prev_sl = super_tile(xp, yp, sup * 4, sup == 0, sup == NSUP - 1, prev_sl,
                               chunk_hook=(cache_hook if (SAMPLE and sup == NSUP - 1) else None))

      for h in range(4):
          p0 = (h % 2) * 64
          outs.append(P.dma("sp", "oC", lambda e, h=h, p0=p0: e.dma_start(out=Cp[h], in_=A(S)[p0:p0 + 64, h // 2, 0:128]), r=[S.k], w=[("oCp", h, h + 1)], total=True))
      dve(lambda e: e.tensor_copy(out=A(ncol)[:, 0:2], in_=A(S)[:, :, 128]), [S.k], [ncol.k])
      outs.append(P.dma("sp", "oN", lambda e: e.dma_start(out=np_t, in_=A(ncol)[:, 0:2]), r=[ncol.k], w=[("onp", 0, 1)]))
      dve(lambda e: e.tensor_tensor(out=A(gcar)[0:4, 2:3], in0=A(gcar)[0:4, 1:2], in1=A(gcar)[0:4, 0:1], op=ALU.subtract), [gcar.k], [gcar.k])
      outs.append(P.dma("sp", "oM", lambda e: e.dma_start(out=mp, in_=A(gcar)[0:4, 2:3]), r=[gcar.k], w=[("omp", 0, 1)]))
      outs.append(P.dma("sp", "oK", lambda e: e.dma_start(out=kwp, in_=A(kwout)[:, :]), r=[kwout.k], w=[("okw", 0, 1)]))
      outs.append(P.dma("sp", "oV", lambda e: e.dma_start(out=vwp, in_=A(vwout)[:, :]), r=[vwout.k], w=[("ovw", 0, 1)]))

      if SAMPLE:
          sample_tile(smS)
    except _Stop:
        pass
    if not outs:
        outs.append(P.dma("sp", "oM", lambda e: e.dma_start(out=mp, in_=A(gcar)[0:4, 2:3]), r=[gcar.k], w=[("omp", 0, 1)]))
    P.emit(final_wait_ops=outs)
    st.close()
    print("ops", len(P.ops), "waits", P.n_waits, "sems", P.n_sems)
    return nc


_CACHE = {}


def _get_program(key):
    if key not in _CACHE:
        _CACHE[key] = build_program(*key)
    return _CACHE[key]


def _common_inputs(norm_mix_w, w_in, mlstm_i_bias, mlstm_f_bias, mlstm_norm_w, q_norm_w, k_norm_w, attn_sinks,
                   w_branch_a, w_branch_b, w_out, norm_ffn_w, w_gate, w_up, w_down):
    f = lambda a: np.ascontiguousarray(np.asarray(a, dtype=np.float32))
    return {
        "w_in": f(w_in[0]), "w_a": f(w_branch_a[0]), "w_b": f(w_branch_b[0]), "w_out": f(w_out[0]),
        "w_gate": f(w_gate[0]), "w_up": f(w_up[0]), "w_down": f(w_down[0]),
        "nmw_t": f(np.asarray(norm_mix_w[0]).reshape(8, 128).T), "nfw_t": f(np.asarray(norm_ffn_w[0]).reshape(8, 128).T),
        "ibias": f(np.asarray(mlstm_i_bias[0]).reshape(4, 1)), "fbias": f(np.asarray(mlstm_f_bias[0]).reshape(4, 1)),
        "mnw": f(np.asarray(mlstm_norm_w[0]).reshape(1, 512)),
        "qnw_dup": f(np.tile(np.asarray(q_norm_w[0]), 2).reshape(128, 1)), "knw_dup": f(np.tile(np.asarray(k_norm_w[0]), 2).reshape(128, 1)),
        "knw_row": f(np.tile(np.asarray(k_norm_w[0]), 2).reshape(1, 128)), "sinks": f(np.asarray(attn_sinks[0]).reshape(1, 8)),
    }


def _sample_inputs(c, x_sample, state_mlstm_C, state_mlstm_n, state_mlstm_m, cache_swa_k, cache_swa_v):
    f = lambda a: np.ascontiguousarray(np.asarray(a, dtype=np.float32))
    sl = slice(16 * c, 16 * c + 16)
    n = np.asarray(state_mlstm_n[0, sl]).reshape(16, 2, 2, 64)
    return {
        "xs": f(np.asarray(x_sample[sl]).reshape(128, D)),
        "sC": f(state_mlstm_C[0, sl]),
        "sn_t": f(n.transpose(2, 3, 0, 1).reshape(128, 32)),
        "sm_t": f(np.asarray(state_mlstm_m[0, sl]).T),
        "ck": f(np.asarray(cache_swa_k[0, sl]).reshape(16, 128, 128)),
        "cv": f(np.asarray(cache_swa_v[0, sl]).reshape(16, 128, 128)),
    }


def _sample_outputs(r):
    ys = r["ys"].reshape(16, 8, D)
    ns = r["ns_t"].reshape(2, 64, 16, 2).transpose(2, 3, 0, 1).reshape(16, 4, 64)
    ms = r["ms_t"].T
    return ys, r["Cs"], ns, ms, r["kws"].reshape(16, 128, 2, 64), r["vws"].reshape(16, 128, 2, 64)


def kernel(x_prompt, x_sample, state_mlstm_C, state_mlstm_n, state_mlstm_m, cache_swa_k, cache_swa_v,
           norm_mix_w, w_in, mlstm_i_bias, mlstm_f_bias, mlstm_norm_w, q_norm_w, k_norm_w, attn_sinks,
           w_branch_a, w_branch_b, w_out, norm_ffn_w, w_gate, w_up, w_down):
    f = lambda a: np.ascontiguousarray(np.asarray(a, dtype=np.float32))
    x_prompt = f(x_prompt)
    NSUP, NPRE, SAMPLE = 4, NPRE_FULL, True
    nc = _get_program((NSUP, NPRE, SAMPLE))
    common = _common_inputs(norm_mix_w, w_in, mlstm_i_bias, mlstm_f_bias, mlstm_norm_w, q_norm_w, k_norm_w, attn_sinks,
                            w_branch_a, w_branch_b, w_out, norm_ffn_w, w_gate, w_up, w_down)
    in_maps = []
    for c in range(8):
        b, j = c // 4, c % 4
        m = dict(common)
        m["xp"] = f(x_prompt[b, j * SEG:(j + 1) * SEG])
        m["xpre"] = f(x_prompt[b, 0:NPRE * 128])
        pm = np.zeros((4, NPRE * 128), np.float32)
        pm[:, :j * SEG] = 1.0
        m["premask"] = pm
        m["premask2"] = ((pm - 1.0) * np.float32(BIG)).astype(np.float32)
        pg = np.zeros((4, 16), np.float32)
        pg[:, :(j * SEG) // 512] = 1.0
        m["pmg"] = pg
        m["pmg2"] = ((pg - 1.0) * np.float32(BIG)).astype(np.float32)
        if j > 0:
            m["xhalo"] = f(x_prompt[b, j * SEG - 128:j * SEG])
            m["halov"] = np.ones((128, 1), np.float32)
        else:
            m["xhalo"] = np.zeros((128, D), np.float32)
            m["halov"] = np.zeros((128, 1), np.float32)
        m.update(_sample_inputs(c, x_sample, state_mlstm_C, state_mlstm_n, state_mlstm_m, cache_swa_k, cache_swa_v))
        in_maps.append(m)
    res = run_bass_kernel_spmd(nc, in_maps, core_ids=list(range(8)))
    R = res.results
    yp = np.stack([np.concatenate([R[b * 4 + j]["yp"] for j in range(4)], axis=0) for b in range(2)])
    Cp = np.stack([R[b * 4 + 3]["Cp"] for b in range(2)])[None]
    npo = np.stack([R[b * 4 + 3]["np_t"].reshape(2, 64, 2).transpose(2, 0, 1).reshape(4, 64) for b in range(2)])[None]
    mpo = np.stack([R[b * 4 + 3]["mp"].reshape(4) for b in range(2)])[None]
    kwp = np.stack([R[b * 4 + 3]["kwp"].reshape(128, 2, 64) for b in range(2)])[None]
    vwp = np.stack([R[b * 4 + 3]["vwp"].reshape(128, 2, 64) for b in range(2)])[None]
    so = [_sample_outputs(R[c]) for c in range(8)]
    cat = lambda i: np.concatenate([o[i] for o in so], axis=0)
    return (yp, cat(0), Cp, npo, mpo, kwp, vwp, cat(1)[None], cat(2)[None], cat(3)[None], cat(4)[None], cat(5)[None])
```
